# Optimizing a Trainium2 kernel written in Bass

```python
import math
import jax, jax.numpy as jnp
from jax import lax
import numpy as np

D_MODEL = 2048
BATCH = 16
SEQ = 2048
DEPTH = 4

HEAD_DIM = 64
A_HEADS = 16
A_KV_HEADS = 2
WINDOW = 128
BLOCK = 128
B_HEADS = 16
KV_RANK = 256
IDX_HEADS = 16
IDX_DIM = 64
TOPK_MAX = 256
NUM_BUCKETS = 32
MAX_DISTANCE = 128
D_FF = 5632
CONV_WIDTH = 3
EPS = 1e-6
NEG_INF = -1e30
IDX_SCALE = IDX_DIM ** -0.5 * IDX_HEADS ** -0.5
MIX_WIDTH = (A_HEADS + B_HEADS) * HEAD_DIM

A_Q = A_HEADS * HEAD_DIM
A_KV = A_KV_HEADS * HEAD_DIM
B_Q = B_HEADS * HEAD_DIM
I_Q = IDX_HEADS * IDX_DIM
IN_WIDTH = A_Q + 2 * A_KV + B_Q + KV_RANK + I_Q + IDX_DIM + IDX_HEADS
SPLIT_POINTS = (
    A_Q,
    A_Q + A_KV,
    A_Q + 2 * A_KV,
    A_Q + 2 * A_KV + B_Q,
    A_Q + 2 * A_KV + B_Q + KV_RANK,
    A_Q + 2 * A_KV + B_Q + KV_RANK + I_Q,
    A_Q + 2 * A_KV + B_Q + KV_RANK + I_Q + IDX_DIM,
)

kernel_name = "hymba_swa_dsa_convffn_adaln"


def rms_norm(x, g):
    xf = x.astype(jnp.float32)
    y = xf * lax.rsqrt(jnp.mean(xf * xf, axis=-1, keepdims=True) + EPS)
    return (y * g.astype(jnp.float32)).astype(x.dtype)


def rel_bucket(dist):
    n = jnp.maximum(dist, 0)
    max_exact = NUM_BUCKETS // 2
    nf = jnp.maximum(n, 1).astype(jnp.float32)
    large = max_exact + (jnp.log(nf / max_exact) / math.log(MAX_DISTANCE / max_exact)
                         * (NUM_BUCKETS - max_exact)).astype(jnp.int32)
    large = jnp.minimum(large, NUM_BUCKETS - 1)
    return jnp.where(n < max_exact, n, large)


def swa_attention(q, k, v, sinks, rel_table):
    Bsz, T, H, D = q.shape
    nblk = T // BLOCK
    G = H // A_KV_HEADS
    qb = q.reshape(Bsz, nblk, BLOCK, A_KV_HEADS, G, D)
    pad = ((0, 0), (BLOCK, 0), (0, 0), (0, 0))
    kp = jnp.pad(k, pad).reshape(Bsz, nblk + 1, BLOCK, A_KV_HEADS, D)
    vp = jnp.pad(v, pad).reshape(Bsz, nblk + 1, BLOCK, A_KV_HEADS, D)
    kb = jnp.concatenate([kp[:, :-1], kp[:, 1:]], axis=2)
    vb = jnp.concatenate([vp[:, :-1], vp[:, 1:]], axis=2)
    s = jnp.einsum('bnqkgd,bnskd->bnkgqs', qb, kb).astype(jnp.float32) * (D ** -0.5)
    i = jnp.arange(BLOCK, dtype=jnp.int32)[:, None]
    j = jnp.arange(2 * BLOCK, dtype=jnp.int32)[None, :]
    dist = i + BLOCK - j
    in_window = (dist >= 0) & (dist < WINDOW)
    key_pos = jnp.arange(nblk, dtype=jnp.int32)[:, None, None] * BLOCK - BLOCK + j[None]
    valid = in_window[None] & (key_pos >= 0)
    bias = jnp.transpose(rel_table[rel_bucket(dist)], (2, 0, 1))
    bias = bias.reshape(A_KV_HEADS, G, BLOCK, 2 * BLOCK).astype(jnp.float32)
    s = jnp.where(valid[None, :, None, None], s + bias[None, None], NEG_INF)
    sink = sinks.astype(jnp.float32).reshape(A_KV_HEADS, G)[None, None, :, :, None, None]
    m = jnp.maximum(jnp.max(s, axis=-1, keepdims=True), sink)
    e = jnp.exp(s - m)
    p = e / (jnp.sum(e, axis=-1, keepdims=True) + jnp.exp(sink - m))
    o = jnp.einsum('bnkgqs,bnskd->bnqkgd', p.astype(v.dtype), vb)
    return o.reshape(Bsz, T, H * D)


def dsa_attention(q, c_kv, w_uk, w_uv, q_idx, k_idx, w_idx, rel_table):
    Bsz, T, H, D = q.shape
    nblk = T // BLOCK
    topk = min(TOPK_MAX, T // 4)
    scale = D ** -0.5
    key_pos = jnp.arange(T, dtype=jnp.int32)

    def one_block(blk):
        start = blk * BLOCK
        qpos = start + jnp.arange(BLOCK, dtype=jnp.int32)
        qi = lax.dynamic_slice_in_dim(q_idx, start, BLOCK, axis=1)
        wi = lax.dynamic_slice_in_dim(w_idx, start, BLOCK, axis=1)
        qb = lax.dynamic_slice_in_dim(q, start, BLOCK, axis=1)
        dots = jnp.einsum('bqhd,bsd->bqhs', qi, k_idx).astype(jnp.float32)
        index_score = jnp.einsum('bqh,bqhs->bqs', wi.astype(jnp.float32),
                                 jax.nn.relu(dots)) * IDX_SCALE
        causal = key_pos[None, :] <= qpos[:, None]
        index_score = jnp.where(causal[None], index_score, NEG_INF)
        _, sel = lax.top_k(index_score, topk)
        c_sel = jax.vmap(lambda ck, ix: ck[ix])(c_kv, sel)
        q_lat = jnp.einsum('bqhd,hrd->bqhr', qb, w_uk)
        s = jnp.einsum('bqhr,bqkr->bqhk', q_lat, c_sel).astype(jnp.float32) * scale
        dist = qpos[None, :, None] - sel
        bias = jnp.moveaxis(rel_table[rel_bucket(dist)], -1, 2).astype(jnp.float32)
        s = jnp.where((dist >= 0)[:, :, None, :], s + bias, NEG_INF)
        p = jax.nn.softmax(s, axis=-1).astype(c_sel.dtype)
        o_lat = jnp.einsum('bqhk,bqkr->bqhr', p, c_sel)
        o = jnp.einsum('bqhr,hrd->bqhd', o_lat, w_uv)
        return o.reshape(Bsz, BLOCK, H * D)

    out = lax.map(one_block, jnp.arange(nblk, dtype=jnp.int32))
    return jnp.transpose(out, (1, 0, 2, 3)).reshape(Bsz, T, H * D)


def conv_ffn(h, w_up, conv_w, conv_b, w_down):
    u, g = jnp.split(h @ w_up, 2, axis=-1)
    u = lax.conv_general_dilated(
        u, conv_w[:, None, :], window_strides=(1,), padding=((CONV_WIDTH - 1, 0),),
        dimension_numbers=('NWC', 'WIO', 'NWC'), feature_group_count=D_FF) + conv_b
    return (jax.nn.silu(u) * g) @ w_down


def setup_inputs(seed: int = 0) -> dict:
    key = jax.random.key(seed)
    ks = jax.random.split(key, 20)
    f32 = jnp.float32
    nrm = lambda k, shape, s: jax.random.normal(k, shape, f32) * s
    return {
        "x": nrm(ks[0], (BATCH, SEQ, D_MODEL), 1.0),
        "c": nrm(ks[1], (BATCH, D_MODEL), 1.0),
        "w_mod": nrm(ks[2], (DEPTH, D_MODEL, 6 * D_MODEL), D_MODEL ** -0.5),
        "b_mod": nrm(ks[3], (DEPTH, 6 * D_MODEL), 0.02),
        "norm_attn": 1.0 + nrm(ks[4], (DEPTH, D_MODEL), 0.02),
        "w_in": nrm(ks[5], (DEPTH, D_MODEL, IN_WIDTH), D_MODEL ** -0.5),
        "attn_sinks": nrm(ks[6], (DEPTH, A_HEADS), 0.5),
        "kv_norm": 1.0 + nrm(ks[7], (DEPTH, KV_RANK), 0.02),
        "w_uk": nrm(ks[8], (DEPTH, B_HEADS, KV_RANK, HEAD_DIM), KV_RANK ** -0.5),
        "w_uv": nrm(ks[9], (DEPTH, B_HEADS, KV_RANK, HEAD_DIM), KV_RANK ** -0.5),
        "rel_bias": nrm(ks[10], (NUM_BUCKETS, A_HEADS + B_HEADS), 0.3),
        "w_out": nrm(ks[11], (DEPTH, MIX_WIDTH, D_MODEL), MIX_WIDTH ** -0.5),
        "norm_ffn": 1.0 + nrm(ks[12], (DEPTH, D_MODEL), 0.02),
        "w_up": nrm(ks[13], (DEPTH, D_MODEL, 2 * D_FF), D_MODEL ** -0.5),
        "conv_w": nrm(ks[14], (DEPTH, CONV_WIDTH, D_FF), CONV_WIDTH ** -0.5),
        "conv_b": nrm(ks[15], (DEPTH, D_FF), 0.02),
        "w_down": nrm(ks[16], (DEPTH, D_FF, D_MODEL), D_FF ** -0.5),
        "norm_final": 1.0 + nrm(ks[17], (D_MODEL,), 0.02),
    }


def reference(x, c, w_mod, b_mod, norm_attn, w_in, attn_sinks, kv_norm, w_uk, w_uv,
              rel_bias, w_out, norm_ffn, w_up, conv_w, conv_b, w_down, norm_final):
    Bsz, T, _ = x.shape
    rel_a = rel_bias[:, :A_HEADS]
    rel_b = rel_bias[:, A_HEADS:]
    c_act = jax.nn.silu(c)
    for l in range(DEPTH):
        mod = c_act @ w_mod[l] + b_mod[l]
        sh1, sc1, g1, sh2, sc2, g2 = [m[:, None, :] for m in jnp.split(mod, 6, axis=-1)]

        h = rms_norm(x, norm_attn[l]) * (1.0 + sc1) + sh1
        qa, ka, va, qb, ckv, qi, ki, wi = jnp.split(h @ w_in[l], SPLIT_POINTS, axis=-1)
        qa = qa.reshape(Bsz, T, A_HEADS, HEAD_DIM)
        ka = ka.reshape(Bsz, T, A_KV_HEADS, HEAD_DIM)
        va = va.reshape(Bsz, T, A_KV_HEADS, HEAD_DIM)
        qb = qb.reshape(Bsz, T, B_HEADS, HEAD_DIM)
        ckv = rms_norm(ckv, kv_norm[l])
        qi = qi.reshape(Bsz, T, IDX_HEADS, IDX_DIM)
        out_a = swa_attention(qa, ka, va, attn_sinks[l], rel_a)
        out_b = dsa_attention(qb, ckv, w_uk[l], w_uv[l], qi, ki, wi, rel_b)
        mix = jnp.concatenate([out_a, out_b], axis=-1) @ w_out[l]
        x = x + g1 * mix

        h = rms_norm(x, norm_ffn[l]) * (1.0 + sc2) + sh2
        x = x + g2 * conv_ffn(h, w_up[l], conv_w[l], conv_b[l], w_down[l])
    return rms_norm(x, norm_final)
```

```python
import math
from contextlib import ExitStack
import numpy as np
import concourse.bass as bass
import concourse.mybir as mybir
from concourse.bass_utils import run_bass_kernel_spmd

F32 = mybir.dt.float32
BF16 = mybir.dt.bfloat16
AF = mybir.ActivationFunctionType
ALU = mybir.AluOpType
AX = mybir.AxisListType

D = 2048
T = 2048
NKC = 16
TA = 512
NTILE = T // TA
NSUB = TA // 128
FF = 5632
NFC = 44
INW = 3664
EPS = 1e-6
NIT = 26
UN = 50432
FFN_OVERLAP = False


class Buf:
    __slots__ = ("name", "lw", "rdc", "rdd", "dsem", "dcnt", "excl")

    def __init__(self, name, excl=False):
        self.name = name
        self.excl = excl
        self.lw = None
        self.rdc = {}
        self.rdd = []
        self.dsem = None
        self.dcnt = 0


class Op:
    __slots__ = ("eng", "fn", "dma", "deps", "tok", "inc", "idx")


class Prog:
    ENGS = ("pe", "act", "dve", "pool", "sp")

    def __init__(self, nc):
        self.nc = nc
        self.ops = []
        self.last = {e: None for e in self.ENGS}
        self.dmas_since_bar = []
        self.dbufs = []

    def add(self, eng, fn, reads=(), writes=(), dma_dst=None):
        o = Op()
        o.eng = eng
        o.fn = fn
        o.dma = dma_dst is not None
        o.inc = False
        o.idx = len(self.ops)
        o.tok = None
        deps = set()
        for b in reads:
            if b.lw is not None:
                deps.add(b.lw)
            if b.excl:
                for e, r in b.rdc.items():
                    if e != eng:
                        deps.add(r)
        for b in writes:
            if b.lw is not None:
                deps.add(b.lw)
            for e, r in b.rdc.items():
                if e == eng and not o.dma:
                    continue
                deps.add(r)
            for r in b.rdd:
                deps.add(r)
        if eng == "pe" and not o.dma:
            deps = {d for d in deps if d.dma or d.eng != "pe"}
        o.deps = deps
        for d in deps:
            if not d.dma:
                d.inc = True
        if o.dma:
            if dma_dst.dsem is None:
                dma_dst.dsem = "pending"
                self.dbufs.append(dma_dst)
            dma_dst.dcnt += 16
            o.tok = (dma_dst, dma_dst.dcnt)
            self.dmas_since_bar.append(o)
        for b in reads:
            if o.dma:
                b.rdd.append(o)
            else:
                b.rdc[eng] = o
        for b in writes:
            b.lw = o
            b.rdc = {}
            b.rdd = []
        self.ops.append(o)
        self.last[eng] = o
        return o

    def barrier(self):
        lasts = [o for o in self.last.values() if o is not None]
        pend = list(self.dmas_since_bar)
        self.dmas_since_bar = []
        for e in self.ENGS:
            o = Op()
            o.eng = e
            o.fn = None
            o.dma = False
            o.inc = False
            o.idx = len(self.ops)
            o.tok = None
            o.deps = set(x for x in lasts if (x.dma or x.eng != e)) | set(pend)
            for d in o.deps:
                if not d.dma:
                    d.inc = True
            self.ops.append(o)
            self.last[e] = o

    def emit(self, es):
        nc = self.nc
        esem = {}
        for e in ("pe", "act", "dve", "pool"):
            esem[e] = es.enter_context(nc.semaphore("es_" + e))
        for b in self.dbufs:
            b.dsem = es.enter_context(nc.semaphore("ds%d_%s" % (self.dbufs.index(b), b.name)))
        cnt = {e: 0 for e in self.ENGS}
        for o in self.ops:
            if o.dma or o.fn is None:
                continue
            if o.inc:
                cnt[o.eng] += 1
                o.tok = (o.eng, cnt[o.eng])
        by_eng = {e: [o for o in self.ops if o.eng == e] for e in self.ENGS}
        self.stats = {e: len(v) for e, v in by_eng.items()}

        def run(ename, eng):
            seen = {}
            for o in by_eng[ename]:
                waits = {}
                for d in o.deps:
                    if d.tok is None:
                        continue
                    key, val = d.tok
                    kid = key if isinstance(key, str) else id(key)
                    if kid not in waits or waits[kid][1] < val:
                        waits[kid] = (key, val)
                for kid, (key, val) in waits.items():
                    if seen.get(kid, 0) >= val:
                        continue
                    seen[kid] = val
                    sem = esem[key] if isinstance(key, str) else key.dsem
                    eng.wait_ge(sem, val)
                if o.fn is None:
                    continue
                inst = o.fn(eng)
                if o.dma:
                    inst.then_inc(o.tok[0].dsem, 16)
                elif o.inc:
                    inst.then_inc(esem[ename], 1)

        with nc.Block() as block:
            @block.tensor
            def _(eng):
                run("pe", eng)

            @block.scalar
            def _(eng):
                run("act", eng)

            @block.vector
            def _(eng):
                run("dve", eng)

            @block.gpsimd
            def _(eng):
                run("pool", eng)

            @block.sync
            def _(eng):
                run("sp", eng)


def _rel_bucket_np(n):
    n = np.maximum(n, 0)
    max_exact = 16
    nf = np.maximum(n, 1).astype(np.float32)
    v = (np.log(nf / np.float32(max_exact)) / np.float32(math.log(128 / max_exact))
         * np.float32(32 - max_exact)).astype(np.float32)
    large = max_exact + v.astype(np.int32)
    large = np.minimum(large, 31)
    return np.where(n < max_exact, n, large)


def _onehot_const():
    oh = np.zeros((33, 2, 384), np.float32)
    for u in range(383):
        dist = 255 - u
        if 0 <= dist < 128:
            oh[int(_rel_bucket_np(np.array(dist))), 0, u] = 1.0
        else:
            oh[32, 0, u] = -30000.0
        if dist >= 0:
            oh[int(_rel_bucket_np(np.array(dist))), 1, u] += 1.0
            oh[31, 1, u] -= 1.0
        else:
            oh[32, 1, u] = -30000.0
    oh[32, :, 383] = -30000.0
    return oh


class _Stop(Exception):
    pass


def build(NL=4, NSEQ=2, debug=False, stop=None):
    nc = bass.Bass("TRN2", target_bir_lowering=False)
    P = Prog(nc)
    dbg_d = nc.dram_tensor("dbg", [16, 128, 2048], F32, kind="ExternalOutput") if debug else None
    dDBG = Buf("dDBG")

    def ck(name):
        if stop == name:
            raise _Stop()

    def dump(slot, ap, bufs, n):
        if debug:
            P.add("pool", lambda e: e.dma_start(out=dbg_d.ap()[slot, 0:ap.shape[0], 0:n], in_=ap), bufs, [dDBG], dma_dst=dDBG)

    def din(name, shape):
        return nc.dram_tensor(name, list(shape), F32, kind="ExternalInput")

    x_d = din("x", [NSEQ, T, D])
    c_d = din("c", [NSEQ, D])
    wmod_d = din("w_mod", [NL, D, 6 * D])
    bmod_d = din("b_mod", [NL, 6 * D])
    nattn_d = din("norm_attn", [NL, D])
    win_d = din("w_in", [NL, D, INW])
    sinks_d = din("attn_sinks", [NL, 16])
    kvn_d = din("kv_norm", [NL, 256])
    wuk_d = din("w_uk", [NL, 16, 256, 64])
    wuv_d = din("w_uv", [NL, 16, 256, 64])
    rel_d = din("rel_bias", [32, 32])
    wout_d = din("w_out", [NL, D, D])
    nffn_d = din("norm_ffn", [NL, D])
    wup_d = din("w_up", [NL, D, 2 * FF])
    convw_d = din("conv_w", [NL, 3, FF])
    convb_d = din("conv_b", [NL, FF])
    wdown_d = din("w_down", [NL, FF, D])
    nfin_d = din("norm_final", [D])
    oh_d = din("oh", [33, 768])
    out_d = nc.dram_tensor("out", [NSEQ, T, D], F32, kind="ExternalOutput")

    xT_d = nc.dram_tensor("xT", [NSEQ, NKC, 128, T], F32, kind="Internal")
    fv_d = nc.dram_tensor("fv", [32, 384], F32, kind="Internal")
    winb_d = nc.dram_tensor("winb", [NL, D, INW], BF16, kind="Internal")
    woutb_d = nc.dram_tensor("woutb", [NL, D, D], BF16, kind="Internal")
    wupb_d = nc.dram_tensor("wupb", [NL, D, 2 * FF], BF16, kind="Internal")
    wdownb_d = nc.dram_tensor("wdownb", [NL, FF, D], BF16, kind="Internal")
    wukb_d = nc.dram_tensor("wukb", [NL, 16, 256, 64], BF16, kind="Internal")
    wuvb_d = nc.dram_tensor("wuvb", [NL, 16, 256, 64], BF16, kind="Internal")

    x_a, c_a, wmod_a, win_a = x_d.ap(), c_d.ap(), wmod_d.ap(), win_d.ap()
    out_a, xT_a = out_d.ap(), xT_d.ap()
    winb_a, woutb_a, wupb_a, wdownb_a = winb_d.ap(), woutb_d.ap(), wupb_d.ap(), wdownb_d.ap()
    wukb_a, wuvb_a = wukb_d.ap(), wuvb_d.ap()

    dX = [[[Buf(f"dX{s}_{j}_{q}") for q in range(2)] for j in range(NTILE)] for s in range(NSEQ)]
    dW = {k: [Buf(f"dW{k}{l}") for l in range(4)] for k in ("in", "out", "up", "down", "uk", "uv")}
    dFV = Buf("dFV")
    dOUT = Buf("dOUT")

    with ExitStack() as es:
        def sb(name, shape, dt):
            return es.enter_context(nc.sbuf_tensor(name, list(shape), dt))

        def pst(name, shape, dt):
            return es.enter_context(nc.psum_tensor(name, list(shape), dt))

        A03 = pst("A03", [128, 2048], F32)
        A45 = pst("A45", [128, 1024], F32)
        B6 = pst("B6", [128, 1024], BF16)
        B7 = pst("B7", [128, 1024], BF16)
        bA = [Buf(f"bA{i}", excl=True) for i in range(6)]
        bB6, bB7 = Buf("bB6", excl=True), Buf("bB7", excl=True)

        def bank(i):
            if i < 4:
                return A03[:, i * 512:(i + 1) * 512]
            return A45[:, (i - 4) * 512:(i - 3) * 512]

        ident_f = sb("ident_f", [128, 128], F32); b_idf = Buf("idf")
        ident_b = sb("ident_b", [128, 128], BF16); b_idb = Buf("idb")
        i32k = sb("i32k", [128, 128], BF16); b_i32k = Buf("i32k")
        J_f = sb("J_f", [128, 128], F32); b_J = Buf("J")
        ones_f = sb("ones_f", [128, 128], F32); b_ones = Buf("ones")
        zer_f = sb("zer_f", [128, 128], F32); b_zer = Buf("zer")
        caus = sb("caus", [128, 128], F32); b_caus = Buf("caus")
        BA = sb("BA", [128, 16, 256], BF16); b_BA = Buf("BA")
        BB = sb("BB", [128, 16, 256], BF16); b_BB = Buf("BB")
        bmodc = sb("bmodc", [128, 384], F32); b_bmodc = Buf("bmodc")
        nattn = sb("nattn", [128, 64], F32); b_nattn = Buf("nattn")
        nffn = sb("nffn", [128, 64], F32); b_nffn = Buf("nffn")
        kvn = sb("kvn", [128, 8], F32); b_kvn = Buf("kvn")
        convw = sb("convw", [128, 528], F32); b_convw = Buf("convw")
        convb = sb("convb", [128, 176], F32); b_convb = Buf("convb")
        nfin = sb("nfin", [128, 16], F32); b_nfin = Buf("nfin")
        cTs = sb("cTs", [128, 32], F32); b_cTs = Buf("cTs")
        cT2 = sb("cT2", [128, 16, NSEQ], F32); b_cT2 = Buf("cT2")
        sink_bc = sb("sink_bc", [128, 64], F32); b_sink = Buf("sink")
        modc = sb("modc", [128, 6, 16, NSEQ], F32); b_modc = Buf("modc")
        gs1 = sb("gs1", [128, NSEQ, 16], F32); b_gs1 = Buf("gs1")
        gs2 = sb("gs2", [128, NSEQ, 16], F32); b_gs2 = Buf("gs2")
        wukT = sb("wukT", [128, 8, 256], BF16); b_wukT = Buf("wukT")
        wuv = sb("wuv", [128, 2, 16, 64], BF16); b_wuv = Buf("wuv")
        mrow = sb("mrow", [2, 256], F32); b_mrow = Buf("mrow")
        uh = sb("uh", [128, NFC, 2], F32); b_uh = Buf("uh")

        hT = sb("hT", [128, 16, TA], BF16); b_hT = Buf("hT")
        WB = [sb(f"WB{i}", [128, 8192], BF16) for i in range(2)]
        b_WB = [Buf(f"WB{i}") for i in range(2)]
        xb = [sb(f"xb{i}", [128, TA], F32) for i in range(4)]
        b_xb = [Buf(f"xb{i}") for i in range(4)]
        sqb = [sb(f"sqb{i}", [128, TA], F32) for i in range(2)]
        b_sqb = [Buf(f"sqb{i}") for i in range(2)]
        rstd = sb("rstd", [128, TA], F32); b_rstd = Buf("rstd")
        tmpn = [sb(f"tmpn{i}", [128, TA], F32) for i in range(2)]
        b_tmpn = [Buf(f"tmpn{i}") for i in range(2)]
        st = sb("st", [128, 128], F32)
        pw = sb("pw", [128, NIT + 1], F32); b_pw = Buf("pw")
        bs2 = sb("bs2", [128, NIT + 1], F32)
        bsn = sb("bsn", [128, NIT + 1], F32)
        wi_sb_t = sb("wi_sb", [128, NSUB, 16], F32)
        junkf_t = sb("junkf", [128, 256], F32)
        U = sb("U", [128, UN], BF16)

        ucur = [0]

        def carve(nbytes_elems, dt, shape=None):
            n2 = nbytes_elems
            a = U[:, ucur[0]:ucur[0] + n2]
            ucur[0] += n2
            assert ucur[0] <= UN, ucur[0]
            if dt == F32:
                a = a.bitcast(F32)
            return a

        def creset():
            ucur[0] = 0

        rr = {}
        _pb = {}

        def PB(name):
            if name not in _pb:
                _pb[name] = Buf(name)
            return _pb[name]

        def rot(key, n):
            v = rr.get(key, 0)
            rr[key] = v + 1
            return v % n

        def dma(q, out, in_, reads, writes, dst):
            P.add(q, lambda e: e.dma_start(out=out, in_=in_), reads, writes, dma_dst=dst)

        def act(out, in_, func, reads, writes, bias=None, scale=None, accum=None):
            kw = {}
            if bias is not None:
                kw["bias"] = bias
            if scale is not None:
                kw["scale"] = scale
            if accum is not None:
                kw["accum_out"] = accum
            P.add("act", lambda e: e.activation(out=out, in_=in_, func=func, **kw), reads, writes)

        def ts(out, in0, s1, op0, reads, writes, s2=None, op1=None, accum=None, eng="dve"):
            kw = {}
            if op1 is not None:
                kw["op1"] = op1
            if accum is not None:
                kw["accum_out"] = accum
            P.add(eng, lambda e: e.tensor_scalar(out=out, in0=in0, scalar1=s1, scalar2=s2, op0=op0, **kw), reads, writes)

        def tt(out, in0, in1, op, reads, writes, eng="dve"):
            P.add(eng, lambda e: e.tensor_tensor(out=out, in0=in0, in1=in1, op=op), reads, writes)

        def stt(out, in0, scalar, in1, op0, op1, reads, writes):
            P.add("dve", lambda e: e.scalar_tensor_tensor(out=out, in0=in0, scalar=scalar, in1=in1, op0=op0, op1=op1), reads, writes)

        def mm(out, lhsT, rhs, start, stop, reads, writes):
            P.add("pe", lambda e: e.matmul(out, lhsT=lhsT, rhs=rhs, start=start, stop=stop), reads, writes)

        def tr(out, in_, ident, reads, writes):
            P.add("pe", lambda e: e.transpose(out=out, in_=in_, identity=ident), reads, writes)

        def copy_alt(out, in_, reads, writes, scale=None):
            if rot("cp", 2) == 0:
                if scale is None:
                    act(out, in_, AF.Copy, reads, writes)
                else:
                    act(out, in_, AF.Copy, reads, writes, scale=scale)
            else:
                if scale is None:
                    P.add("dve", lambda e: e.tensor_copy(out=out, in_=in_), reads, writes)
                else:
                    ts(out, in_, scale, ALU.mult, reads, writes)

        def _body():
            P.add("pool", lambda e: e.memset(ones_f[:], 1.0), [], [b_ones])
            P.add("pool", lambda e: e.memset(zer_f[:], 0.0), [], [b_zer])
            P.add("pool", lambda e: e.memset(st[:], 0.0), [], [])
            P.add("pool", lambda e: e.affine_select(out=ident_f[:], in_=ones_f[:], pattern=[[-1, 128]], compare_op=ALU.is_equal, fill=0.0, base=0, channel_multiplier=1), [b_ones], [b_idf])
            P.add("pool", lambda e: e.affine_select(out=J_f[:], in_=ones_f[:], pattern=[[1, 128]], compare_op=ALU.is_equal, fill=0.0, base=-127, channel_multiplier=1), [b_ones], [b_J])
            P.add("pool", lambda e: e.affine_select(out=caus[:], in_=zer_f[:], pattern=[[-1, 128]], compare_op=ALU.is_ge, fill=-1e30, base=0, channel_multiplier=1), [b_zer], [b_caus])
            P.add("dve", lambda e: e.tensor_copy(out=ident_b[:], in_=ident_f[:]), [b_idf], [b_idb])
            ts(i32k[:], ident_f[:], 32768.0, ALU.mult, [b_idf], [b_i32k])
            P.add("pool", lambda e: e.memset(uh[:], 0.0), [], [b_uh])
            for k in range(NIT + 1):
                P.add("pool", lambda e, k=k: e.memset(pw[:, k:k + 1], 2.0 ** -(k + 1)), [], [b_pw])

            ck("c0")
            def cast_copy(src_t, dst_t, l, nelem, dbuf):
                rows = nelem // 2048
                r0 = 0
                while r0 < rows:
                    n = min(2048, rows - r0)
                    si = bass.AP(src_t, l * nelem + r0 * 2048, [[2048, n], [1, 2048]])
                    di = bass.AP(dst_t, l * nelem + r0 * 2048, [[2048, n], [1, 2048]])
                    dma("pool", di, si, [], [dbuf], dbuf)
                    r0 += n

            for l in range(NL):
                cast_copy(win_d, winb_d, l, D * INW, dW["in"][l])
                cast_copy(wuk_d, wukb_d, l, 16 * 256 * 64, dW["uk"][l])
                cast_copy(wuv_d, wuvb_d, l, 16 * 256 * 64, dW["uv"][l])
                cast_copy(wout_d, woutb_d, l, D * D, dW["out"][l])
                cast_copy(wup_d, wupb_d, l, D * 2 * FF, dW["up"][l])
                cast_copy(wdown_d, wdownb_d, l, FF * D, dW["down"][l])

            ck("cast")
            rowbuf = [carve(256, F32) for i in range(2)]
            b_rowbuf = [Buf(f"rowbuf{i}") for i in range(2)]

            def col_load(dst, b_dst, src_t, nrows):
                r0 = 0
                while r0 < nrows:
                    nb = min(128, nrows - r0)
                    k = rot("rowbuf", 2)
                    src = bass.AP(src_t, r0 * 128, [[128, nb], [1, 128]])
                    dma("sp", rowbuf[k][0:nb, :], src, [], [b_rowbuf[k]], b_rowbuf[k])
                    pb = rot("cl_ps", 2)
                    tr(bank(4 + pb)[:, 0:nb], rowbuf[k][0:nb, :], ident_f[0:nb, 0:nb], [b_rowbuf[k], b_idf], [bA[4 + pb]])
                    copy_alt(dst[:, r0:r0 + nb], bank(4 + pb)[:, 0:nb], [bA[4 + pb]], [b_dst])
                    r0 += nb

            col_load(bmodc, b_bmodc, bmod_d, NL * 96)
            col_load(nattn, b_nattn, nattn_d, NL * 16)
            col_load(nffn, b_nffn, nffn_d, NL * 16)
            col_load(kvn, b_kvn, kvn_d, NL * 2)
            col_load(convw, b_convw, convw_d, NL * 132)
            col_load(convb, b_convb, convb_d, NL * 44)
            col_load(nfin, b_nfin, nfin_d, 16)
            col_load(cTs, b_cTs, c_d, NSEQ * 16)
            act(cTs[:, 0:NSEQ * 16], cTs[:, 0:NSEQ * 16], AF.Silu, [b_cTs], [b_cTs])
            P.add("dve", lambda e: e.tensor_copy(out=cT2[:], in_=cTs[:, 0:NSEQ * 16].rearrange("p (b k) -> p k b", b=NSEQ)), [b_cTs], [b_cT2])
            dma("sp", sink_bc[:, 0:NL * 16], bass.AP(sinks_d, 0, [[0, 128], [1, NL * 16]]), [], [b_sink], b_sink)

            ck("cols")
            rel_aug = sb("rel_aug", [33, 32], F32); b_rel = Buf("rel_aug")
            oh_sb = carve(768 * 2, F32); b_oh = Buf("oh_sb")
            fv_sb = carve(384 * 2, F32); b_fvsb = Buf("fv_sb")
            hk = [carve(512, F32) for i in range(2)]
            b_hk = [Buf(f"hk{i}") for i in range(2)]
            P.add("pool", lambda e: e.memset(rel_aug[32:33, :], 1.0), [], [b_rel])
            dma("sp", rel_aug[0:32, :], rel_d.ap(), [], [b_rel], b_rel)
            dma("sp", oh_sb[0:33, :], oh_d.ap(), [], [b_oh], b_oh)
            for kind in range(2):
                mm(bank(4)[0:16, 0:384], rel_aug[0:33, kind * 16:(kind + 1) * 16], oh_sb[0:33, kind * 384:(kind + 1) * 384], True, True, [b_rel, b_oh], [bA[4]])
                P.add("dve", lambda e: e.tensor_copy(out=fv_sb[0:16, :], in_=bank(4)[0:16, 0:384]), [bA[4]], [b_fvsb])
                dma("sp", fv_d.ap()[kind * 16:(kind + 1) * 16, :], fv_sb[0:16, :], [b_fvsb], [dFV], dFV)
            for hh in range(32):
                k = rot("hk", 2)
                dma("sp", hk[k][:], bass.AP(fv_d, hh * 384, [[1, 128], [1, 256]]), [dFV], [b_hk[k]], b_hk[k])
                pb = rot("cl_ps", 2)
                mm(bank(4 + pb)[:, 0:256], J_f[:], hk[k][:], True, True, [b_J, b_hk[k]], [bA[4 + pb]])
                if hh < 16:
                    copy_alt(BA[:, hh, :], bank(4 + pb)[:, 0:256], [bA[4 + pb]], [b_BA])
                else:
                    copy_alt(BB[:, hh - 16, :], bank(4 + pb)[:, 0:256], [bA[4 + pb]], [b_BB])

            ck("bias")
            P.barrier()
            creset()
            xrow = carve(4 * 2048 * 2, F32).rearrange("p (a b) -> p a b", a=4)
            b_xrow = [Buf(f"xrow{i}") for i in range(4)]
            for s in range(NSEQ):
                for j in range(NTILE):
                    for tsub in range(NSUB):
                        r0 = j * TA + tsub * 128
                        dma("sp", xrow[:, tsub, :], x_a[s, r0:r0 + 128, :], [], [b_xrow[tsub]], b_xrow[tsub])
                    for kc in range(NKC):
                        pb = rot("xt_ps", 4)
                        for tsub in range(NSUB):
                            tr(bank(pb)[:, tsub * 128:(tsub + 1) * 128], xrow[:, tsub, kc * 128:(kc + 1) * 128], ident_f[:], [b_xrow[tsub], b_idf], [bA[pb]])
                        k = rot("xb", 4)
                        copy_alt(xb[k][:], bank(pb), [bA[pb]], [b_xb[k]])
                        dma("sp", xT_a[s, kc, :, j * TA:(j + 1) * TA], xb[k][:], [b_xb[k]], [dX[s][j][kc % 2]], dX[s][j][kc % 2])

            ck("xT")
            def norm_p1(s, j, kc):
                t0 = j * TA
                k = rot("xb", 4)
                dma("sp", xb[k][:], xT_a[s, kc, :, t0:t0 + TA], [dX[s][j][kc % 2]], [b_xb[k]], b_xb[k])
                q = rot("sqb", 2)
                act(sqb[q][:], xb[k][:], AF.Square, [b_xb[k]], [b_sqb[q]])
                mm(bank(5), ones_f[:], sqb[q][:], kc == 0, kc == NKC - 1, [b_ones, b_sqb[q]], [bA[5]])

            def norm_mid():
                act(rstd[:], bank(5), AF.Sqrt, [bA[5]], [b_rstd], bias=EPS, scale=1.0 / D)
                P.add("dve", lambda e: e.reciprocal(out=rstd[:], in_=rstd[:]), [b_rstd], [b_rstd])

            def norm_p2(s, j, kc, gs_ap, dst_fn):
                t0 = j * TA
                k = rot("xb", 4)
                dma("sp", xb[k][:], xT_a[s, kc, :, t0:t0 + TA], [dX[s][j][kc % 2]], [b_xb[k]], b_xb[k])
                q = rot("tmpn", 2)
                stt(tmpn[q][:], xb[k][:], gs_ap(kc), rstd[:], ALU.mult, ALU.mult, [b_xb[k], b_rstd, b_gs1, b_gs2, b_nfin], [b_tmpn[q]])
                dst_fn(kc, tmpn[q], b_tmpn[q])

            def norm_tile(s, j, gs_ap, sh_ap, dst_fn):
                for kc in range(NKC):
                    norm_p1(s, j, kc)
                norm_mid()
                for kc in range(NKC):
                    norm_p2(s, j, kc, gs_ap, dst_fn)

            def to_hT(sh_ap):
                def f(kc, tm, b_tm):
                    act(hT[:, kc, :], tm[:], AF.Identity, [b_tm, b_modc], [b_hT], bias=sh_ap(kc), scale=1.0)
                return f

            for l in range(NL):
                for kind in range(6):
                    for grp in range(8):
                        c0 = kind * 2048 + grp * 256
                        w = rot("WB", 2)
                        Wt = WB[w][:, 0:8192].bitcast(F32).rearrange("p (k c) -> p k c", k=16)
                        dma("sp", Wt, wmod_a[l, :, c0:c0 + 256].rearrange("(k p) c -> p k c", p=128), [], [b_WB[w]], b_WB[w])
                        for kc in range(NKC):
                            mm(bank(4)[0:NSEQ, 0:256], cT2[:, kc, :], Wt[:, kc, :], kc == 0, kc == NKC - 1, [b_cT2, b_WB[w]], [bA[4]])
                        act(mrow[0:NSEQ, :], bank(4)[0:NSEQ, 0:256], AF.Copy, [bA[4]], [b_mrow])
                        for jj in range(2):
                            tr(bank(5)[:, jj * NSEQ:(jj + 1) * NSEQ], mrow[0:NSEQ, jj * 128:(jj + 1) * 128], ident_f[0:NSEQ, 0:NSEQ], [b_mrow, b_idf], [bA[5]])
                        for jj in range(2):
                            ch = grp * 2 + jj
                            col = l * 96 + kind * 16 + ch
                            ts(modc[:, kind, ch, :], bank(5)[:, jj * NSEQ:(jj + 1) * NSEQ], bmodc[:, col:col + 1], ALU.add, [bA[5], b_bmodc], [b_modc])
                for s in range(NSEQ):
                    stt(gs1[:, s, :], modc[:, 1, :, s], 1.0, nattn[:, l * 16:(l + 1) * 16], ALU.add, ALU.mult, [b_modc, b_nattn], [b_gs1])
                    stt(gs2[:, s, :], modc[:, 4, :, s], 1.0, nffn[:, l * 16:(l + 1) * 16], ALU.add, ALU.mult, [b_modc, b_nffn], [b_gs2])
                ck("mod")
                P.barrier()
                creset()
                wukraw = carve(2 * 16 * 64, BF16).rearrange("p (r h d) -> p r h d", r=2, h=16)
                b_wukraw = PB("wukraw")
                for rc in range(2):
                    dma("sp", wukraw[:, rc, :, :], wukb_a[l, :, rc * 128:(rc + 1) * 128, :].rearrange("h p d -> p h d"), [dW["uk"][l]], [b_wukraw], b_wukraw)
                    dma("sp", wuv[:, rc, :, :], wuvb_a[l, :, rc * 128:(rc + 1) * 128, :].rearrange("h p d -> p h d"), [dW["uv"][l]], [b_wuv], b_wuv)
                for pp in range(8):
                    for rc in range(2):
                        idx = pp * 2 + rc
                        Bx, bBx = (B6, bB6) if idx < 8 else (B7, bB7)
                        tr(Bx[:, (idx % 8) * 128:(idx % 8 + 1) * 128], wukraw[:, rc, 2 * pp:2 * pp + 2, :].rearrange("p h d -> p (h d)"), ident_b[:], [b_wukraw, b_idb], [bBx])
                act(wukT[:, 0:4, :].rearrange("p a b -> p (a b)"), B6[:], AF.Copy, [bB6], [b_wukT])
                P.add("dve", lambda e: e.tensor_copy(out=wukT[:, 4:8, :].rearrange("p a b -> p (a b)"), in_=B7[:]), [bB7], [b_wukT])

                ck("wuk")
                for s in range(NSEQ):
                    P.barrier()
                    creset()
                    kaT0 = carve(2048, BF16); kaT1 = carve(2048, BF16); kiT = carve(2048, BF16)
                    b_kaT0, b_kaT1, b_kiT = Buf("kaT0"), Buf("kaT1"), Buf("kiT")
                    va = carve(16 * 128, BF16).rearrange("p (b d) -> p b d", b=16); b_va = Buf("va")
                    ckv = carve(16 * 256, BF16).rearrange("p (b d) -> p b d", b=16); b_ckv = Buf("ckv")
                    ckvT = carve(2 * 2048, BF16).rearrange("p (r t) -> p r t", r=2); b_ckvT = Buf("ckvT")
                    qaT = carve(8 * TA, BF16).rearrange("p (c t) -> p c t", c=8); b_qaT = Buf("qaT")
                    qbT = carve(8 * TA, BF16).rearrange("p (c t) -> p c t", c=8); b_qbT = Buf("qbT")
                    qiT = carve(8 * TA, BF16).rearrange("p (c t) -> p c t", c=8); b_qiT = Buf("qiT")
                    acc = carve(2048 * 2, F32); b_acc = Buf("acc")
                    rtmp = [carve(512 * 2, F32) for _ in range(2)]; b_rtmp = [Buf("rtmp0"), Buf("rtmp1")]
                    mneg = carve(2048, BF16); b_mneg = Buf("mneg")
                    Psb = [carve(2048, BF16) for _ in range(2)]; b_Psb = [Buf("P0"), Buf("P1")]
                    PTs = [carve(16 * 128, BF16).rearrange("p (b t) -> p b t", b=16) for _ in range(2)]; b_PTs = [Buf("PT0"), Buf("PT1")]
                    qlat = [carve(4 * 128, BF16).rearrange("p (h r t) -> p h r t", h=2, r=2) for _ in range(2)]; b_qlat = [Buf("ql0"), Buf("ql1")]
                    olat = [carve(2 * 128, BF16).rearrange("p (r t) -> p r t", r=2) for _ in range(2)]; b_olat = [Buf("ol0"), Buf("ol1")]
                    mixtok = carve(2048, BF16); b_mixtok = Buf("mixtok")
                    wi_sb = wi_sb_t[:]; b_wi = Buf("wi")
                    junkf = junkf_t[:]; b_junkf = Buf("junkf")
                    jk8 = carve(1024, BF16).bitcast(mybir.dt.uint8); b_jk8 = Buf("jk8")
                    b_mx = Buf("mx"); b_rs = [Buf("rs0"), Buf("rs1")]; b_rr = [Buf("rr0"), Buf("rr1")]
                    prep = [None]
                    b_st = Buf("stA")
                    b_bis = Buf("bis")

                    for j in range(NTILE):
                        t0 = j * TA
                        norm_tile(s, j, lambda kc: gs1[:, s, kc:kc + 1], None, to_hT(lambda kc: modc[:, 0, kc, s:s + 1]))
                        ck("norm")
                        def load_w(pieces, src_a, dbuf):
                            w = rot("WB", 2)
                            Wv = WB[w][:, 0:8192].rearrange("p (k c) -> p k c", k=16)
                            for (off, c0, n) in pieces:
                                dma("sp", Wv[:, :, off:off + n], src_a[:, c0:c0 + n].rearrange("(k p) c -> p k c", p=128), [dbuf], [b_WB[w]], b_WB[w])
                            return Wv, b_WB[w]

                        def feat_chunk(Wv, bW, cc, dest, b_dest, scale):
                            pb = rot("pj_ps", 4)
                            for kc in range(NKC):
                                mm(bank(pb), Wv[:, kc, cc * 128:(cc + 1) * 128], hT[:, kc, :], kc == 0, kc == NKC - 1, [bW, b_hT], [bA[pb]])
                            copy_alt(dest, bank(pb), [bA[pb]], [b_dest], scale=scale)

                        wl = winb_a[l]
                        for half in range(2):
                            Wv, bW = load_w([(0, half * 512, 512)], wl, dW["in"][l])
                            for cc in range(4):
                                feat_chunk(Wv, bW, cc, qaT[:, half * 4 + cc, :], b_qaT, 0.125)
                        ck("pqa")
                        Wv, bW = load_w([(0, 1024, 64), (64, 1024, 64), (128, 1088, 64), (192, 1088, 64), (256, 3584, 64), (320, 3584, 64)], wl, dW["in"][l])
                        feat_chunk(Wv, bW, 0, kaT0[:, t0:t0 + TA], b_kaT0, None)
                        feat_chunk(Wv, bW, 1, kaT1[:, t0:t0 + TA], b_kaT1, None)
                        feat_chunk(Wv, bW, 2, kiT[:, t0:t0 + TA], b_kiT, None)
                        ck("pkq")
                        for half in range(2):
                            Wv, bW = load_w([(0, 1280 + half * 512, 512)], wl, dW["in"][l])
                            for cc in range(4):
                                feat_chunk(Wv, bW, cc, qbT[:, half * 4 + cc, :], b_qbT, 0.125)
                        for half in range(2):
                            Wv, bW = load_w([(0, 2560 + half * 512, 512)], wl, dW["in"][l])
                            for cc in range(4):
                                feat_chunk(Wv, bW, cc, qiT[:, half * 4 + cc, :], b_qiT, None)
                        ck("pq")
                        Wv, bW = load_w([(0, 1152, 128), (128, 2304, 256), (384, 3600, 64)], wl, dW["in"][l])
                        ck("tm_ld")
                        for tsub in range(NSUB):
                            blk = j * NSUB + tsub
                            pb = rot("pj_ps", 4)
                            for kc in range(NKC):
                                mm(bank(pb)[:, 0:448], hT[:, kc, tsub * 128:(tsub + 1) * 128], Wv[:, kc, 0:448], kc == 0, kc == NKC - 1, [bW, b_hT], [bA[pb]])
                            ck("tm_mm")
                            P.add("dve", lambda e, blk=blk, pb=pb: e.tensor_copy(out=va[:, blk, :], in_=bank(pb)[:, 0:128]), [bA[pb]], [b_va])
                            ck("tm_va")
                            act(wi_sb[:, tsub, :], bank(pb)[:, 432:448], AF.Copy, [bA[pb]], [b_wi])
                            ck("tm_cp")
                            act(junkf[:], bank(pb)[:, 128:384], AF.Square, [bA[pb]], [b_junkf, b_st], accum=st[:, 0:1])
                            ck("tm_sq")
                            act(st[:, 1:2], st[:, 0:1], AF.Sqrt, [b_st], [b_st], bias=EPS, scale=1.0 / 256)
                            P.add("dve", lambda e: e.reciprocal(out=st[:, 2:3], in_=st[:, 1:2]), [b_st], [b_st])
                            ts(ckv[:, blk, :], bank(pb)[:, 128:384], st[:, 2:3], ALU.mult, [bA[pb], b_st], [b_ckv])
                            ck("tm_ckv")
                            for rc in range(2):
                                tr(B6[:, rc * 128:(rc + 1) * 128], ckv[:, blk, rc * 128:(rc + 1) * 128], ident_b[:], [b_ckv, b_idb], [bB6])
                            copy_alt(ckvT[:, :, blk * 128:(blk + 1) * 128], B6[:, 0:256].rearrange("p (r t) -> p r t", r=2), [bB6], [b_ckvT])

                        ck("proj")
                        for tsub in range(NSUB):
                            i = j * NSUB + tsub
                            tc = slice(tsub * 128, (tsub + 1) * 128)
                            nk = 256 if i > 0 else 128
                            ks = (i - 1) * 128 if i > 0 else 0
                            bc0 = 0 if i > 0 else 128
                            nkb = nk // 128
                            for half in range(2):
                                pr = rot("Psb", 2)
                                hb = half * 64
                                for hh in range(8):
                                    h = 2 * hh + half
                                    kaT, b_kaT = (kaT0, b_kaT0) if h < 8 else (kaT1, b_kaT1)
                                    o_ = A03[:, hh * 256:hh * 256 + nk]
                                    mm(o_, qaT[hb:hb + 64, hh, tc], kaT[hb:hb + 64, ks:ks + nk], True, False, [b_qaT, b_kaT], [bA[hh // 2]])
                                    mm(o_, ident_b[:], BA[:, h, bc0:bc0 + nk], False, True, [b_idb, b_BA], [bA[hh // 2]])
                                S3 = A03[:].rearrange("p (h k) -> p h k", h=8)[:, :, 0:nk]
                                P.add("dve", lambda e, S3=S3: e.tensor_reduce(out=st[:, 8:16], in_=S3, axis=AX.X, op=ALU.max), [bA[0], bA[1], bA[2], bA[3], b_st], [b_st])
                                _s = sink_bc[:, l * 16 + half:l * 16 + half + 1]
                                sk = bass.AP(_s.tensor, _s.offset, [list(_s.ap[0]), [2, 8]])
                                tt(st[:, 16:24], st[:, 8:16], sk, ALU.max, [b_st, b_sink], [b_st])
                                ts(st[:, 24:32], st[:, 16:24], -1.0, ALU.mult, [b_st], [b_st])
                                for hh in range(8):
                                    act(Psb[pr][:, hh * 256:hh * 256 + nk], A03[:, hh * 256:hh * 256 + nk], AF.Exp, [bA[hh // 2], b_st], [b_Psb[pr], b_st],
                                        bias=st[:, 24 + hh:25 + hh], scale=1.0, accum=st[:, 32 + hh:33 + hh])
                                tt(st[:, 40:48], sk, st[:, 24:32], ALU.add, [b_st, b_sink], [b_st])
                                act(st[:, 40:48], st[:, 40:48], AF.Exp, [b_st], [b_st])
                                tt(st[:, 40:48], st[:, 40:48], st[:, 32:40], ALU.add, [b_st], [b_st])
                                P.add("dve", lambda e, half=half: e.reciprocal(out=st[:, 48 + half * 8:56 + half * 8], in_=st[:, 40:48]), [b_st], [b_st])
                                pt = rot("PTs", 2)
                                for hh in range(8):
                                    for kb in range(nkb):
                                        sl = hh * 2 + kb
                                        Bx, bBx = (B6, bB6) if sl < 8 else (B7, bB7)
                                        tr(Bx[:, (sl % 8) * 128:(sl % 8 + 1) * 128], Psb[pr][:, hh * 256 + kb * 128:hh * 256 + (kb + 1) * 128], ident_b[:], [b_Psb[pr], b_idb], [bBx])
                                act(PTs[pt][:, 0:8, :].rearrange("p a b -> p (a b)"), B6[:], AF.Copy, [bB6], [b_PTs[pt]])
                                P.add("dve", lambda e, pt=pt: e.tensor_copy(out=PTs[pt][:, 8:16, :].rearrange("p a b -> p (a b)"), in_=B7[:]), [bB7], [b_PTs[pt]])
                                for hh in range(8):
                                    h = 2 * hh + half
                                    g = h // 8
                                    for kb in range(nkb):
                                        kblk = (i - 1 + kb) if i > 0 else 0
                                        mm(A45[:, h * 64:(h + 1) * 64], PTs[pt][:, hh * 2 + kb, :], va[:, kblk, g * 64:(g + 1) * 64], kb == 0, kb == nkb - 1, [b_PTs[pt], b_va], [bA[4 + h // 8]])
                            _a = st[:, 48:64]
                            rden = bass.AP(_a.tensor, _a.offset, [list(_a.ap[0]), [1, 8], [8, 2], [0, 64]])
                            tt(mixtok[:, 0:1024].rearrange("p (h e d) -> p h e d", h=8, e=2), A45[:].rearrange("p (h e d) -> p h e d", h=8, e=2), rden, ALU.mult, [bA[4], bA[5], b_st], [b_mixtok])

                            ck("swa%d" % i)
                            S = 128 * (i + 1)
                            nch = (S + 511) // 512

                            def indexer(ii, tsb):
                                S_ = 128 * (ii + 1)
                                tcc = slice(tsb * 128, (tsb + 1) * 128)
                                for h in range(16):
                                    pch, hb = h // 2, (h % 2) * 64
                                    for c in range((S_ + 511) // 512):
                                        w_ = min(512, S_ - c * 512)
                                        pb = rot("ix_ps", 4)
                                        mm(bank(pb)[:, 0:w_], qiT[hb:hb + 64, pch, tcc], kiT[hb:hb + 64, c * 512:c * 512 + w_], True, True, [b_qiT, b_kiT], [bA[pb]])
                                        rq = rot("rtmp", 2)
                                        act(rtmp[rq][:, 0:w_], bank(pb)[:, 0:w_], AF.Relu, [bA[pb]], [b_rtmp[rq]])
                                        a_ = acc[:, c * 512:c * 512 + w_]
                                        if h == 0:
                                            ts(a_, rtmp[rq][:, 0:w_], wi_sb[:, tsb, 0:1], ALU.mult, [b_rtmp[rq], b_wi], [b_acc])
                                        else:
                                            stt(a_, rtmp[rq][:, 0:w_], wi_sb[:, tsb, h:h + 1], a_, ALU.mult, ALU.add, [b_rtmp[rq], b_wi, b_acc], [b_acc])
                                lo, hi, wd, mid, cnt, tmp = (st[:, 64 + k:65 + k] for k in range(6))
                                P.add("dve", lambda e: e.tensor_reduce(out=lo, in_=acc[:, 0:S_], axis=AX.X, op=ALU.min), [b_acc], [b_bis])
                                P.add("dve", lambda e: e.tensor_reduce(out=hi, in_=acc[:, 0:S_], axis=AX.X, op=ALU.max), [b_acc], [b_bis])
                                tt(acc[:, ii * 128:(ii + 1) * 128], acc[:, ii * 128:(ii + 1) * 128], caus[:], ALU.add, [b_acc, b_caus], [b_acc])
                                tt(wd, hi, lo, ALU.subtract, [b_bis], [b_bis])
                                ts(bs2[:], pw[:], wd, ALU.mult, [b_bis, b_pw], [b_bis])
                                ts(bsn[:], pw[:], wd, ALU.mult, [b_bis, b_pw], [b_bis], s2=-0.5, op1=ALU.mult)
                                tt(mid, lo, bs2[:, 0:1], ALU.add, [b_bis], [b_bis])

                            def bis_iter(ii, k):
                                S_ = 128 * (ii + 1)
                                mid, cnt, tmp = st[:, 67:68], st[:, 68:69], st[:, 69:70]
                                ts(jk8[:, 0:S_], acc[:, 0:S_], mid, ALU.is_ge, [b_acc, b_bis], [b_jk8, b_bis], s2=None, op1=ALU.add, accum=cnt)
                                if k < NIT - 1:
                                    ts(tmp, cnt, 255.5, ALU.is_ge, [b_bis], [b_bis], s2=bs2[:, k:k + 1], op1=ALU.mult)
                                    stt(mid, tmp, bsn[:, k:k + 1], mid, ALU.add, ALU.add, [b_bis], [b_bis])
                                else:
                                    ts(tmp, cnt, 255.5, ALU.is_lt, [b_bis], [b_bis], s2=bs2[:, k:k + 1], op1=ALU.mult)
                                    tt(mid, mid, tmp, ALU.subtract, [b_bis], [b_bis])

                            def bis_final(ii):
                                S_ = 128 * (ii + 1)
                                ts(mneg[:, 0:S_], acc[:, 0:S_], st[:, 67:68], ALU.is_ge, [b_acc, b_bis], [b_mneg], s2=-1.0, op1=ALU.add)

                            if i >= 2 and prep[0] != i:
                                indexer(i, tsub)
                                for k in range(NIT):
                                    bis_iter(i, k)
                                bis_final(i)
                            nxt = (i + 1) if (tsub < NSUB - 1 and i + 1 >= 2) else None
                            if nxt is not None:
                                indexer(nxt, tsub + 1)
                            bis_k = [0]

                            def bis_some(n):
                                if nxt is None:
                                    return
                                for _ in range(n):
                                    if bis_k[0] < NIT:
                                        bis_iter(nxt, bis_k[0])
                                        bis_k[0] += 1

                            def pre_pe(h, ql, hs):
                                for c in range(nch):
                                    c0 = c * 512
                                    w_ = min(512, S - c0)
                                    ops = [(qlat[ql][:, hs, 0, :], ckvT[:, 0, c0:c0 + w_], 0, w_, [b_qlat[ql], b_ckvT]),
                                           (qlat[ql][:, hs, 1, :], ckvT[:, 1, c0:c0 + w_], 0, w_, [b_qlat[ql], b_ckvT])]
                                    if i >= 2:
                                        ops.append((i32k[:], mneg[:, c0:c0 + w_], 0, w_, [b_i32k, b_mneg]))
                                    for kb in ((i - 1, i) if i > 0 else (i,)):
                                        if c0 <= kb * 128 < c0 + w_:
                                            bb = (kb - (i - 1)) if i > 0 else 1
                                            ops.append((ident_b[:], BB[:, h, bb * 128:(bb + 1) * 128], kb * 128 - c0, 128, [b_idb, b_BB]))
                                    for n_, (lh, rh, o0, ow, rd) in enumerate(ops):
                                        mm(bank(c)[:, o0:o0 + ow], lh, rh, n_ == 0, n_ == len(ops) - 1, rd, [bA[c]])

                            def pre_dve(h):
                                banks_r = [bA[c] for c in range(nch)]
                                P.add("dve", lambda e, S=S: e.tensor_reduce(out=st[:, 72:73], in_=A03[:, 0:S], axis=AX.X, op=ALU.max), banks_r, [b_mx])
                                ts(st[:, 73:74], st[:, 72:73], -1.0, ALU.mult, [b_mx], [b_mx])

                            def pre_act(h):
                                banks_r = [bA[c] for c in range(nch)]
                                pr = h % 2
                                act(Psb[pr][:, 0:S], A03[:, 0:S], AF.Exp, banks_r + [b_mx], [b_Psb[pr], b_rs[h % 2]], bias=st[:, 73:74], scale=1.0, accum=st[:, 80 + h:81 + h])

                            def post_T_pe(h):
                                pr = h % 2
                                for b_ in range(i + 1):
                                    Bx, bBx = (B6, bB6) if b_ < 8 else (B7, bB7)
                                    tr(Bx[:, (b_ % 8) * 128:(b_ % 8 + 1) * 128], Psb[pr][:, b_ * 128:(b_ + 1) * 128], ident_b[:], [b_Psb[pr], b_idb], [bBx])

                            def post_T_act(h):
                                pt = h % 2
                                n6 = min(8, i + 1)
                                act(PTs[pt][:, 0:n6, :].rearrange("p a b -> p (a b)"), B6[:, 0:n6 * 128], AF.Copy, [bB6], [b_PTs[pt]])

                            def post_T_dve(h):
                                pt = h % 2
                                if i + 1 > 8:
                                    n7 = i + 1 - 8
                                    P.add("dve", lambda e, pt=pt, n7=n7: e.tensor_copy(out=PTs[pt][:, 8:8 + n7, :].rearrange("p a b -> p (a b)"), in_=B7[:, 0:n7 * 128]), [bB7], [b_PTs[pt]])

                            def post_rest(h):
                                pt = h % 2
                                for rc in range(2):
                                    for b_ in range(i + 1):
                                        mm(bank(5)[:, rc * 128:(rc + 1) * 128], ckv[:, b_, rc * 128:(rc + 1) * 128], PTs[pt][:, b_, :], b_ == 0, b_ == i, [b_ckv, b_PTs[pt]], [bA[5]])
                                ol = rot("olat", 2)
                                for rc in range(2):
                                    ts(olat[ol][:, rc, :], bank(5)[:, rc * 128:(rc + 1) * 128], kvn[:, l * 2 + rc:l * 2 + rc + 1], ALU.mult, [bA[5], b_kvn], [b_olat[ol]])
                                for rc in range(2):
                                    mm(bank(5)[:, 256:320], olat[ol][:, rc, :], wuv[:, rc, h, :], rc == 0, rc == 1, [b_olat[ol], b_wuv], [bA[5]])
                                P.add("dve", lambda e, h=h: e.reciprocal(out=st[:, 96 + h:97 + h], in_=st[:, 80 + h:81 + h]), [b_rs[h % 2]], [b_rr[h % 2]])
                                ts(mixtok[:, 1024 + h * 64:1024 + (h + 1) * 64], bank(5)[:, 256:320], st[:, 96 + h:97 + h], ALU.mult, [bA[5], b_rr[h % 2]], [b_mixtok])

                            prev = None
                            for pp in range(8):
                                ql = rot("qlat", 2)
                                for hs in range(2):
                                    hb = hs * 64
                                    for rc in range(2):
                                        mm(bank(4 + hs)[:, rc * 128:(rc + 1) * 128], wukT[hb:hb + 64, pp, rc * 128:(rc + 1) * 128], qbT[hb:hb + 64, pp, tc], True, True, [b_wukT, b_qbT], [bA[4 + hs]])
                                for hs in range(2):
                                    for rc in range(2):
                                        ts(qlat[ql][:, hs, rc, :], bank(4 + hs)[:, rc * 128:(rc + 1) * 128], kvn[:, l * 2 + rc:l * 2 + rc + 1], ALU.mult, [bA[4 + hs], b_kvn], [b_qlat[ql]])
                                for hs in range(2):
                                    h = pp * 2 + hs
                                    pre_pe(h, ql, hs)
                                    if prev is not None:
                                        post_T_pe(prev)
                                        post_T_act(prev)
                                    pre_dve(h)
                                    pre_act(h)
                                    if prev is not None:
                                        post_T_dve(prev)
                                        post_rest(prev)
                                    bis_some(2)
                                    prev = h
                            post_T_pe(prev)
                            post_T_act(prev)
                            post_T_dve(prev)
                            post_rest(prev)
                            if nxt is not None:
                                bis_some(NIT)
                                bis_final(nxt)
                                prep[0] = nxt
                            ck("dsa%d" % i)
                            for c in range(16):
                                Bx, bBx = (B6, bB6) if c < 8 else (B7, bB7)
                                tr(Bx[:, (c % 8) * 128:(c % 8 + 1) * 128], mixtok[:, c * 128:(c + 1) * 128], ident_b[:], [b_mixtok, b_idb], [bBx])
                            act(hT[:, 0:8, tc], B6[:].rearrange("p (c t) -> p c t", c=8), AF.Copy, [bB6], [b_hT])
                            P.add("dve", lambda e, tc=tc: e.tensor_copy(out=hT[:, 8:16, tc], in_=B7[:].rearrange("p (c t) -> p c t", c=8)), [bB7], [b_hT])

                        ck("attn%d" % j)
                        for g in range(4):
                            w = rot("WB", 2)
                            Wv = WB[w][:, 0:8192].rearrange("p (k c) -> p k c", k=16)
                            dma("sp", Wv, woutb_a[l][:, g * 512:(g + 1) * 512].rearrange("(k p) c -> p k c", p=128), [dW["out"][l]], [b_WB[w]], b_WB[w])
                            for mm_ in range(4):
                                m = g * 4 + mm_
                                pb = rot("pj_ps", 4)
                                for kc in range(NKC):
                                    mm(bank(pb), Wv[:, kc, mm_ * 128:(mm_ + 1) * 128], hT[:, kc, :], kc == 0, kc == NKC - 1, [b_WB[w], b_hT], [bA[pb]])
                                k = rot("xb", 4)
                                dma("sp", xb[k][:], xT_a[s, m, :, t0:t0 + TA], [dX[s][j][m % 2]], [b_xb[k]], b_xb[k])
                                stt(xb[k][:], bank(pb), modc[:, 2, m, s:s + 1], xb[k][:], ALU.mult, ALU.add, [bA[pb], b_modc, b_xb[k]], [b_xb[k]])
                                dma("sp", xT_a[s, m, :, t0:t0 + TA], xb[k][:], [b_xb[k]], [dX[s][j][m % 2]], dX[s][j][m % 2])

                    ck("att")
                    P.barrier()
                    creset()
                    actT = carve(NFC * TA, BF16).rearrange("p (c t) -> p c t", c=NFC); b_actT = Buf("actT")
                    a_t = [carve(TA * 2, F32) for _ in range(2)]; b_at = [Buf("at0"), Buf("at1")]
                    s_t = [carve(TA * 2, F32) for _ in range(2)]; b_stt = [Buf("st0"), Buf("st1")]
                    WF = [WB[0][:, 0:8192], WB[1][:, 0:8192], carve(8192, BF16), carve(8192, BF16)]
                    b_WF = [b_WB[0], b_WB[1], PB("WF2"), PB("WF3")]
                    gs2f = lambda kc: gs2[:, s, kc:kc + 1]
                    hT2 = to_hT(lambda kc: modc[:, 3, kc, s:s + 1])
                    for j in range(NTILE):
                        t0 = j * TA
                        if j == 0 or not FFN_OVERLAP:
                            norm_tile(s, j, gs2f, None, hT2)
                        for q in range(11):
                            wu = rot("WF", 4)
                            Wu = WF[wu].rearrange("p (k c) -> p k c", k=16)
                            dma("sp", Wu, wupb_a[l][:, q * 512:(q + 1) * 512].rearrange("(k p) c -> p k c", p=128), [dW["up"][l]], [b_WF[wu]], b_WF[wu])
                            wg = rot("WF", 4)
                            Wg = WF[wg].rearrange("p (k c) -> p k c", k=16)
                            dma("sp", Wg, wupb_a[l][:, FF + q * 512:FF + (q + 1) * 512].rearrange("(k p) c -> p k c", p=128), [dW["up"][l]], [b_WF[wg]], b_WF[wg])
                            for cc in range(4):
                                c = q * 4 + cc
                                pu = rot("ff_ps", 3) * 2
                                pg = pu + 1
                                for kc in range(NKC):
                                    mm(bank(pu), Wu[:, kc, cc * 128:(cc + 1) * 128], hT[:, kc, :], kc == 0, kc == NKC - 1, [b_WF[wu], b_hT], [bA[pu]])
                                for kc in range(NKC):
                                    mm(bank(pg), Wg[:, kc, cc * 128:(cc + 1) * 128], hT[:, kc, :], kc == 0, kc == NKC - 1, [b_WF[wg], b_hT], [bA[pg]])
                                ai = rot("a_t", 2)
                                a_ = a_t[ai]
                                w0 = convw[:, (l * 3 + 0) * NFC + c:(l * 3 + 0) * NFC + c + 1]
                                w1 = convw[:, (l * 3 + 1) * NFC + c:(l * 3 + 1) * NFC + c + 1]
                                w2 = convw[:, (l * 3 + 2) * NFC + c:(l * 3 + 2) * NFC + c + 1]
                                cb = convb[:, l * NFC + c:l * NFC + c + 1]
                                up = bank(pu)
                                act(a_[:, 0:TA], up, AF.Identity, [bA[pu], b_convw, b_convb], [b_at[ai]], bias=cb, scale=w2)
                                stt(a_[:, 1:TA], up[:, 0:TA - 1], w1, a_[:, 1:TA], ALU.mult, ALU.add, [bA[pu], b_at[ai], b_convw], [b_at[ai]])
                                stt(a_[:, 2:TA], up[:, 0:TA - 2], w0, a_[:, 2:TA], ALU.mult, ALU.add, [bA[pu], b_at[ai], b_convw], [b_at[ai]])
                                if j > 0:
                                    stt(a_[:, 0:1], uh[:, c, 1:2], w1, a_[:, 0:1], ALU.mult, ALU.add, [b_uh, b_at[ai], b_convw], [b_at[ai]])
                                    stt(a_[:, 0:2], uh[:, c, 0:2], w0, a_[:, 0:2], ALU.mult, ALU.add, [b_uh, b_at[ai], b_convw], [b_at[ai]])
                                P.add("dve", lambda e, c=c, up=up: e.tensor_copy(out=uh[:, c, :], in_=up[:, TA - 2:TA]), [bA[pu]], [b_uh])
                                si = rot("s_t", 2)
                                act(s_t[si][:, 0:TA], a_[:, 0:TA], AF.Silu, [b_at[ai]], [b_stt[si]])
                                tt(actT[:, c, :], s_t[si][:, 0:TA], bank(pg), ALU.mult, [b_stt[si], bA[pg]], [b_actT])
                        for mp in range(8):
                            wds = []
                            for hf in range(2):
                                wd = rot("WF", 4)
                                Wd = WF[wd][:, 0:22 * 256].rearrange("p (c m) -> p c m", c=22)
                                dma("sp", Wd, wdownb_a[l][hf * 2816:(hf + 1) * 2816, mp * 256:(mp + 1) * 256].rearrange("(c p) m -> p c m", p=128), [dW["down"][l]], [b_WF[wd]], b_WF[wd])
                                wds.append((Wd, b_WF[wd]))
                            for mm_ in range(2):
                                m = mp * 2 + mm_
                                pb = rot("pj_ps", 4)
                                for c in range(NFC):
                                    Wd, bWd = wds[c // 22]
                                    mm(bank(pb), Wd[:, c % 22, mm_ * 128:(mm_ + 1) * 128], actT[:, c, :], c == 0, c == NFC - 1, [bWd, b_actT], [bA[pb]])
                                k = rot("xb", 4)
                                dma("sp", xb[k][:], xT_a[s, m, :, t0:t0 + TA], [dX[s][j][m % 2]], [b_xb[k]], b_xb[k])
                                stt(xb[k][:], bank(pb), modc[:, 5, m, s:s + 1], xb[k][:], ALU.mult, ALU.add, [bA[pb], b_modc, b_xb[k]], [b_xb[k]])
                                dma("sp", xT_a[s, m, :, t0:t0 + TA], xb[k][:], [b_xb[k]], [dX[s][j][m % 2]], dX[s][j][m % 2])

            ck("layers")
            P.barrier()
            creset()
            orow = carve(4 * 2048 * 2, F32).rearrange("p (a b) -> p a b", a=4)
            b_orow = [Buf(f"orow{i}") for i in range(4)]
            for s in range(NSEQ):
                for j in range(NTILE):
                    def fin(kc, tm, b_tm):
                        pb = rot("xt_ps", 4)
                        for tsub in range(NSUB):
                            tr(bank(pb)[:, tsub * 128:(tsub + 1) * 128], tm[:, tsub * 128:(tsub + 1) * 128], ident_f[:], [b_tm, b_idf], [bA[pb]])
                        for tsub in range(NSUB):
                            copy_alt(orow[:, tsub, kc * 128:(kc + 1) * 128], bank(pb)[:, tsub * 128:(tsub + 1) * 128], [bA[pb]], [b_orow[tsub]])
                    norm_tile(s, j, lambda kc: nfin[:, kc:kc + 1], None, fin)
                    for tsub in range(NSUB):
                        r0 = j * TA + tsub * 128
                        dma("sp", out_a[s, r0:r0 + 128, :], orow[:, tsub, :], [b_orow[tsub]], [dOUT], dOUT)

        try:
            _body()
        except _Stop:
            pass
        P.add("sp", None, [dOUT, dDBG], [])
        P.emit(es)
    return nc, P


_CACHE = {}


def kernel(**inputs):
    n = 8
    x = np.ascontiguousarray(np.asarray(inputs["x"], dtype=np.float32))
    c = np.ascontiguousarray(np.asarray(inputs["c"], dtype=np.float32))
    if "nc" not in _CACHE:
        _CACHE["nc"] = build()[0]
    nc = _CACHE["nc"]
    oh = _onehot_const().reshape(33, 768)
    shared = {k: np.ascontiguousarray(np.asarray(inputs[k], dtype=np.float32)) for k in
              ("w_mod", "b_mod", "norm_attn", "w_in", "attn_sinks", "kv_norm", "w_uk", "w_uv", "rel_bias",
               "w_out", "norm_ffn", "w_up", "conv_w", "conv_b", "w_down", "norm_final")}
    in_maps = []
    for i in range(n):
        m = dict(shared)
        m["x"] = x[2 * i:2 * i + 2]
        m["c"] = c[2 * i:2 * i + 2]
        m["oh"] = oh
        in_maps.append(m)
    res = run_bass_kernel_spmd(nc, in_maps, core_ids=list(range(n)))
    return np.concatenate([r["out"] for r in res.results], axis=0)
```

```python
import math
from contextlib import ExitStack
import numpy as np
import concourse.bass as bass
import concourse.mybir as mybir
from concourse.bass_utils import run_bass_kernel_spmd

F32 = mybir.dt.float32
BF16 = mybir.dt.bfloat16
AF = mybir.ActivationFunctionType
ALU = mybir.AluOpType
AX = mybir.AxisListType

D = 2048
T = 2048
NKC = 16
TA = 512
NTILE = T // TA
NSUB = TA // 128
FF = 5632
NFC = 44
INW = 3664
EPS = 1e-6
NIT = 26
UN = 50432
FFN_OVERLAP = True


class Buf:
    __slots__ = ("name", "lw", "rdc", "rdd", "dsem", "dcnt", "excl")

    def __init__(self, name, excl=False):
        self.name = name
        self.excl = excl
        self.lw = None
        self.rdc = {}
        self.rdd = []
        self.dsem = None
        self.dcnt = 0


class Op:
    __slots__ = ("eng", "fn", "dma", "deps", "tok", "inc", "idx")


class Prog:
    ENGS = ("pe", "act", "dve", "pool", "sp")

    def __init__(self, nc):
        self.nc = nc
        self.ops = []
        self.last = {e: None for e in self.ENGS}
        self.dmas_since_bar = []
        self.dbufs = []

    def add(self, eng, fn, reads=(), writes=(), dma_dst=None):
        o = Op()
        o.eng = eng
        o.fn = fn
        o.dma = dma_dst is not None
        o.inc = False
        o.idx = len(self.ops)
        o.tok = None
        deps = set()
        for b in reads:
            if b.lw is not None:
                deps.add(b.lw)
            if b.excl:
                for e, r in b.rdc.items():
                    if e != eng:
                        deps.add(r)
        for b in writes:
            if b.lw is not None:
                deps.add(b.lw)
            for e, r in b.rdc.items():
                if e == eng and not o.dma:
                    continue
                deps.add(r)
            for r in b.rdd:
                deps.add(r)
        if eng == "pe" and not o.dma:
            deps = {d for d in deps if d.dma or d.eng != "pe"}
        o.deps = deps
        for d in deps:
            if not d.dma:
                d.inc = True
        if o.dma:
            if dma_dst.dsem is None:
                dma_dst.dsem = "pending"
                self.dbufs.append(dma_dst)
            dma_dst.dcnt += 16
            o.tok = (dma_dst, dma_dst.dcnt)
            self.dmas_since_bar.append(o)
        for b in reads:
            if o.dma:
                b.rdd.append(o)
            else:
                b.rdc[eng] = o
        for b in writes:
            b.lw = o
            b.rdc = {}
            b.rdd = []
        self.ops.append(o)
        self.last[eng] = o
        return o

    def barrier(self):
        lasts = [o for o in self.last.values() if o is not None]
        pend = list(self.dmas_since_bar)
        self.dmas_since_bar = []
        for e in self.ENGS:
            o = Op()
            o.eng = e
            o.fn = None
            o.dma = False
            o.inc = False
            o.idx = len(self.ops)
            o.tok = None
            o.deps = set(x for x in lasts if (x.dma or x.eng != e)) | set(pend)
            for d in o.deps:
                if not d.dma:
                    d.inc = True
            self.ops.append(o)
            self.last[e] = o

    def emit(self, es):
        nc = self.nc
        esem = {}
        for e in ("pe", "act", "dve", "pool"):
            esem[e] = es.enter_context(nc.semaphore("es_" + e))
        for b in self.dbufs:
            b.dsem = es.enter_context(nc.semaphore("ds%d_%s" % (self.dbufs.index(b), b.name)))
        cnt = {e: 0 for e in self.ENGS}
        for o in self.ops:
            if o.dma or o.fn is None:
                continue
            if o.inc:
                cnt[o.eng] += 1
                o.tok = (o.eng, cnt[o.eng])
        by_eng = {e: [o for o in self.ops if o.eng == e] for e in self.ENGS}
        self.stats = {e: len(v) for e, v in by_eng.items()}

        def run(ename, eng):
            seen = {}
            for o in by_eng[ename]:
                waits = {}
                for d in o.deps:
                    if d.tok is None:
                        continue
                    key, val = d.tok
                    kid = key if isinstance(key, str) else id(key)
                    if kid not in waits or waits[kid][1] < val:
                        waits[kid] = (key, val)
                for kid, (key, val) in waits.items():
                    if seen.get(kid, 0) >= val:
                        continue
                    seen[kid] = val
                    sem = esem[key] if isinstance(key, str) else key.dsem
                    eng.wait_ge(sem, val)
                if o.fn is None:
                    continue
                inst = o.fn(eng)
                if o.dma:
                    inst.then_inc(o.tok[0].dsem, 16)
                elif o.inc:
                    inst.then_inc(esem[ename], 1)

        with nc.Block() as block:
            @block.tensor
            def _(eng):
                run("pe", eng)

            @block.scalar
            def _(eng):
                run("act", eng)

            @block.vector
            def _(eng):
                run("dve", eng)

            @block.gpsimd
            def _(eng):
                run("pool", eng)

            @block.sync
            def _(eng):
                run("sp", eng)


def _rel_bucket_np(n):
    n = np.maximum(n, 0)
    max_exact = 16
    nf = np.maximum(n, 1).astype(np.float32)
    v = (np.log(nf / np.float32(max_exact)) / np.float32(math.log(128 / max_exact))
         * np.float32(32 - max_exact)).astype(np.float32)
    large = max_exact + v.astype(np.int32)
    large = np.minimum(large, 31)
    return np.where(n < max_exact, n, large)


def _onehot_const():
    oh = np.zeros((33, 2, 384), np.float32)
    for u in range(383):
        dist = 255 - u
        if 0 <= dist < 128:
            oh[int(_rel_bucket_np(np.array(dist))), 0, u] = 1.0
        else:
            oh[32, 0, u] = -30000.0
        if dist >= 0:
            oh[int(_rel_bucket_np(np.array(dist))), 1, u] += 1.0
            oh[31, 1, u] -= 1.0
        else:
            oh[32, 1, u] = -30000.0
    oh[32, :, 383] = -30000.0
    return oh


class _Stop(Exception):
    pass


def build(NL=4, NSEQ=2, debug=False, stop=None):
    nc = bass.Bass("TRN2", target_bir_lowering=False)
    P = Prog(nc)
    dbg_d = nc.dram_tensor("dbg", [16, 128, 2048], F32, kind="ExternalOutput") if debug else None
    dDBG = Buf("dDBG")

    def ck(name):
        if stop == name:
            raise _Stop()

    def dump(slot, ap, bufs, n):
        if debug:
            P.add("pool", lambda e: e.dma_start(out=dbg_d.ap()[slot, 0:ap.shape[0], 0:n], in_=ap), bufs, [dDBG], dma_dst=dDBG)

    def din(name, shape):
        return nc.dram_tensor(name, list(shape), F32, kind="ExternalInput")

    x_d = din("x", [NSEQ, T, D])
    c_d = din("c", [NSEQ, D])
    wmod_d = din("w_mod", [NL, D, 6 * D])
    bmod_d = din("b_mod", [NL, 6 * D])
    nattn_d = din("norm_attn", [NL, D])
    win_d = din("w_in", [NL, D, INW])
    sinks_d = din("attn_sinks", [NL, 16])
    kvn_d = din("kv_norm", [NL, 256])
    wuk_d = din("w_uk", [NL, 16, 256, 64])
    wuv_d = din("w_uv", [NL, 16, 256, 64])
    rel_d = din("rel_bias", [32, 32])
    wout_d = din("w_out", [NL, D, D])
    nffn_d = din("norm_ffn", [NL, D])
    wup_d = din("w_up", [NL, D, 2 * FF])
    convw_d = din("conv_w", [NL, 3, FF])
    convb_d = din("conv_b", [NL, FF])
    wdown_d = din("w_down", [NL, FF, D])
    nfin_d = din("norm_final", [D])
    oh_d = din("oh", [33, 768])
    out_d = nc.dram_tensor("out", [NSEQ, T, D], F32, kind="ExternalOutput")

    xT_d = nc.dram_tensor("xT", [NSEQ, NKC, 128, T], F32, kind="Internal")
    fv_d = nc.dram_tensor("fv", [32, 384], F32, kind="Internal")
    winb_d = nc.dram_tensor("winb", [NL, D, INW], BF16, kind="Internal")
    woutb_d = nc.dram_tensor("woutb", [NL, D, D], BF16, kind="Internal")
    wupb_d = nc.dram_tensor("wupb", [NL, D, 2 * FF], BF16, kind="Internal")
    wdownb_d = nc.dram_tensor("wdownb", [NL, FF, D], BF16, kind="Internal")
    wukb_d = nc.dram_tensor("wukb", [NL, 16, 256, 64], BF16, kind="Internal")
    wuvb_d = nc.dram_tensor("wuvb", [NL, 16, 256, 64], BF16, kind="Internal")

    x_a, c_a, wmod_a, win_a = x_d.ap(), c_d.ap(), wmod_d.ap(), win_d.ap()
    out_a, xT_a = out_d.ap(), xT_d.ap()
    winb_a, woutb_a, wupb_a, wdownb_a = winb_d.ap(), woutb_d.ap(), wupb_d.ap(), wdownb_d.ap()
    wukb_a, wuvb_a = wukb_d.ap(), wuvb_d.ap()

    dX = [[[Buf(f"dX{s}_{j}_{q}") for q in range(2)] for j in range(NTILE)] for s in range(NSEQ)]
    dW = {k: [Buf(f"dW{k}{l}") for l in range(4)] for k in ("in", "out", "up", "down", "uk", "uv")}
    dFV = Buf("dFV")
    dOUT = Buf("dOUT")

    with ExitStack() as es:
        def sb(name, shape, dt):
            return es.enter_context(nc.sbuf_tensor(name, list(shape), dt))

        def pst(name, shape, dt):
            return es.enter_context(nc.psum_tensor(name, list(shape), dt))

        A03 = pst("A03", [128, 2048], F32)
        A45 = pst("A45", [128, 1024], F32)
        B6 = pst("B6", [128, 1024], BF16)
        B7 = pst("B7", [128, 1024], BF16)
        bA = [Buf(f"bA{i}", excl=True) for i in range(6)]
        bB6, bB7 = Buf("bB6", excl=True), Buf("bB7", excl=True)

        def bank(i):
            if i < 4:
                return A03[:, i * 512:(i + 1) * 512]
            return A45[:, (i - 4) * 512:(i - 3) * 512]

        ident_f = sb("ident_f", [128, 128], F32); b_idf = Buf("idf")
        ident_b = sb("ident_b", [128, 128], BF16); b_idb = Buf("idb")
        i32k = sb("i32k", [128, 128], BF16); b_i32k = Buf("i32k")
        J_f = sb("J_f", [128, 128], F32); b_J = Buf("J")
        ones_f = sb("ones_f", [128, 128], F32); b_ones = Buf("ones")
        zer_f = sb("zer_f", [128, 128], F32); b_zer = Buf("zer")
        caus = sb("caus", [128, 128], F32); b_caus = Buf("caus")
        BA = sb("BA", [128, 16, 256], BF16); b_BA = Buf("BA")
        BB = sb("BB", [128, 16, 256], BF16); b_BB = Buf("BB")
        bmodc = sb("bmodc", [128, 384], F32); b_bmodc = Buf("bmodc")
        nattn = sb("nattn", [128, 64], F32); b_nattn = Buf("nattn")
        nffn = sb("nffn", [128, 64], F32); b_nffn = Buf("nffn")
        kvn = sb("kvn", [128, 8], F32); b_kvn = Buf("kvn")
        convw = sb("convw", [128, 528], F32); b_convw = Buf("convw")
        convb = sb("convb", [128, 176], F32); b_convb = Buf("convb")
        nfin = sb("nfin", [128, 16], F32); b_nfin = Buf("nfin")
        cTs = sb("cTs", [128, 32], F32); b_cTs = Buf("cTs")
        cT2 = sb("cT2", [128, 16, NSEQ], F32); b_cT2 = Buf("cT2")
        sink_bc = sb("sink_bc", [128, 64], F32); b_sink = Buf("sink")
        modc = sb("modc", [128, 6, 16, NSEQ], F32); b_modc = Buf("modc")
        gs1 = sb("gs1", [128, NSEQ, 16], F32); b_gs1 = Buf("gs1")
        gs2 = sb("gs2", [128, NSEQ, 16], F32); b_gs2 = Buf("gs2")
        wukT = sb("wukT", [128, 8, 256], BF16); b_wukT = Buf("wukT")
        wuv = sb("wuv", [128, 2, 16, 64], BF16); b_wuv = Buf("wuv")
        mrow = sb("mrow", [2, 256], F32); b_mrow = Buf("mrow")
        uh = sb("uh", [128, NFC, 2], F32); b_uh = Buf("uh")

        hT = sb("hT", [128, 16, TA], BF16); b_hT = Buf("hT")
        WB = [sb(f"WB{i}", [128, 8192], BF16) for i in range(2)]
        b_WB = [Buf(f"WB{i}") for i in range(2)]
        xb = [sb(f"xb{i}", [128, TA], F32) for i in range(4)]
        b_xb = [Buf(f"xb{i}") for i in range(4)]
        sqb = [sb(f"sqb{i}", [128, TA], F32) for i in range(2)]
        b_sqb = [Buf(f"sqb{i}") for i in range(2)]
        rstd = sb("rstd", [128, TA], F32); b_rstd = Buf("rstd")
        tmpn = [sb(f"tmpn{i}", [128, TA], F32) for i in range(2)]
        b_tmpn = [Buf(f"tmpn{i}") for i in range(2)]
        st = sb("st", [128, 128], F32)
        pw = sb("pw", [128, NIT + 1], F32); b_pw = Buf("pw")
        bs2 = sb("bs2", [128, NIT + 1], F32)
        bsn = sb("bsn", [128, NIT + 1], F32)
        wi_sb_t = sb("wi_sb", [128, NSUB, 16], F32)
        junkf_t = sb("junkf", [128, 256], F32)
        U = sb("U", [128, UN], BF16)

        ucur = [0]

        def carve(nbytes_elems, dt, shape=None):
            n2 = nbytes_elems
            a = U[:, ucur[0]:ucur[0] + n2]
            ucur[0] += n2
            assert ucur[0] <= UN, ucur[0]
            if dt == F32:
                a = a.bitcast(F32)
            return a

        def creset():
            ucur[0] = 0

        rr = {}
        _pb = {}

        def PB(name):
            if name not in _pb:
                _pb[name] = Buf(name)
            return _pb[name]

        def rot(key, n):
            v = rr.get(key, 0)
            rr[key] = v + 1
            return v % n

        def dma(q, out, in_, reads, writes, dst):
            P.add(q, lambda e: e.dma_start(out=out, in_=in_), reads, writes, dma_dst=dst)

        def act(out, in_, func, reads, writes, bias=None, scale=None, accum=None):
            kw = {}
            if bias is not None:
                kw["bias"] = bias
            if scale is not None:
                kw["scale"] = scale
            if accum is not None:
                kw["accum_out"] = accum
            P.add("act", lambda e: e.activation(out=out, in_=in_, func=func, **kw), reads, writes)

        def ts(out, in0, s1, op0, reads, writes, s2=None, op1=None, accum=None, eng="dve"):
            kw = {}
            if op1 is not None:
                kw["op1"] = op1
            if accum is not None:
                kw["accum_out"] = accum
            P.add(eng, lambda e: e.tensor_scalar(out=out, in0=in0, scalar1=s1, scalar2=s2, op0=op0, **kw), reads, writes)

        def tt(out, in0, in1, op, reads, writes, eng="dve"):
            P.add(eng, lambda e: e.tensor_tensor(out=out, in0=in0, in1=in1, op=op), reads, writes)

        def stt(out, in0, scalar, in1, op0, op1, reads, writes):
            P.add("dve", lambda e: e.scalar_tensor_tensor(out=out, in0=in0, scalar=scalar, in1=in1, op0=op0, op1=op1), reads, writes)

        def mm(out, lhsT, rhs, start, stop, reads, writes):
            P.add("pe", lambda e: e.matmul(out, lhsT=lhsT, rhs=rhs, start=start, stop=stop), reads, writes)

        def tr(out, in_, ident, reads, writes):
            P.add("pe", lambda e: e.transpose(out=out, in_=in_, identity=ident), reads, writes)

        def copy_alt(out, in_, reads, writes, scale=None):
            if rot("cp", 2) == 0:
                if scale is None:
                    act(out, in_, AF.Copy, reads, writes)
                else:
                    act(out, in_, AF.Copy, reads, writes, scale=scale)
            else:
                if scale is None:
                    P.add("dve", lambda e: e.tensor_copy(out=out, in_=in_), reads, writes)
                else:
                    ts(out, in_, scale, ALU.mult, reads, writes)

        def _body():
            P.add("pool", lambda e: e.memset(ones_f[:], 1.0), [], [b_ones])
            P.add("pool", lambda e: e.memset(zer_f[:], 0.0), [], [b_zer])
            P.add("pool", lambda e: e.memset(st[:], 0.0), [], [])
            P.add("pool", lambda e: e.affine_select(out=ident_f[:], in_=ones_f[:], pattern=[[-1, 128]], compare_op=ALU.is_equal, fill=0.0, base=0, channel_multiplier=1), [b_ones], [b_idf])
            P.add("pool", lambda e: e.affine_select(out=J_f[:], in_=ones_f[:], pattern=[[1, 128]], compare_op=ALU.is_equal, fill=0.0, base=-127, channel_multiplier=1), [b_ones], [b_J])
            P.add("pool", lambda e: e.affine_select(out=caus[:], in_=zer_f[:], pattern=[[-1, 128]], compare_op=ALU.is_ge, fill=-1e30, base=0, channel_multiplier=1), [b_zer], [b_caus])
            P.add("dve", lambda e: e.tensor_copy(out=ident_b[:], in_=ident_f[:]), [b_idf], [b_idb])
            ts(i32k[:], ident_f[:], 32768.0, ALU.mult, [b_idf], [b_i32k])
            P.add("pool", lambda e: e.memset(uh[:], 0.0), [], [b_uh])
            for k in range(NIT + 1):
                P.add("pool", lambda e, k=k: e.memset(pw[:, k:k + 1], 2.0 ** -(k + 1)), [], [b_pw])

            ck("c0")
            def cast_copy(src_t, dst_t, l, nelem, dbuf):
                rows = nelem // 2048
                r0 = 0
                while r0 < rows:
                    n = min(2048, rows - r0)
                    si = bass.AP(src_t, l * nelem + r0 * 2048, [[2048, n], [1, 2048]])
                    di = bass.AP(dst_t, l * nelem + r0 * 2048, [[2048, n], [1, 2048]])
                    dma("pool", di, si, [], [dbuf], dbuf)
                    r0 += n

            for l in range(NL):
                cast_copy(win_d, winb_d, l, D * INW, dW["in"][l])
                cast_copy(wuk_d, wukb_d, l, 16 * 256 * 64, dW["uk"][l])
                cast_copy(wuv_d, wuvb_d, l, 16 * 256 * 64, dW["uv"][l])
                cast_copy(wout_d, woutb_d, l, D * D, dW["out"][l])
                cast_copy(wup_d, wupb_d, l, D * 2 * FF, dW["up"][l])
                cast_copy(wdown_d, wdownb_d, l, FF * D, dW["down"][l])

            ck("cast")
            rowbuf = [carve(256, F32) for i in range(2)]
            b_rowbuf = [Buf(f"rowbuf{i}") for i in range(2)]

            def col_load(dst, b_dst, src_t, nrows):
                r0 = 0
                while r0 < nrows:
                    nb = min(128, nrows - r0)
                    k = rot("rowbuf", 2)
                    src = bass.AP(src_t, r0 * 128, [[128, nb], [1, 128]])
                    dma("sp", rowbuf[k][0:nb, :], src, [], [b_rowbuf[k]], b_rowbuf[k])
                    pb = rot("cl_ps", 2)
                    tr(bank(4 + pb)[:, 0:nb], rowbuf[k][0:nb, :], ident_f[0:nb, 0:nb], [b_rowbuf[k], b_idf], [bA[4 + pb]])
                    copy_alt(dst[:, r0:r0 + nb], bank(4 + pb)[:, 0:nb], [bA[4 + pb]], [b_dst])
                    r0 += nb

            col_load(bmodc, b_bmodc, bmod_d, NL * 96)
            col_load(nattn, b_nattn, nattn_d, NL * 16)
            col_load(nffn, b_nffn, nffn_d, NL * 16)
            col_load(kvn, b_kvn, kvn_d, NL * 2)
            col_load(convw, b_convw, convw_d, NL * 132)
            col_load(convb, b_convb, convb_d, NL * 44)
            col_load(nfin, b_nfin, nfin_d, 16)
            col_load(cTs, b_cTs, c_d, NSEQ * 16)
            act(cTs[:, 0:NSEQ * 16], cTs[:, 0:NSEQ * 16], AF.Silu, [b_cTs], [b_cTs])
            P.add("dve", lambda e: e.tensor_copy(out=cT2[:], in_=cTs[:, 0:NSEQ * 16].rearrange("p (b k) -> p k b", b=NSEQ)), [b_cTs], [b_cT2])
            dma("sp", sink_bc[:, 0:NL * 16], bass.AP(sinks_d, 0, [[0, 128], [1, NL * 16]]), [], [b_sink], b_sink)

            ck("cols")
            rel_aug = sb("rel_aug", [33, 32], F32); b_rel = Buf("rel_aug")
            oh_sb = carve(768 * 2, F32); b_oh = Buf("oh_sb")
            fv_sb = carve(384 * 2, F32); b_fvsb = Buf("fv_sb")
            hk = [carve(512, F32) for i in range(2)]
            b_hk = [Buf(f"hk{i}") for i in range(2)]
            P.add("pool", lambda e: e.memset(rel_aug[32:33, :], 1.0), [], [b_rel])
            dma("sp", rel_aug[0:32, :], rel_d.ap(), [], [b_rel], b_rel)
            dma("sp", oh_sb[0:33, :], oh_d.ap(), [], [b_oh], b_oh)
            for kind in range(2):
                mm(bank(4)[0:16, 0:384], rel_aug[0:33, kind * 16:(kind + 1) * 16], oh_sb[0:33, kind * 384:(kind + 1) * 384], True, True, [b_rel, b_oh], [bA[4]])
                P.add("dve", lambda e: e.tensor_copy(out=fv_sb[0:16, :], in_=bank(4)[0:16, 0:384]), [bA[4]], [b_fvsb])
                dma("sp", fv_d.ap()[kind * 16:(kind + 1) * 16, :], fv_sb[0:16, :], [b_fvsb], [dFV], dFV)
            for hh in range(32):
                k = rot("hk", 2)
                dma("sp", hk[k][:], bass.AP(fv_d, hh * 384, [[1, 128], [1, 256]]), [dFV], [b_hk[k]], b_hk[k])
                pb = rot("cl_ps", 2)
                mm(bank(4 + pb)[:, 0:256], J_f[:], hk[k][:], True, True, [b_J, b_hk[k]], [bA[4 + pb]])
                if hh < 16:
                    copy_alt(BA[:, hh, :], bank(4 + pb)[:, 0:256], [bA[4 + pb]], [b_BA])
                else:
                    copy_alt(BB[:, hh - 16, :], bank(4 + pb)[:, 0:256], [bA[4 + pb]], [b_BB])

            ck("bias")
            P.barrier()
            creset()
            xrow = carve(4 * 2048 * 2, F32).rearrange("p (a b) -> p a b", a=4)
            b_xrow = [Buf(f"xrow{i}") for i in range(4)]
            for s in range(NSEQ):
                for j in range(NTILE):
                    for tsub in range(NSUB):
                        r0 = j * TA + tsub * 128
                        dma("sp", xrow[:, tsub, :], x_a[s, r0:r0 + 128, :], [], [b_xrow[tsub]], b_xrow[tsub])
                    for kc in range(NKC):
                        pb = rot("xt_ps", 4)
                        for tsub in range(NSUB):
                            tr(bank(pb)[:, tsub * 128:(tsub + 1) * 128], xrow[:, tsub, kc * 128:(kc + 1) * 128], ident_f[:], [b_xrow[tsub], b_idf], [bA[pb]])
                        k = rot("xb", 4)
                        copy_alt(xb[k][:], bank(pb), [bA[pb]], [b_xb[k]])
                        dma("sp", xT_a[s, kc, :, j * TA:(j + 1) * TA], xb[k][:], [b_xb[k]], [dX[s][j][kc % 2]], dX[s][j][kc % 2])

            ck("xT")
            def norm_p1(s, j, kc):
                t0 = j * TA
                k = rot("xb", 4)
                dma("sp", xb[k][:], xT_a[s, kc, :, t0:t0 + TA], [dX[s][j][kc % 2]], [b_xb[k]], b_xb[k])
                q = rot("sqb", 2)
                act(sqb[q][:], xb[k][:], AF.Square, [b_xb[k]], [b_sqb[q]])
                mm(bank(5), ones_f[:], sqb[q][:], kc == 0, kc == NKC - 1, [b_ones, b_sqb[q]], [bA[5]])

            def norm_mid():
                act(rstd[:], bank(5), AF.Sqrt, [bA[5]], [b_rstd], bias=EPS, scale=1.0 / D)
                P.add("dve", lambda e: e.reciprocal(out=rstd[:], in_=rstd[:]), [b_rstd], [b_rstd])

            def norm_p2(s, j, kc, gs_ap, dst_fn):
                t0 = j * TA
                k = rot("xb", 4)
                dma("sp", xb[k][:], xT_a[s, kc, :, t0:t0 + TA], [dX[s][j][kc % 2]], [b_xb[k]], b_xb[k])
                q = rot("tmpn", 2)
                stt(tmpn[q][:], xb[k][:], gs_ap(kc), rstd[:], ALU.mult, ALU.mult, [b_xb[k], b_rstd, b_gs1, b_gs2, b_nfin], [b_tmpn[q]])
                dst_fn(kc, tmpn[q], b_tmpn[q])

            def norm_tile(s, j, gs_ap, sh_ap, dst_fn):
                for kc in range(NKC):
                    norm_p1(s, j, kc)
                norm_mid()
                for kc in range(NKC):
                    norm_p2(s, j, kc, gs_ap, dst_fn)

            def to_hT(sh_ap):
                def f(kc, tm, b_tm):
                    act(hT[:, kc, :], tm[:], AF.Identity, [b_tm, b_modc], [b_hT], bias=sh_ap(kc), scale=1.0)
                return f

            for l in range(NL):
                for kind in range(6):
                    for grp in range(8):
                        c0 = kind * 2048 + grp * 256
                        w = rot("WB", 2)
                        Wt = WB[w][:, 0:8192].bitcast(F32).rearrange("p (k c) -> p k c", k=16)
                        dma("sp", Wt, wmod_a[l, :, c0:c0 + 256].rearrange("(k p) c -> p k c", p=128), [], [b_WB[w]], b_WB[w])
                        for kc in range(NKC):
                            mm(bank(4)[0:NSEQ, 0:256], cT2[:, kc, :], Wt[:, kc, :], kc == 0, kc == NKC - 1, [b_cT2, b_WB[w]], [bA[4]])
                        act(mrow[0:NSEQ, :], bank(4)[0:NSEQ, 0:256], AF.Copy, [bA[4]], [b_mrow])
                        for jj in range(2):
                            tr(bank(5)[:, jj * NSEQ:(jj + 1) * NSEQ], mrow[0:NSEQ, jj * 128:(jj + 1) * 128], ident_f[0:NSEQ, 0:NSEQ], [b_mrow, b_idf], [bA[5]])
                        for jj in range(2):
                            ch = grp * 2 + jj
                            col = l * 96 + kind * 16 + ch
                            ts(modc[:, kind, ch, :], bank(5)[:, jj * NSEQ:(jj + 1) * NSEQ], bmodc[:, col:col + 1], ALU.add, [bA[5], b_bmodc], [b_modc])
                for s in range(NSEQ):
                    stt(gs1[:, s, :], modc[:, 1, :, s], 1.0, nattn[:, l * 16:(l + 1) * 16], ALU.add, ALU.mult, [b_modc, b_nattn], [b_gs1])
                    stt(gs2[:, s, :], modc[:, 4, :, s], 1.0, nffn[:, l * 16:(l + 1) * 16], ALU.add, ALU.mult, [b_modc, b_nffn], [b_gs2])
                ck("mod")
                P.barrier()
                creset()
                wukraw = carve(2 * 16 * 64, BF16).rearrange("p (r h d) -> p r h d", r=2, h=16)
                b_wukraw = PB("wukraw")
                for rc in range(2):
                    dma("sp", wukraw[:, rc, :, :], wukb_a[l, :, rc * 128:(rc + 1) * 128, :].rearrange("h p d -> p h d"), [dW["uk"][l]], [b_wukraw], b_wukraw)
                    dma("sp", wuv[:, rc, :, :], wuvb_a[l, :, rc * 128:(rc + 1) * 128, :].rearrange("h p d -> p h d"), [dW["uv"][l]], [b_wuv], b_wuv)
                for pp in range(8):
                    for rc in range(2):
                        idx = pp * 2 + rc
                        Bx, bBx = (B6, bB6) if idx < 8 else (B7, bB7)
                        tr(Bx[:, (idx % 8) * 128:(idx % 8 + 1) * 128], wukraw[:, rc, 2 * pp:2 * pp + 2, :].rearrange("p h d -> p (h d)"), ident_b[:], [b_wukraw, b_idb], [bBx])
                act(wukT[:, 0:4, :].rearrange("p a b -> p (a b)"), B6[:], AF.Copy, [bB6], [b_wukT])
                P.add("dve", lambda e: e.tensor_copy(out=wukT[:, 4:8, :].rearrange("p a b -> p (a b)"), in_=B7[:]), [bB7], [b_wukT])

                ck("wuk")
                for s in range(NSEQ):
                    P.barrier()
                    creset()
                    kaT0 = carve(2048, BF16); kaT1 = carve(2048, BF16); kiT = carve(2048, BF16)
                    b_kaT0, b_kaT1, b_kiT = Buf("kaT0"), Buf("kaT1"), Buf("kiT")
                    va = carve(16 * 128, BF16).rearrange("p (b d) -> p b d", b=16); b_va = Buf("va")
                    ckv = carve(16 * 256, BF16).rearrange("p (b d) -> p b d", b=16); b_ckv = Buf("ckv")
                    ckvT = carve(2 * 2048, BF16).rearrange("p (r t) -> p r t", r=2); b_ckvT = Buf("ckvT")
                    qaT = carve(8 * TA, BF16).rearrange("p (c t) -> p c t", c=8); b_qaT = Buf("qaT")
                    qbT = carve(8 * TA, BF16).rearrange("p (c t) -> p c t", c=8); b_qbT = Buf("qbT")
                    qiT = carve(8 * TA, BF16).rearrange("p (c t) -> p c t", c=8); b_qiT = Buf("qiT")
                    acc = carve(2048 * 2, F32); b_acc = Buf("acc")
                    rtmp = [carve(512 * 2, F32) for _ in range(2)]; b_rtmp = [Buf("rtmp0"), Buf("rtmp1")]
                    mneg = carve(2048, BF16); b_mneg = Buf("mneg")
                    Psb = [carve(2048, BF16) for _ in range(2)]; b_Psb = [Buf("P0"), Buf("P1")]
                    PTs = [carve(16 * 128, BF16).rearrange("p (b t) -> p b t", b=16) for _ in range(2)]; b_PTs = [Buf("PT0"), Buf("PT1")]
                    qlat = [carve(4 * 128, BF16).rearrange("p (h r t) -> p h r t", h=2, r=2) for _ in range(2)]; b_qlat = [Buf("ql0"), Buf("ql1")]
                    olat = [carve(2 * 128, BF16).rearrange("p (r t) -> p r t", r=2) for _ in range(2)]; b_olat = [Buf("ol0"), Buf("ol1")]
                    mixtok = carve(2048, BF16); b_mixtok = Buf("mixtok")
                    wi_sb = wi_sb_t[:]; b_wi = Buf("wi")
                    junkf = junkf_t[:]; b_junkf = Buf("junkf")
                    jk8 = carve(1024, BF16).bitcast(mybir.dt.uint8); b_jk8 = Buf("jk8")
                    b_mx = Buf("mx"); b_rs = [Buf("rs0"), Buf("rs1")]; b_rr = [Buf("rr0"), Buf("rr1")]
                    prep = [None]
                    b_st = Buf("stA")
                    b_bis = Buf("bis")

                    for j in range(NTILE):
                        t0 = j * TA
                        norm_tile(s, j, lambda kc: gs1[:, s, kc:kc + 1], None, to_hT(lambda kc: modc[:, 0, kc, s:s + 1]))
                        ck("norm")
                        def load_w(pieces, src_a, dbuf):
                            w = rot("WB", 2)
                            Wv = WB[w][:, 0:8192].rearrange("p (k c) -> p k c", k=16)
                            for (off, c0, n) in pieces:
                                dma("sp", Wv[:, :, off:off + n], src_a[:, c0:c0 + n].rearrange("(k p) c -> p k c", p=128), [dbuf], [b_WB[w]], b_WB[w])
                            return Wv, b_WB[w]

                        def feat_chunk(Wv, bW, cc, dest, b_dest, scale):
                            pb = rot("pj_ps", 4)
                            for kc in range(NKC):
                                mm(bank(pb), Wv[:, kc, cc * 128:(cc + 1) * 128], hT[:, kc, :], kc == 0, kc == NKC - 1, [bW, b_hT], [bA[pb]])
                            copy_alt(dest, bank(pb), [bA[pb]], [b_dest], scale=scale)

                        wl = winb_a[l]
                        for half in range(2):
                            Wv, bW = load_w([(0, half * 512, 512)], wl, dW["in"][l])
                            for cc in range(4):
                                feat_chunk(Wv, bW, cc, qaT[:, half * 4 + cc, :], b_qaT, 0.125)
                        ck("pqa")
                        Wv, bW = load_w([(0, 1024, 64), (64, 1024, 64), (128, 1088, 64), (192, 1088, 64), (256, 3584, 64), (320, 3584, 64)], wl, dW["in"][l])
                        feat_chunk(Wv, bW, 0, kaT0[:, t0:t0 + TA], b_kaT0, None)
                        feat_chunk(Wv, bW, 1, kaT1[:, t0:t0 + TA], b_kaT1, None)
                        feat_chunk(Wv, bW, 2, kiT[:, t0:t0 + TA], b_kiT, None)
                        ck("pkq")
                        for half in range(2):
                            Wv, bW = load_w([(0, 1280 + half * 512, 512)], wl, dW["in"][l])
                            for cc in range(4):
                                feat_chunk(Wv, bW, cc, qbT[:, half * 4 + cc, :], b_qbT, 0.125)
                        for half in range(2):
                            Wv, bW = load_w([(0, 2560 + half * 512, 512)], wl, dW["in"][l])
                            for cc in range(4):
                                feat_chunk(Wv, bW, cc, qiT[:, half * 4 + cc, :], b_qiT, None)
                        ck("pq")
                        Wv, bW = load_w([(0, 1152, 128), (128, 2304, 256), (384, 3600, 64)], wl, dW["in"][l])
                        ck("tm_ld")
                        for tsub in range(NSUB):
                            blk = j * NSUB + tsub
                            pb = rot("pj_ps", 4)
                            for kc in range(NKC):
                                mm(bank(pb)[:, 0:448], hT[:, kc, tsub * 128:(tsub + 1) * 128], Wv[:, kc, 0:448], kc == 0, kc == NKC - 1, [bW, b_hT], [bA[pb]])
                            ck("tm_mm")
                            P.add("dve", lambda e, blk=blk, pb=pb: e.tensor_copy(out=va[:, blk, :], in_=bank(pb)[:, 0:128]), [bA[pb]], [b_va])
                            ck("tm_va")
                            act(wi_sb[:, tsub, :], bank(pb)[:, 432:448], AF.Copy, [bA[pb]], [b_wi])
                            ck("tm_cp")
                            act(junkf[:], bank(pb)[:, 128:384], AF.Square, [bA[pb]], [b_junkf, b_st], accum=st[:, 0:1])
                            ck("tm_sq")
                            act(st[:, 1:2], st[:, 0:1], AF.Sqrt, [b_st], [b_st], bias=EPS, scale=1.0 / 256)
                            P.add("dve", lambda e: e.reciprocal(out=st[:, 2:3], in_=st[:, 1:2]), [b_st], [b_st])
                            ts(ckv[:, blk, :], bank(pb)[:, 128:384], st[:, 2:3], ALU.mult, [bA[pb], b_st], [b_ckv])
                            ck("tm_ckv")
                            for rc in range(2):
                                tr(B6[:, rc * 128:(rc + 1) * 128], ckv[:, blk, rc * 128:(rc + 1) * 128], ident_b[:], [b_ckv, b_idb], [bB6])
                            copy_alt(ckvT[:, :, blk * 128:(blk + 1) * 128], B6[:, 0:256].rearrange("p (r t) -> p r t", r=2), [bB6], [b_ckvT])

                        ck("proj")
                        for tsub in range(NSUB):
                            i = j * NSUB + tsub
                            tc = slice(tsub * 128, (tsub + 1) * 128)
                            nk = 256 if i > 0 else 128
                            ks = (i - 1) * 128 if i > 0 else 0
                            bc0 = 0 if i > 0 else 128
                            nkb = nk // 128
                            for half in range(2):
                                pr = rot("Psb", 2)
                                hb = half * 64
                                for hh in range(8):
                                    h = 2 * hh + half
                                    kaT, b_kaT = (kaT0, b_kaT0) if h < 8 else (kaT1, b_kaT1)
                                    o_ = A03[:, hh * 256:hh * 256 + nk]
                                    mm(o_, qaT[hb:hb + 64, hh, tc], kaT[hb:hb + 64, ks:ks + nk], True, False, [b_qaT, b_kaT], [bA[hh // 2]])
                                    mm(o_, ident_b[:], BA[:, h, bc0:bc0 + nk], False, True, [b_idb, b_BA], [bA[hh // 2]])
                                S3 = A03[:].rearrange("p (h k) -> p h k", h=8)[:, :, 0:nk]
                                P.add("dve", lambda e, S3=S3: e.tensor_reduce(out=st[:, 8:16], in_=S3, axis=AX.X, op=ALU.max), [bA[0], bA[1], bA[2], bA[3], b_st], [b_st])
                                _s = sink_bc[:, l * 16 + half:l * 16 + half + 1]
                                sk = bass.AP(_s.tensor, _s.offset, [list(_s.ap[0]), [2, 8]])
                                tt(st[:, 16:24], st[:, 8:16], sk, ALU.max, [b_st, b_sink], [b_st])
                                ts(st[:, 24:32], st[:, 16:24], -1.0, ALU.mult, [b_st], [b_st])
                                for hh in range(8):
                                    act(Psb[pr][:, hh * 256:hh * 256 + nk], A03[:, hh * 256:hh * 256 + nk], AF.Exp, [bA[hh // 2], b_st], [b_Psb[pr], b_st],
                                        bias=st[:, 24 + hh:25 + hh], scale=1.0, accum=st[:, 32 + hh:33 + hh])
                                tt(st[:, 40:48], sk, st[:, 24:32], ALU.add, [b_st, b_sink], [b_st])
                                act(st[:, 40:48], st[:, 40:48], AF.Exp, [b_st], [b_st])
                                tt(st[:, 40:48], st[:, 40:48], st[:, 32:40], ALU.add, [b_st], [b_st])
                                P.add("dve", lambda e, half=half: e.reciprocal(out=st[:, 48 + half * 8:56 + half * 8], in_=st[:, 40:48]), [b_st], [b_st])
                                pt = rot("PTs", 2)
                                for hh in range(8):
                                    for kb in range(nkb):
                                        sl = hh * 2 + kb
                                        Bx, bBx = (B6, bB6) if sl < 8 else (B7, bB7)
                                        tr(Bx[:, (sl % 8) * 128:(sl % 8 + 1) * 128], Psb[pr][:, hh * 256 + kb * 128:hh * 256 + (kb + 1) * 128], ident_b[:], [b_Psb[pr], b_idb], [bBx])
                                act(PTs[pt][:, 0:8, :].rearrange("p a b -> p (a b)"), B6[:], AF.Copy, [bB6], [b_PTs[pt]])
                                P.add("dve", lambda e, pt=pt: e.tensor_copy(out=PTs[pt][:, 8:16, :].rearrange("p a b -> p (a b)"), in_=B7[:]), [bB7], [b_PTs[pt]])
                                for hh in range(8):
                                    h = 2 * hh + half
                                    g = h // 8
                                    for kb in range(nkb):
                                        kblk = (i - 1 + kb) if i > 0 else 0
                                        mm(A45[:, h * 64:(h + 1) * 64], PTs[pt][:, hh * 2 + kb, :], va[:, kblk, g * 64:(g + 1) * 64], kb == 0, kb == nkb - 1, [b_PTs[pt], b_va], [bA[4 + h // 8]])
                            _a = st[:, 48:64]
                            rden = bass.AP(_a.tensor, _a.offset, [list(_a.ap[0]), [1, 8], [8, 2], [0, 64]])
                            tt(mixtok[:, 0:1024].rearrange("p (h e d) -> p h e d", h=8, e=2), A45[:].rearrange("p (h e d) -> p h e d", h=8, e=2), rden, ALU.mult, [bA[4], bA[5], b_st], [b_mixtok])

                            ck("swa%d" % i)
                            S = 128 * (i + 1)
                            nch = (S + 511) // 512

                            def indexer(ii, tsb):
                                S_ = 128 * (ii + 1)
                                tcc = slice(tsb * 128, (tsb + 1) * 128)
                                for h in range(16):
                                    pch, hb = h // 2, (h % 2) * 64
                                    for c in range((S_ + 511) // 512):
                                        w_ = min(512, S_ - c * 512)
                                        pb = rot("ix_ps", 4)
                                        mm(bank(pb)[:, 0:w_], qiT[hb:hb + 64, pch, tcc], kiT[hb:hb + 64, c * 512:c * 512 + w_], True, True, [b_qiT, b_kiT], [bA[pb]])
                                        rq = rot("rtmp", 2)
                                        act(rtmp[rq][:, 0:w_], bank(pb)[:, 0:w_], AF.Relu, [bA[pb]], [b_rtmp[rq]])
                                        a_ = acc[:, c * 512:c * 512 + w_]
                                        if h == 0:
                                            ts(a_, rtmp[rq][:, 0:w_], wi_sb[:, tsb, 0:1], ALU.mult, [b_rtmp[rq], b_wi], [b_acc])
                                        else:
                                            stt(a_, rtmp[rq][:, 0:w_], wi_sb[:, tsb, h:h + 1], a_, ALU.mult, ALU.add, [b_rtmp[rq], b_wi, b_acc], [b_acc])
                                lo, hi, wd, mid, cnt, tmp = (st[:, 64 + k:65 + k] for k in range(6))
                                P.add("dve", lambda e: e.tensor_reduce(out=lo, in_=acc[:, 0:S_], axis=AX.X, op=ALU.min), [b_acc], [b_bis])
                                P.add("dve", lambda e: e.tensor_reduce(out=hi, in_=acc[:, 0:S_], axis=AX.X, op=ALU.max), [b_acc], [b_bis])
                                tt(acc[:, ii * 128:(ii + 1) * 128], acc[:, ii * 128:(ii + 1) * 128], caus[:], ALU.add, [b_acc, b_caus], [b_acc])
                                tt(wd, hi, lo, ALU.subtract, [b_bis], [b_bis])
                                ts(bs2[:], pw[:], wd, ALU.mult, [b_bis, b_pw], [b_bis])
                                ts(bsn[:], pw[:], wd, ALU.mult, [b_bis, b_pw], [b_bis], s2=-0.5, op1=ALU.mult)
                                tt(mid, lo, bs2[:, 0:1], ALU.add, [b_bis], [b_bis])

                            def bis_iter(ii, k):
                                S_ = 128 * (ii + 1)
                                mid, cnt, tmp = st[:, 67:68], st[:, 68:69], st[:, 69:70]
                                ts(jk8[:, 0:S_], acc[:, 0:S_], mid, ALU.is_ge, [b_acc, b_bis], [b_jk8, b_bis], s2=None, op1=ALU.add, accum=cnt)
                                if k < NIT - 1:
                                    ts(tmp, cnt, 255.5, ALU.is_ge, [b_bis], [b_bis], s2=bs2[:, k:k + 1], op1=ALU.mult)
                                    stt(mid, tmp, bsn[:, k:k + 1], mid, ALU.add, ALU.add, [b_bis], [b_bis])
                                else:
                                    ts(tmp, cnt, 255.5, ALU.is_lt, [b_bis], [b_bis], s2=bs2[:, k:k + 1], op1=ALU.mult)
                                    tt(mid, mid, tmp, ALU.subtract, [b_bis], [b_bis])

                            def bis_final(ii):
                                S_ = 128 * (ii + 1)
                                ts(mneg[:, 0:S_], acc[:, 0:S_], st[:, 67:68], ALU.is_ge, [b_acc, b_bis], [b_mneg], s2=-1.0, op1=ALU.add)

                            if i >= 2 and prep[0] != i:
                                indexer(i, tsub)
                                for k in range(NIT):
                                    bis_iter(i, k)
                                bis_final(i)
                            nxt = (i + 1) if (tsub < NSUB - 1 and i + 1 >= 2) else None
                            if nxt is not None:
                                indexer(nxt, tsub + 1)
                            bis_k = [0]

                            def bis_some(n):
                                if nxt is None:
                                    return
                                for _ in range(n):
                                    if bis_k[0] < NIT:
                                        bis_iter(nxt, bis_k[0])
                                        bis_k[0] += 1

                            def pre_pe(h, ql, hs):
                                for c in range(nch):
                                    c0 = c * 512
                                    w_ = min(512, S - c0)
                                    ops = [(qlat[ql][:, hs, 0, :], ckvT[:, 0, c0:c0 + w_], 0, w_, [b_qlat[ql], b_ckvT]),
                                           (qlat[ql][:, hs, 1, :], ckvT[:, 1, c0:c0 + w_], 0, w_, [b_qlat[ql], b_ckvT])]
                                    if i >= 2:
                                        ops.append((i32k[:], mneg[:, c0:c0 + w_], 0, w_, [b_i32k, b_mneg]))
                                    for kb in ((i - 1, i) if i > 0 else (i,)):
                                        if c0 <= kb * 128 < c0 + w_:
                                            bb = (kb - (i - 1)) if i > 0 else 1
                                            ops.append((ident_b[:], BB[:, h, bb * 128:(bb + 1) * 128], kb * 128 - c0, 128, [b_idb, b_BB]))
                                    for n_, (lh, rh, o0, ow, rd) in enumerate(ops):
                                        mm(bank(c)[:, o0:o0 + ow], lh, rh, n_ == 0, n_ == len(ops) - 1, rd, [bA[c]])

                            def pre_dve(h):
                                banks_r = [bA[c] for c in range(nch)]
                                P.add("dve", lambda e, S=S: e.tensor_reduce(out=st[:, 72:73], in_=A03[:, 0:S], axis=AX.X, op=ALU.max), banks_r, [b_mx])
                                ts(st[:, 73:74], st[:, 72:73], -1.0, ALU.mult, [b_mx], [b_mx])

                            def pre_act(h):
                                banks_r = [bA[c] for c in range(nch)]
                                pr = h % 2
                                act(Psb[pr][:, 0:S], A03[:, 0:S], AF.Exp, banks_r + [b_mx], [b_Psb[pr], b_rs[h % 2]], bias=st[:, 73:74], scale=1.0, accum=st[:, 80 + h:81 + h])

                            def post_T_pe(h):
                                pr = h % 2
                                for b_ in range(i + 1):
                                    Bx, bBx = (B6, bB6) if b_ < 8 else (B7, bB7)
                                    tr(Bx[:, (b_ % 8) * 128:(b_ % 8 + 1) * 128], Psb[pr][:, b_ * 128:(b_ + 1) * 128], ident_b[:], [b_Psb[pr], b_idb], [bBx])

                            def post_T_act(h):
                                pt = h % 2
                                n6 = min(8, i + 1)
                                act(PTs[pt][:, 0:n6, :].rearrange("p a b -> p (a b)"), B6[:, 0:n6 * 128], AF.Copy, [bB6], [b_PTs[pt]])

                            def post_T_dve(h):
                                pt = h % 2
                                if i + 1 > 8:
                                    n7 = i + 1 - 8
                                    P.add("dve", lambda e, pt=pt, n7=n7: e.tensor_copy(out=PTs[pt][:, 8:8 + n7, :].rearrange("p a b -> p (a b)"), in_=B7[:, 0:n7 * 128]), [bB7], [b_PTs[pt]])

                            def post_rest(h):
                                pt = h % 2
                                for rc in range(2):
                                    for b_ in range(i + 1):
                                        mm(bank(5)[:, rc * 128:(rc + 1) * 128], ckv[:, b_, rc * 128:(rc + 1) * 128], PTs[pt][:, b_, :], b_ == 0, b_ == i, [b_ckv, b_PTs[pt]], [bA[5]])
                                ol = rot("olat", 2)
                                for rc in range(2):
                                    ts(olat[ol][:, rc, :], bank(5)[:, rc * 128:(rc + 1) * 128], kvn[:, l * 2 + rc:l * 2 + rc + 1], ALU.mult, [bA[5], b_kvn], [b_olat[ol]])
                                for rc in range(2):
                                    mm(bank(5)[:, 256:320], olat[ol][:, rc, :], wuv[:, rc, h, :], rc == 0, rc == 1, [b_olat[ol], b_wuv], [bA[5]])
                                P.add("dve", lambda e, h=h: e.reciprocal(out=st[:, 96 + h:97 + h], in_=st[:, 80 + h:81 + h]), [b_rs[h % 2]], [b_rr[h % 2]])
                                ts(mixtok[:, 1024 + h * 64:1024 + (h + 1) * 64], bank(5)[:, 256:320], st[:, 96 + h:97 + h], ALU.mult, [bA[5], b_rr[h % 2]], [b_mixtok])

                            prev = None
                            for pp in range(8):
                                ql = rot("qlat", 2)
                                for hs in range(2):
                                    hb = hs * 64
                                    for rc in range(2):
                                        mm(bank(4 + hs)[:, rc * 128:(rc + 1) * 128], wukT[hb:hb + 64, pp, rc * 128:(rc + 1) * 128], qbT[hb:hb + 64, pp, tc], True, True, [b_wukT, b_qbT], [bA[4 + hs]])
                                for hs in range(2):
                                    for rc in range(2):
                                        ts(qlat[ql][:, hs, rc, :], bank(4 + hs)[:, rc * 128:(rc + 1) * 128], kvn[:, l * 2 + rc:l * 2 + rc + 1], ALU.mult, [bA[4 + hs], b_kvn], [b_qlat[ql]])
                                for hs in range(2):
                                    h = pp * 2 + hs
                                    pre_pe(h, ql, hs)
                                    if prev is not None:
                                        post_T_pe(prev)
                                        post_T_act(prev)
                                        post_T_dve(prev)
                                    pre_dve(h)
                                    pre_act(h)
                                    if prev is not None:
                                        post_rest(prev)
                                    bis_some(2)
                                    prev = h
                            post_T_pe(prev)
                            post_T_act(prev)
                            post_T_dve(prev)
                            post_rest(prev)
                            if nxt is not None:
                                bis_some(NIT)
                                bis_final(nxt)
                                prep[0] = nxt
                            ck("dsa%d" % i)
                            for c in range(16):
                                Bx, bBx = (B6, bB6) if c < 8 else (B7, bB7)
                                tr(Bx[:, (c % 8) * 128:(c % 8 + 1) * 128], mixtok[:, c * 128:(c + 1) * 128], ident_b[:], [b_mixtok, b_idb], [bBx])
                            act(hT[:, 0:8, tc], B6[:].rearrange("p (c t) -> p c t", c=8), AF.Copy, [bB6], [b_hT])
                            P.add("dve", lambda e, tc=tc: e.tensor_copy(out=hT[:, 8:16, tc], in_=B7[:].rearrange("p (c t) -> p c t", c=8)), [bB7], [b_hT])

                        ck("attn%d" % j)
                        for g in range(4):
                            w = rot("WB", 2)
                            Wv = WB[w][:, 0:8192].rearrange("p (k c) -> p k c", k=16)
                            dma("sp", Wv, woutb_a[l][:, g * 512:(g + 1) * 512].rearrange("(k p) c -> p k c", p=128), [dW["out"][l]], [b_WB[w]], b_WB[w])
                            for mm_ in range(4):
                                m = g * 4 + mm_
                                pb = rot("pj_ps", 4)
                                for kc in range(NKC):
                                    mm(bank(pb), Wv[:, kc, mm_ * 128:(mm_ + 1) * 128], hT[:, kc, :], kc == 0, kc == NKC - 1, [b_WB[w], b_hT], [bA[pb]])
                                k = rot("xb", 4)
                                dma("sp", xb[k][:], xT_a[s, m, :, t0:t0 + TA], [dX[s][j][m % 2]], [b_xb[k]], b_xb[k])
                                stt(xb[k][:], bank(pb), modc[:, 2, m, s:s + 1], xb[k][:], ALU.mult, ALU.add, [bA[pb], b_modc, b_xb[k]], [b_xb[k]])
                                dma("sp", xT_a[s, m, :, t0:t0 + TA], xb[k][:], [b_xb[k]], [dX[s][j][m % 2]], dX[s][j][m % 2])

                    ck("att")
                    P.barrier()
                    creset()
                    actT = carve(NFC * TA, BF16).rearrange("p (c t) -> p c t", c=NFC); b_actT = Buf("actT")
                    a_t = [carve(TA * 2, F32) for _ in range(2)]; b_at = [Buf("at0"), Buf("at1")]
                    s_t = [carve(TA * 2, F32) for _ in range(2)]; b_stt = [Buf("st0"), Buf("st1")]
                    WF = [WB[0][:, 0:8192], WB[1][:, 0:8192], carve(8192, BF16), carve(8192, BF16)]
                    b_WF = [b_WB[0], b_WB[1], PB("WF2"), PB("WF3")]
                    gs2f = lambda kc: gs2[:, s, kc:kc + 1]
                    hT2 = to_hT(lambda kc: modc[:, 3, kc, s:s + 1])
                    for j in range(NTILE):
                        t0 = j * TA
                        if j == 0 or not FFN_OVERLAP:
                            norm_tile(s, j, gs2f, None, hT2)
                        for q in range(11):
                            wu = rot("WF", 4)
                            Wu = WF[wu].rearrange("p (k c) -> p k c", k=16)
                            dma("sp", Wu, wupb_a[l][:, q * 512:(q + 1) * 512].rearrange("(k p) c -> p k c", p=128), [dW["up"][l]], [b_WF[wu]], b_WF[wu])
                            wg = rot("WF", 4)
                            Wg = WF[wg].rearrange("p (k c) -> p k c", k=16)
                            dma("sp", Wg, wupb_a[l][:, FF + q * 512:FF + (q + 1) * 512].rearrange("(k p) c -> p k c", p=128), [dW["up"][l]], [b_WF[wg]], b_WF[wg])
                            for cc in range(4):
                                c = q * 4 + cc
                                pu = rot("ff_ps", 3) * 2
                                pg = pu + 1
                                for kc in range(NKC):
                                    mm(bank(pu), Wu[:, kc, cc * 128:(cc + 1) * 128], hT[:, kc, :], kc == 0, kc == NKC - 1, [b_WF[wu], b_hT], [bA[pu]])
                                for kc in range(NKC):
                                    mm(bank(pg), Wg[:, kc, cc * 128:(cc + 1) * 128], hT[:, kc, :], kc == 0, kc == NKC - 1, [b_WF[wg], b_hT], [bA[pg]])
                                ai = rot("a_t", 2)
                                a_ = a_t[ai]
                                w0 = convw[:, (l * 3 + 0) * NFC + c:(l * 3 + 0) * NFC + c + 1]
                                w1 = convw[:, (l * 3 + 1) * NFC + c:(l * 3 + 1) * NFC + c + 1]
                                w2 = convw[:, (l * 3 + 2) * NFC + c:(l * 3 + 2) * NFC + c + 1]
                                cb = convb[:, l * NFC + c:l * NFC + c + 1]
                                up = bank(pu)
                                act(a_[:, 0:TA], up, AF.Identity, [bA[pu], b_convw, b_convb], [b_at[ai]], bias=cb, scale=w2)
                                stt(a_[:, 1:TA], up[:, 0:TA - 1], w1, a_[:, 1:TA], ALU.mult, ALU.add, [bA[pu], b_at[ai], b_convw], [b_at[ai]])
                                stt(a_[:, 2:TA], up[:, 0:TA - 2], w0, a_[:, 2:TA], ALU.mult, ALU.add, [bA[pu], b_at[ai], b_convw], [b_at[ai]])
                                if j > 0:
                                    stt(a_[:, 0:1], uh[:, c, 1:2], w1, a_[:, 0:1], ALU.mult, ALU.add, [b_uh, b_at[ai], b_convw], [b_at[ai]])
                                    stt(a_[:, 0:2], uh[:, c, 0:2], w0, a_[:, 0:2], ALU.mult, ALU.add, [b_uh, b_at[ai], b_convw], [b_at[ai]])
                                P.add("dve", lambda e, c=c, up=up: e.tensor_copy(out=uh[:, c, :], in_=up[:, TA - 2:TA]), [bA[pu]], [b_uh])
                                si = rot("s_t", 2)
                                act(s_t[si][:, 0:TA], a_[:, 0:TA], AF.Silu, [b_at[ai]], [b_stt[si]])
                                tt(actT[:, c, :], s_t[si][:, 0:TA], bank(pg), ALU.mult, [b_stt[si], bA[pg]], [b_actT])
                        for mp in range(8):
                            wds = []
                            for hf in range(2):
                                wd = rot("WF", 4)
                                Wd = WF[wd][:, 0:22 * 256].rearrange("p (c m) -> p c m", c=22)
                                dma("sp", Wd, wdownb_a[l][hf * 2816:(hf + 1) * 2816, mp * 256:(mp + 1) * 256].rearrange("(c p) m -> p c m", p=128), [dW["down"][l]], [b_WF[wd]], b_WF[wd])
                                wds.append((Wd, b_WF[wd]))
                            for mm_ in range(2):
                                m = mp * 2 + mm_
                                pb = rot("pj_ps", 4)
                                for c in range(NFC):
                                    Wd, bWd = wds[c // 22]
                                    mm(bank(pb), Wd[:, c % 22, mm_ * 128:(mm_ + 1) * 128], actT[:, c, :], c == 0, c == NFC - 1, [bWd, b_actT], [bA[pb]])
                                k = rot("xb", 4)
                                dma("sp", xb[k][:], xT_a[s, m, :, t0:t0 + TA], [dX[s][j][m % 2]], [b_xb[k]], b_xb[k])
                                stt(xb[k][:], bank(pb), modc[:, 5, m, s:s + 1], xb[k][:], ALU.mult, ALU.add, [bA[pb], b_modc, b_xb[k]], [b_xb[k]])
                                dma("sp", xT_a[s, m, :, t0:t0 + TA], xb[k][:], [b_xb[k]], [dX[s][j][m % 2]], dX[s][j][m % 2])
                                if FFN_OVERLAP and j + 1 < NTILE:
                                    if m < 8:
                                        norm_p1(s, j + 1, 2 * m)
                                        norm_p1(s, j + 1, 2 * m + 1)
                                        if m == 7:
                                            norm_mid()
                                    else:
                                        norm_p2(s, j + 1, 2 * (m - 8), gs2f, hT2)
                                        norm_p2(s, j + 1, 2 * (m - 8) + 1, gs2f, hT2)

            ck("layers")
            P.barrier()
            creset()
            orow = carve(4 * 2048 * 2, F32).rearrange("p (a b) -> p a b", a=4)
            b_orow = [Buf(f"orow{i}") for i in range(4)]
            for s in range(NSEQ):
                for j in range(NTILE):
                    def fin(kc, tm, b_tm):
                        pb = rot("xt_ps", 4)
                        for tsub in range(NSUB):
                            tr(bank(pb)[:, tsub * 128:(tsub + 1) * 128], tm[:, tsub * 128:(tsub + 1) * 128], ident_f[:], [b_tm, b_idf], [bA[pb]])
                        for tsub in range(NSUB):
                            copy_alt(orow[:, tsub, kc * 128:(kc + 1) * 128], bank(pb)[:, tsub * 128:(tsub + 1) * 128], [bA[pb]], [b_orow[tsub]])
                    norm_tile(s, j, lambda kc: nfin[:, kc:kc + 1], None, fin)
                    for tsub in range(NSUB):
                        r0 = j * TA + tsub * 128
                        dma("sp", out_a[s, r0:r0 + 128, :], orow[:, tsub, :], [b_orow[tsub]], [dOUT], dOUT)

        try:
            _body()
        except _Stop:
            pass
        P.add("sp", None, [dOUT, dDBG], [])
        P.emit(es)
    return nc, P


_CACHE = {}


def kernel(**inputs):
    n = 8
    x = np.ascontiguousarray(np.asarray(inputs["x"], dtype=np.float32))
    c = np.ascontiguousarray(np.asarray(inputs["c"], dtype=np.float32))
    if "nc" not in _CACHE:
        _CACHE["nc"] = build()[0]
    nc = _CACHE["nc"]
    oh = _onehot_const().reshape(33, 768)
    shared = {k: np.ascontiguousarray(np.asarray(inputs[k], dtype=np.float32)) for k in
              ("w_mod", "b_mod", "norm_attn", "w_in", "attn_sinks", "kv_norm", "w_uk", "w_uv", "rel_bias",
               "w_out", "norm_ffn", "w_up", "conv_w", "conv_b", "w_down", "norm_final")}
    in_maps = []
    for i in range(n):
        m = dict(shared)
        m["x"] = x[2 * i:2 * i + 2]
        m["c"] = c[2 * i:2 * i + 2]
        m["oh"] = oh
        in_maps.append(m)
    res = run_bass_kernel_spmd(nc, in_maps, core_ids=list(range(n)))
    return np.concatenate([r["out"] for r in res.results], axis=0)
```

```python
import math
from contextlib import ExitStack
import numpy as np
import concourse.bass as bass
import concourse.mybir as mybir
from concourse.bass_utils import run_bass_kernel_spmd

F32 = mybir.dt.float32
BF16 = mybir.dt.bfloat16
AF = mybir.ActivationFunctionType
ALU = mybir.AluOpType
AX = mybir.AxisListType

D = 2048
T = 2048
NKC = 16
TA = 512
NTILE = T // TA
NSUB = TA // 128
FF = 5632
NFC = 44
INW = 3664
EPS = 1e-6
NIT = 26
UN = 50432
FFN_OVERLAP = True


class Buf:
    __slots__ = ("name", "lw", "rdc", "rdd", "dsem", "dcnt", "excl")

    def __init__(self, name, excl=False):
        self.name = name
        self.excl = excl
        self.lw = None
        self.rdc = {}
        self.rdd = []
        self.dsem = None
        self.dcnt = 0


class Op:
    __slots__ = ("eng", "fn", "dma", "deps", "tok", "inc", "idx")


class Prog:
    ENGS = ("pe", "act", "dve", "pool", "sp")

    def __init__(self, nc):
        self.nc = nc
        self.ops = []
        self.last = {e: None for e in self.ENGS}
        self.dmas_since_bar = []
        self.dbufs = []

    def add(self, eng, fn, reads=(), writes=(), dma_dst=None):
        o = Op()
        o.eng = eng
        o.fn = fn
        o.dma = dma_dst is not None
        o.inc = False
        o.idx = len(self.ops)
        o.tok = None
        deps = set()
        for b in reads:
            if b.lw is not None:
                deps.add(b.lw)
            if b.excl:
                for e, r in b.rdc.items():
                    if e != eng:
                        deps.add(r)
        for b in writes:
            if b.lw is not None:
                deps.add(b.lw)
            for e, r in b.rdc.items():
                if e == eng and not o.dma:
                    continue
                deps.add(r)
            for r in b.rdd:
                deps.add(r)
        if eng == "pe" and not o.dma:
            deps = {d for d in deps if d.dma or d.eng != "pe"}
        o.deps = deps
        for d in deps:
            if not d.dma:
                d.inc = True
        if o.dma:
            if dma_dst.dsem is None:
                dma_dst.dsem = "pending"
                self.dbufs.append(dma_dst)
            dma_dst.dcnt += 16
            o.tok = (dma_dst, dma_dst.dcnt)
            self.dmas_since_bar.append(o)
        for b in reads:
            if o.dma:
                b.rdd.append(o)
            else:
                b.rdc[eng] = o
        for b in writes:
            b.lw = o
            b.rdc = {}
            b.rdd = []
        self.ops.append(o)
        self.last[eng] = o
        return o

    def barrier(self):
        lasts = [o for o in self.last.values() if o is not None]
        pend = list(self.dmas_since_bar)
        self.dmas_since_bar = []
        for e in self.ENGS:
            o = Op()
            o.eng = e
            o.fn = None
            o.dma = False
            o.inc = False
            o.idx = len(self.ops)
            o.tok = None
            o.deps = set(x for x in lasts if (x.dma or x.eng != e)) | set(pend)
            for d in o.deps:
                if not d.dma:
                    d.inc = True
            self.ops.append(o)
            self.last[e] = o

    def emit(self, es):
        nc = self.nc
        esem = {}
        for e in ("pe", "act", "dve", "pool"):
            esem[e] = es.enter_context(nc.semaphore("es_" + e))
        for b in self.dbufs:
            b.dsem = es.enter_context(nc.semaphore("ds%d_%s" % (self.dbufs.index(b), b.name)))
        cnt = {e: 0 for e in self.ENGS}
        for o in self.ops:
            if o.dma or o.fn is None:
                continue
            if o.inc:
                cnt[o.eng] += 1
                o.tok = (o.eng, cnt[o.eng])
        by_eng = {e: [o for o in self.ops if o.eng == e] for e in self.ENGS}
        self.stats = {e: len(v) for e, v in by_eng.items()}

        def run(ename, eng):
            seen = {}
            for o in by_eng[ename]:
                waits = {}
                for d in o.deps:
                    if d.tok is None:
                        continue
                    key, val = d.tok
                    kid = key if isinstance(key, str) else id(key)
                    if kid not in waits or waits[kid][1] < val:
                        waits[kid] = (key, val)
                for kid, (key, val) in waits.items():
                    if seen.get(kid, 0) >= val:
                        continue
                    seen[kid] = val
                    sem = esem[key] if isinstance(key, str) else key.dsem
                    eng.wait_ge(sem, val)
                if o.fn is None:
                    continue
                inst = o.fn(eng)
                if o.dma:
                    inst.then_inc(o.tok[0].dsem, 16)
                elif o.inc:
                    inst.then_inc(esem[ename], 1)

        with nc.Block() as block:
            @block.tensor
            def _(eng):
                run("pe", eng)

            @block.scalar
            def _(eng):
                run("act", eng)

            @block.vector
            def _(eng):
                run("dve", eng)

            @block.gpsimd
            def _(eng):
                run("pool", eng)

            @block.sync
            def _(eng):
                run("sp", eng)


def _rel_bucket_np(n):
    n = np.maximum(n, 0)
    max_exact = 16
    nf = np.maximum(n, 1).astype(np.float32)
    v = (np.log(nf / np.float32(max_exact)) / np.float32(math.log(128 / max_exact))
         * np.float32(32 - max_exact)).astype(np.float32)
    large = max_exact + v.astype(np.int32)
    large = np.minimum(large, 31)
    return np.where(n < max_exact, n, large)


def _onehot_const():
    oh = np.zeros((33, 2, 384), np.float32)
    for u in range(383):
        dist = 255 - u
        if 0 <= dist < 128:
            oh[int(_rel_bucket_np(np.array(dist))), 0, u] = 1.0
        else:
            oh[32, 0, u] = -30000.0
        if dist >= 0:
            oh[int(_rel_bucket_np(np.array(dist))), 1, u] += 1.0
            oh[31, 1, u] -= 1.0
        else:
            oh[32, 1, u] = -30000.0
    oh[32, :, 383] = -30000.0
    return oh


class _Stop(Exception):
    pass


def build(NL=4, NSEQ=2, debug=False, stop=None):
    nc = bass.Bass("TRN2", target_bir_lowering=False)
    P = Prog(nc)
    dbg_d = nc.dram_tensor("dbg", [16, 128, 2048], F32, kind="ExternalOutput") if debug else None
    dDBG = Buf("dDBG")

    def ck(name):
        if stop == name:
            raise _Stop()

    def dump(slot, ap, bufs, n):
        if debug:
            P.add("pool", lambda e: e.dma_start(out=dbg_d.ap()[slot, 0:ap.shape[0], 0:n], in_=ap), bufs, [dDBG], dma_dst=dDBG)

    def din(name, shape):
        return nc.dram_tensor(name, list(shape), F32, kind="ExternalInput")

    x_d = din("x", [NSEQ, T, D])
    c_d = din("c", [NSEQ, D])
    wmod_d = din("w_mod", [NL, D, 6 * D])
    bmod_d = din("b_mod", [NL, 6 * D])
    nattn_d = din("norm_attn", [NL, D])
    win_d = din("w_in", [NL, D, INW])
    sinks_d = din("attn_sinks", [NL, 16])
    kvn_d = din("kv_norm", [NL, 256])
    wuk_d = din("w_uk", [NL, 16, 256, 64])
    wuv_d = din("w_uv", [NL, 16, 256, 64])
    rel_d = din("rel_bias", [32, 32])
    wout_d = din("w_out", [NL, D, D])
    nffn_d = din("norm_ffn", [NL, D])
    wup_d = din("w_up", [NL, D, 2 * FF])
    convw_d = din("conv_w", [NL, 3, FF])
    convb_d = din("conv_b", [NL, FF])
    wdown_d = din("w_down", [NL, FF, D])
    nfin_d = din("norm_final", [D])
    oh_d = din("oh", [33, 768])
    out_d = nc.dram_tensor("out", [NSEQ, T, D], F32, kind="ExternalOutput")

    xT_d = nc.dram_tensor("xT", [NSEQ, NKC, 128, T], F32, kind="Internal")
    fv_d = nc.dram_tensor("fv", [32, 384], F32, kind="Internal")
    winb_d = nc.dram_tensor("winb", [NL, D, INW], BF16, kind="Internal")
    woutb_d = nc.dram_tensor("woutb", [NL, D, D], BF16, kind="Internal")
    wupb_d = nc.dram_tensor("wupb", [NL, D, 2 * FF], BF16, kind="Internal")
    wdownb_d = nc.dram_tensor("wdownb", [NL, FF, D], BF16, kind="Internal")
    wukb_d = nc.dram_tensor("wukb", [NL, 16, 256, 64], BF16, kind="Internal")
    wuvb_d = nc.dram_tensor("wuvb", [NL, 16, 256, 64], BF16, kind="Internal")

    x_a, c_a, wmod_a, win_a = x_d.ap(), c_d.ap(), wmod_d.ap(), win_d.ap()
    out_a, xT_a = out_d.ap(), xT_d.ap()
    winb_a, woutb_a, wupb_a, wdownb_a = winb_d.ap(), woutb_d.ap(), wupb_d.ap(), wdownb_d.ap()
    wukb_a, wuvb_a = wukb_d.ap(), wuvb_d.ap()

    dX = [[[Buf(f"dX{s}_{j}_{q}") for q in range(2)] for j in range(NTILE)] for s in range(NSEQ)]
    dW = {k: [Buf(f"dW{k}{l}") for l in range(4)] for k in ("in", "out", "up", "down", "uk", "uv")}
    dFV = Buf("dFV")
    dOUT = Buf("dOUT")

    with ExitStack() as es:
        def sb(name, shape, dt):
            return es.enter_context(nc.sbuf_tensor(name, list(shape), dt))

        def pst(name, shape, dt):
            return es.enter_context(nc.psum_tensor(name, list(shape), dt))

        A03 = pst("A03", [128, 2048], F32)
        A45 = pst("A45", [128, 1024], F32)
        B6 = pst("B6", [128, 1024], BF16)
        B7 = pst("B7", [128, 1024], BF16)
        bA = [Buf(f"bA{i}", excl=True) for i in range(6)]
        bB6, bB7 = Buf("bB6", excl=True), Buf("bB7", excl=True)

        def bank(i):
            if i < 4:
                return A03[:, i * 512:(i + 1) * 512]
            return A45[:, (i - 4) * 512:(i - 3) * 512]

        ident_f = sb("ident_f", [128, 128], F32); b_idf = Buf("idf")
        ident_b = sb("ident_b", [128, 128], BF16); b_idb = Buf("idb")
        i32k = sb("i32k", [128, 128], BF16); b_i32k = Buf("i32k")
        J_f = sb("J_f", [128, 128], F32); b_J = Buf("J")
        ones_f = sb("ones_f", [128, 128], F32); b_ones = Buf("ones")
        zer_f = sb("zer_f", [128, 128], F32); b_zer = Buf("zer")
        caus = sb("caus", [128, 128], F32); b_caus = Buf("caus")
        BA = sb("BA", [128, 16, 256], BF16); b_BA = Buf("BA")
        BB = sb("BB", [128, 16, 256], BF16); b_BB = Buf("BB")
        bmodc = sb("bmodc", [128, 384], F32); b_bmodc = Buf("bmodc")
        nattn = sb("nattn", [128, 64], F32); b_nattn = Buf("nattn")
        nffn = sb("nffn", [128, 64], F32); b_nffn = Buf("nffn")
        kvn = sb("kvn", [128, 8], F32); b_kvn = Buf("kvn")
        convw = sb("convw", [128, 528], F32); b_convw = Buf("convw")
        convb = sb("convb", [128, 176], F32); b_convb = Buf("convb")
        nfin = sb("nfin", [128, 16], F32); b_nfin = Buf("nfin")
        cTs = sb("cTs", [128, 32], F32); b_cTs = Buf("cTs")
        cT2 = sb("cT2", [128, 16, NSEQ], F32); b_cT2 = Buf("cT2")
        sink_bc = sb("sink_bc", [128, 64], F32); b_sink = Buf("sink")
        modc = sb("modc", [128, 6, 16, NSEQ], F32); b_modc = Buf("modc")
        gs1 = sb("gs1", [128, NSEQ, 16], F32); b_gs1 = Buf("gs1")
        gs2 = sb("gs2", [128, NSEQ, 16], F32); b_gs2 = Buf("gs2")
        wukT = sb("wukT", [128, 8, 256], BF16); b_wukT = Buf("wukT")
        wuv = sb("wuv", [128, 2, 16, 64], BF16); b_wuv = Buf("wuv")
        mrow = sb("mrow", [2, 256], F32); b_mrow = Buf("mrow")
        uh = sb("uh", [128, NFC, 2], F32); b_uh = Buf("uh")

        hT = sb("hT", [128, 16, TA], BF16); b_hT = Buf("hT")
        WB = [sb(f"WB{i}", [128, 8192], BF16) for i in range(2)]
        b_WB = [Buf(f"WB{i}") for i in range(2)]
        xb = [sb(f"xb{i}", [128, TA], F32) for i in range(4)]
        b_xb = [Buf(f"xb{i}") for i in range(4)]
        sqb = [sb(f"sqb{i}", [128, TA], F32) for i in range(2)]
        b_sqb = [Buf(f"sqb{i}") for i in range(2)]
        rstd = sb("rstd", [128, TA], F32); b_rstd = Buf("rstd")
        tmpn = [sb(f"tmpn{i}", [128, TA], F32) for i in range(2)]
        b_tmpn = [Buf(f"tmpn{i}") for i in range(2)]
        st = sb("st", [128, 128], F32)
        pw = sb("pw", [128, NIT + 1], F32); b_pw = Buf("pw")
        bs2 = sb("bs2", [128, NIT + 1], F32)
        bsn = sb("bsn", [128, NIT + 1], F32)
        wi_sb_t = sb("wi_sb", [128, NSUB, 16], F32)
        junkf_t = sb("junkf", [128, 256], F32)
        U = sb("U", [128, UN], BF16)

        ucur = [0]

        def carve(nbytes_elems, dt, shape=None):
            n2 = nbytes_elems
            a = U[:, ucur[0]:ucur[0] + n2]
            ucur[0] += n2
            assert ucur[0] <= UN, ucur[0]
            if dt == F32:
                a = a.bitcast(F32)
            return a

        def creset():
            ucur[0] = 0

        rr = {}
        _pb = {}

        def PB(name):
            if name not in _pb:
                _pb[name] = Buf(name)
            return _pb[name]

        def rot(key, n):
            v = rr.get(key, 0)
            rr[key] = v + 1
            return v % n

        def dma(q, out, in_, reads, writes, dst):
            P.add(q, lambda e: e.dma_start(out=out, in_=in_), reads, writes, dma_dst=dst)

        def act(out, in_, func, reads, writes, bias=None, scale=None, accum=None):
            kw = {}
            if bias is not None:
                kw["bias"] = bias
            if scale is not None:
                kw["scale"] = scale
            if accum is not None:
                kw["accum_out"] = accum
            P.add("act", lambda e: e.activation(out=out, in_=in_, func=func, **kw), reads, writes)

        def ts(out, in0, s1, op0, reads, writes, s2=None, op1=None, accum=None, eng="dve"):
            kw = {}
            if op1 is not None:
                kw["op1"] = op1
            if accum is not None:
                kw["accum_out"] = accum
            P.add(eng, lambda e: e.tensor_scalar(out=out, in0=in0, scalar1=s1, scalar2=s2, op0=op0, **kw), reads, writes)

        def tt(out, in0, in1, op, reads, writes, eng="dve"):
            P.add(eng, lambda e: e.tensor_tensor(out=out, in0=in0, in1=in1, op=op), reads, writes)

        def stt(out, in0, scalar, in1, op0, op1, reads, writes):
            P.add("dve", lambda e: e.scalar_tensor_tensor(out=out, in0=in0, scalar=scalar, in1=in1, op0=op0, op1=op1), reads, writes)

        def mm(out, lhsT, rhs, start, stop, reads, writes):
            P.add("pe", lambda e: e.matmul(out, lhsT=lhsT, rhs=rhs, start=start, stop=stop), reads, writes)

        def tr(out, in_, ident, reads, writes):
            P.add("pe", lambda e: e.transpose(out=out, in_=in_, identity=ident), reads, writes)

        def copy_alt(out, in_, reads, writes, scale=None):
            if rot("cp", 2) == 0:
                if scale is None:
                    act(out, in_, AF.Copy, reads, writes)
                else:
                    act(out, in_, AF.Copy, reads, writes, scale=scale)
            else:
                if scale is None:
                    P.add("dve", lambda e: e.tensor_copy(out=out, in_=in_), reads, writes)
                else:
                    ts(out, in_, scale, ALU.mult, reads, writes)

        def _body():
            P.add("pool", lambda e: e.memset(ones_f[:], 1.0), [], [b_ones])
            P.add("pool", lambda e: e.memset(zer_f[:], 0.0), [], [b_zer])
            P.add("pool", lambda e: e.memset(st[:], 0.0), [], [])
            P.add("pool", lambda e: e.affine_select(out=ident_f[:], in_=ones_f[:], pattern=[[-1, 128]], compare_op=ALU.is_equal, fill=0.0, base=0, channel_multiplier=1), [b_ones], [b_idf])
            P.add("pool", lambda e: e.affine_select(out=J_f[:], in_=ones_f[:], pattern=[[1, 128]], compare_op=ALU.is_equal, fill=0.0, base=-127, channel_multiplier=1), [b_ones], [b_J])
            P.add("pool", lambda e: e.affine_select(out=caus[:], in_=zer_f[:], pattern=[[-1, 128]], compare_op=ALU.is_ge, fill=-1e30, base=0, channel_multiplier=1), [b_zer], [b_caus])
            P.add("dve", lambda e: e.tensor_copy(out=ident_b[:], in_=ident_f[:]), [b_idf], [b_idb])
            ts(i32k[:], ident_f[:], 32768.0, ALU.mult, [b_idf], [b_i32k])
            P.add("pool", lambda e: e.memset(uh[:], 0.0), [], [b_uh])
            for k in range(NIT + 1):
                P.add("pool", lambda e, k=k: e.memset(pw[:, k:k + 1], 2.0 ** -(k + 1)), [], [b_pw])

            ck("c0")
            def cast_copy(src_t, dst_t, l, nelem, dbuf):
                rows = nelem // 2048
                r0 = 0
                while r0 < rows:
                    n = min(2048, rows - r0)
                    si = bass.AP(src_t, l * nelem + r0 * 2048, [[2048, n], [1, 2048]])
                    di = bass.AP(dst_t, l * nelem + r0 * 2048, [[2048, n], [1, 2048]])
                    dma("pool", di, si, [], [dbuf], dbuf)
                    r0 += n

            for l in range(NL):
                cast_copy(win_d, winb_d, l, D * INW, dW["in"][l])
                cast_copy(wuk_d, wukb_d, l, 16 * 256 * 64, dW["uk"][l])
                cast_copy(wuv_d, wuvb_d, l, 16 * 256 * 64, dW["uv"][l])
                cast_copy(wout_d, woutb_d, l, D * D, dW["out"][l])
                cast_copy(wup_d, wupb_d, l, D * 2 * FF, dW["up"][l])
                cast_copy(wdown_d, wdownb_d, l, FF * D, dW["down"][l])

            ck("cast")
            rowbuf = [carve(256, F32) for i in range(2)]
            b_rowbuf = [Buf(f"rowbuf{i}") for i in range(2)]

            def col_load(dst, b_dst, src_t, nrows):
                r0 = 0
                while r0 < nrows:
                    nb = min(128, nrows - r0)
                    k = rot("rowbuf", 2)
                    src = bass.AP(src_t, r0 * 128, [[128, nb], [1, 128]])
                    dma("sp", rowbuf[k][0:nb, :], src, [], [b_rowbuf[k]], b_rowbuf[k])
                    pb = rot("cl_ps", 2)
                    tr(bank(4 + pb)[:, 0:nb], rowbuf[k][0:nb, :], ident_f[0:nb, 0:nb], [b_rowbuf[k], b_idf], [bA[4 + pb]])
                    copy_alt(dst[:, r0:r0 + nb], bank(4 + pb)[:, 0:nb], [bA[4 + pb]], [b_dst])
                    r0 += nb

            col_load(bmodc, b_bmodc, bmod_d, NL * 96)
            col_load(nattn, b_nattn, nattn_d, NL * 16)
            col_load(nffn, b_nffn, nffn_d, NL * 16)
            col_load(kvn, b_kvn, kvn_d, NL * 2)
            col_load(convw, b_convw, convw_d, NL * 132)
            col_load(convb, b_convb, convb_d, NL * 44)
            col_load(nfin, b_nfin, nfin_d, 16)
            col_load(cTs, b_cTs, c_d, NSEQ * 16)
            act(cTs[:, 0:NSEQ * 16], cTs[:, 0:NSEQ * 16], AF.Silu, [b_cTs], [b_cTs])
            P.add("dve", lambda e: e.tensor_copy(out=cT2[:], in_=cTs[:, 0:NSEQ * 16].rearrange("p (b k) -> p k b", b=NSEQ)), [b_cTs], [b_cT2])
            dma("sp", sink_bc[:, 0:NL * 16], bass.AP(sinks_d, 0, [[0, 128], [1, NL * 16]]), [], [b_sink], b_sink)

            ck("cols")
            rel_aug = sb("rel_aug", [33, 32], F32); b_rel = Buf("rel_aug")
            oh_sb = carve(768 * 2, F32); b_oh = Buf("oh_sb")
            fv_sb = carve(384 * 2, F32); b_fvsb = Buf("fv_sb")
            hk = [carve(512, F32) for i in range(2)]
            b_hk = [Buf(f"hk{i}") for i in range(2)]
            P.add("pool", lambda e: e.memset(rel_aug[32:33, :], 1.0), [], [b_rel])
            dma("sp", rel_aug[0:32, :], rel_d.ap(), [], [b_rel], b_rel)
            dma("sp", oh_sb[0:33, :], oh_d.ap(), [], [b_oh], b_oh)
            for kind in range(2):
                mm(bank(4)[0:16, 0:384], rel_aug[0:33, kind * 16:(kind + 1) * 16], oh_sb[0:33, kind * 384:(kind + 1) * 384], True, True, [b_rel, b_oh], [bA[4]])
                P.add("dve", lambda e: e.tensor_copy(out=fv_sb[0:16, :], in_=bank(4)[0:16, 0:384]), [bA[4]], [b_fvsb])
                dma("sp", fv_d.ap()[kind * 16:(kind + 1) * 16, :], fv_sb[0:16, :], [b_fvsb], [dFV], dFV)
            for hh in range(32):
                k = rot("hk", 2)
                dma("sp", hk[k][:], bass.AP(fv_d, hh * 384, [[1, 128], [1, 256]]), [dFV], [b_hk[k]], b_hk[k])
                pb = rot("cl_ps", 2)
                mm(bank(4 + pb)[:, 0:256], J_f[:], hk[k][:], True, True, [b_J, b_hk[k]], [bA[4 + pb]])
                if hh < 16:
                    copy_alt(BA[:, hh, :], bank(4 + pb)[:, 0:256], [bA[4 + pb]], [b_BA])
                else:
                    copy_alt(BB[:, hh - 16, :], bank(4 + pb)[:, 0:256], [bA[4 + pb]], [b_BB])

            ck("bias")
            P.barrier()
            creset()
            xrow = carve(4 * 2048 * 2, F32).rearrange("p (a b) -> p a b", a=4)
            b_xrow = [Buf(f"xrow{i}") for i in range(4)]
            for s in range(NSEQ):
                for j in range(NTILE):
                    for tsub in range(NSUB):
                        r0 = j * TA + tsub * 128
                        dma("sp", xrow[:, tsub, :], x_a[s, r0:r0 + 128, :], [], [b_xrow[tsub]], b_xrow[tsub])
                    for kc in range(NKC):
                        pb = rot("xt_ps", 4)
                        for tsub in range(NSUB):
                            tr(bank(pb)[:, tsub * 128:(tsub + 1) * 128], xrow[:, tsub, kc * 128:(kc + 1) * 128], ident_f[:], [b_xrow[tsub], b_idf], [bA[pb]])
                        k = rot("xb", 4)
                        copy_alt(xb[k][:], bank(pb), [bA[pb]], [b_xb[k]])
                        dma("sp", xT_a[s, kc, :, j * TA:(j + 1) * TA], xb[k][:], [b_xb[k]], [dX[s][j][kc % 2]], dX[s][j][kc % 2])

            ck("xT")
            def norm_p1(s, j, kc):
                t0 = j * TA
                k = rot("xb", 4)
                dma("sp", xb[k][:], xT_a[s, kc, :, t0:t0 + TA], [dX[s][j][kc % 2]], [b_xb[k]], b_xb[k])
                q = rot("sqb", 2)
                act(sqb[q][:], xb[k][:], AF.Square, [b_xb[k]], [b_sqb[q]])
                mm(bank(5), ones_f[:], sqb[q][:], kc == 0, kc == NKC - 1, [b_ones, b_sqb[q]], [bA[5]])

            def norm_mid():
                act(rstd[:], bank(5), AF.Sqrt, [bA[5]], [b_rstd], bias=EPS, scale=1.0 / D)
                P.add("dve", lambda e: e.reciprocal(out=rstd[:], in_=rstd[:]), [b_rstd], [b_rstd])

            def norm_p2(s, j, kc, gs_ap, dst_fn):
                t0 = j * TA
                k = rot("xb", 4)
                dma("sp", xb[k][:], xT_a[s, kc, :, t0:t0 + TA], [dX[s][j][kc % 2]], [b_xb[k]], b_xb[k])
                q = rot("tmpn", 2)
                stt(tmpn[q][:], xb[k][:], gs_ap(kc), rstd[:], ALU.mult, ALU.mult, [b_xb[k], b_rstd, b_gs1, b_gs2, b_nfin], [b_tmpn[q]])
                dst_fn(kc, tmpn[q], b_tmpn[q])

            def norm_tile(s, j, gs_ap, sh_ap, dst_fn):
                for kc in range(NKC):
                    norm_p1(s, j, kc)
                norm_mid()
                for kc in range(NKC):
                    norm_p2(s, j, kc, gs_ap, dst_fn)

            def to_hT(sh_ap):
                def f(kc, tm, b_tm):
                    act(hT[:, kc, :], tm[:], AF.Identity, [b_tm, b_modc], [b_hT], bias=sh_ap(kc), scale=1.0)
                return f

            for l in range(NL):
                for kind in range(6):
                    for grp in range(8):
                        c0 = kind * 2048 + grp * 256
                        w = rot("WB", 2)
                        Wt = WB[w][:, 0:8192].bitcast(F32).rearrange("p (k c) -> p k c", k=16)
                        dma("sp", Wt, wmod_a[l, :, c0:c0 + 256].rearrange("(k p) c -> p k c", p=128), [], [b_WB[w]], b_WB[w])
                        for kc in range(NKC):
                            mm(bank(4)[0:NSEQ, 0:256], cT2[:, kc, :], Wt[:, kc, :], kc == 0, kc == NKC - 1, [b_cT2, b_WB[w]], [bA[4]])
                        act(mrow[0:NSEQ, :], bank(4)[0:NSEQ, 0:256], AF.Copy, [bA[4]], [b_mrow])
                        for jj in range(2):
                            tr(bank(5)[:, jj * NSEQ:(jj + 1) * NSEQ], mrow[0:NSEQ, jj * 128:(jj + 1) * 128], ident_f[0:NSEQ, 0:NSEQ], [b_mrow, b_idf], [bA[5]])
                        for jj in range(2):
                            ch = grp * 2 + jj
                            col = l * 96 + kind * 16 + ch
                            ts(modc[:, kind, ch, :], bank(5)[:, jj * NSEQ:(jj + 1) * NSEQ], bmodc[:, col:col + 1], ALU.add, [bA[5], b_bmodc], [b_modc])
                for s in range(NSEQ):
                    stt(gs1[:, s, :], modc[:, 1, :, s], 1.0, nattn[:, l * 16:(l + 1) * 16], ALU.add, ALU.mult, [b_modc, b_nattn], [b_gs1])
                    stt(gs2[:, s, :], modc[:, 4, :, s], 1.0, nffn[:, l * 16:(l + 1) * 16], ALU.add, ALU.mult, [b_modc, b_nffn], [b_gs2])
                ck("mod")
                P.barrier()
                creset()
                wukraw = carve(2 * 16 * 64, BF16).rearrange("p (r h d) -> p r h d", r=2, h=16)
                b_wukraw = PB("wukraw")
                for rc in range(2):
                    dma("sp", wukraw[:, rc, :, :], wukb_a[l, :, rc * 128:(rc + 1) * 128, :].rearrange("h p d -> p h d"), [dW["uk"][l]], [b_wukraw], b_wukraw)
                    dma("sp", wuv[:, rc, :, :], wuvb_a[l, :, rc * 128:(rc + 1) * 128, :].rearrange("h p d -> p h d"), [dW["uv"][l]], [b_wuv], b_wuv)
                for pp in range(8):
                    for rc in range(2):
                        idx = pp * 2 + rc
                        Bx, bBx = (B6, bB6) if idx < 8 else (B7, bB7)
                        tr(Bx[:, (idx % 8) * 128:(idx % 8 + 1) * 128], wukraw[:, rc, 2 * pp:2 * pp + 2, :].rearrange("p h d -> p (h d)"), ident_b[:], [b_wukraw, b_idb], [bBx])
                act(wukT[:, 0:4, :].rearrange("p a b -> p (a b)"), B6[:], AF.Copy, [bB6], [b_wukT])
                P.add("dve", lambda e: e.tensor_copy(out=wukT[:, 4:8, :].rearrange("p a b -> p (a b)"), in_=B7[:]), [bB7], [b_wukT])

                ck("wuk")
                for s in range(NSEQ):
                    P.barrier()
                    creset()
                    kaT0 = carve(2048, BF16); kaT1 = carve(2048, BF16); kiT = carve(2048, BF16)
                    b_kaT0, b_kaT1, b_kiT = Buf("kaT0"), Buf("kaT1"), Buf("kiT")
                    va = carve(16 * 128, BF16).rearrange("p (b d) -> p b d", b=16); b_va = Buf("va")
                    ckv = carve(16 * 256, BF16).rearrange("p (b d) -> p b d", b=16); b_ckv = Buf("ckv")
                    ckvT = carve(2 * 2048, BF16).rearrange("p (r t) -> p r t", r=2); b_ckvT = Buf("ckvT")
                    qaT = carve(8 * TA, BF16).rearrange("p (c t) -> p c t", c=8); b_qaT = Buf("qaT")
                    qbT = carve(8 * TA, BF16).rearrange("p (c t) -> p c t", c=8); b_qbT = Buf("qbT")
                    qiT = carve(8 * TA, BF16).rearrange("p (c t) -> p c t", c=8); b_qiT = Buf("qiT")
                    acc = carve(2048 * 2, F32); b_acc = Buf("acc")
                    rtmp = [carve(512 * 2, F32) for _ in range(2)]; b_rtmp = [Buf("rtmp0"), Buf("rtmp1")]
                    mneg = carve(2048, BF16); b_mneg = Buf("mneg")
                    Psb = [carve(2048, BF16) for _ in range(2)]; b_Psb = [Buf("P0"), Buf("P1")]
                    PTs = [carve(16 * 128, BF16).rearrange("p (b t) -> p b t", b=16) for _ in range(2)]; b_PTs = [Buf("PT0"), Buf("PT1")]
                    qlat = [carve(4 * 128, BF16).rearrange("p (h r t) -> p h r t", h=2, r=2) for _ in range(2)]; b_qlat = [Buf("ql0"), Buf("ql1")]
                    olat = [carve(2 * 128, BF16).rearrange("p (r t) -> p r t", r=2) for _ in range(2)]; b_olat = [Buf("ol0"), Buf("ol1")]
                    mixtok = carve(2048, BF16); b_mixtok = Buf("mixtok")
                    wi_sb = wi_sb_t[:]; b_wi = Buf("wi")
                    junkf = junkf_t[:]; b_junkf = Buf("junkf")
                    jk8 = carve(1024, BF16).bitcast(mybir.dt.uint8); b_jk8 = Buf("jk8")
                    b_mx = Buf("mx"); b_rs = [Buf("rs0"), Buf("rs1")]; b_rr = [Buf("rr0"), Buf("rr1")]
                    prep = [None]
                    b_st = Buf("stA")
                    b_bis = Buf("bis")

                    for j in range(NTILE):
                        t0 = j * TA
                        norm_tile(s, j, lambda kc: gs1[:, s, kc:kc + 1], None, to_hT(lambda kc: modc[:, 0, kc, s:s + 1]))
                        ck("norm")
                        def load_w(pieces, src_a, dbuf):
                            w = rot("WB", 2)
                            Wv = WB[w][:, 0:8192].rearrange("p (k c) -> p k c", k=16)
                            for (off, c0, n) in pieces:
                                dma("sp", Wv[:, :, off:off + n], src_a[:, c0:c0 + n].rearrange("(k p) c -> p k c", p=128), [dbuf], [b_WB[w]], b_WB[w])
                            return Wv, b_WB[w]

                        def feat_chunk(Wv, bW, cc, dest, b_dest, scale):
                            pb = rot("pj_ps", 4)
                            for kc in range(NKC):
                                mm(bank(pb), Wv[:, kc, cc * 128:(cc + 1) * 128], hT[:, kc, :], kc == 0, kc == NKC - 1, [bW, b_hT], [bA[pb]])
                            copy_alt(dest, bank(pb), [bA[pb]], [b_dest], scale=scale)

                        wl = winb_a[l]
                        for half in range(2):
                            Wv, bW = load_w([(0, half * 512, 512)], wl, dW["in"][l])
                            for cc in range(4):
                                feat_chunk(Wv, bW, cc, qaT[:, half * 4 + cc, :], b_qaT, 0.125)
                        ck("pqa")
                        Wv, bW = load_w([(0, 1024, 64), (64, 1024, 64), (128, 1088, 64), (192, 1088, 64), (256, 3584, 64), (320, 3584, 64)], wl, dW["in"][l])
                        feat_chunk(Wv, bW, 0, kaT0[:, t0:t0 + TA], b_kaT0, None)
                        feat_chunk(Wv, bW, 1, kaT1[:, t0:t0 + TA], b_kaT1, None)
                        feat_chunk(Wv, bW, 2, kiT[:, t0:t0 + TA], b_kiT, None)
                        ck("pkq")
                        for half in range(2):
                            Wv, bW = load_w([(0, 1280 + half * 512, 512)], wl, dW["in"][l])
                            for cc in range(4):
                                feat_chunk(Wv, bW, cc, qbT[:, half * 4 + cc, :], b_qbT, 0.125)
                        for half in range(2):
                            Wv, bW = load_w([(0, 2560 + half * 512, 512)], wl, dW["in"][l])
                            for cc in range(4):
                                feat_chunk(Wv, bW, cc, qiT[:, half * 4 + cc, :], b_qiT, None)
                        ck("pq")
                        Wv, bW = load_w([(0, 1152, 128), (128, 2304, 256), (384, 3600, 64)], wl, dW["in"][l])
                        ck("tm_ld")
                        for tsub in range(NSUB):
                            blk = j * NSUB + tsub
                            pb = rot("pj_ps", 4)
                            for kc in range(NKC):
                                mm(bank(pb)[:, 0:448], hT[:, kc, tsub * 128:(tsub + 1) * 128], Wv[:, kc, 0:448], kc == 0, kc == NKC - 1, [bW, b_hT], [bA[pb]])
                            ck("tm_mm")
                            P.add("dve", lambda e, blk=blk, pb=pb: e.tensor_copy(out=va[:, blk, :], in_=bank(pb)[:, 0:128]), [bA[pb]], [b_va])
                            ck("tm_va")
                            act(wi_sb[:, tsub, :], bank(pb)[:, 432:448], AF.Copy, [bA[pb]], [b_wi])
                            ck("tm_cp")
                            act(junkf[:], bank(pb)[:, 128:384], AF.Square, [bA[pb]], [b_junkf, b_st], accum=st[:, 0:1])
                            ck("tm_sq")
                            act(st[:, 1:2], st[:, 0:1], AF.Sqrt, [b_st], [b_st], bias=EPS, scale=1.0 / 256)
                            P.add("dve", lambda e: e.reciprocal(out=st[:, 2:3], in_=st[:, 1:2]), [b_st], [b_st])
                            ts(ckv[:, blk, :], bank(pb)[:, 128:384], st[:, 2:3], ALU.mult, [bA[pb], b_st], [b_ckv])
                            ck("tm_ckv")
                            for rc in range(2):
                                tr(B6[:, rc * 128:(rc + 1) * 128], ckv[:, blk, rc * 128:(rc + 1) * 128], ident_b[:], [b_ckv, b_idb], [bB6])
                            copy_alt(ckvT[:, :, blk * 128:(blk + 1) * 128], B6[:, 0:256].rearrange("p (r t) -> p r t", r=2), [bB6], [b_ckvT])

                        ck("proj")
                        for tsub in range(NSUB):
                            i = j * NSUB + tsub
                            tc = slice(tsub * 128, (tsub + 1) * 128)
                            nk = 256 if i > 0 else 128
                            ks = (i - 1) * 128 if i > 0 else 0
                            bc0 = 0 if i > 0 else 128
                            nkb = nk // 128
                            for half in range(2):
                                pr = rot("Psb", 2)
                                hb = half * 64
                                for hh in range(8):
                                    h = 2 * hh + half
                                    kaT, b_kaT = (kaT0, b_kaT0) if h < 8 else (kaT1, b_kaT1)
                                    o_ = A03[:, hh * 256:hh * 256 + nk]
                                    mm(o_, qaT[hb:hb + 64, hh, tc], kaT[hb:hb + 64, ks:ks + nk], True, False, [b_qaT, b_kaT], [bA[hh // 2]])
                                    mm(o_, ident_b[:], BA[:, h, bc0:bc0 + nk], False, True, [b_idb, b_BA], [bA[hh // 2]])
                                S3 = A03[:].rearrange("p (h k) -> p h k", h=8)[:, :, 0:nk]
                                P.add("dve", lambda e, S3=S3: e.tensor_reduce(out=st[:, 8:16], in_=S3, axis=AX.X, op=ALU.max), [bA[0], bA[1], bA[2], bA[3], b_st], [b_st])
                                _s = sink_bc[:, l * 16 + half:l * 16 + half + 1]
                                sk = bass.AP(_s.tensor, _s.offset, [list(_s.ap[0]), [2, 8]])
                                tt(st[:, 16:24], st[:, 8:16], sk, ALU.max, [b_st, b_sink], [b_st])
                                ts(st[:, 24:32], st[:, 16:24], -1.0, ALU.mult, [b_st], [b_st])
                                for hh in range(8):
                                    act(Psb[pr][:, hh * 256:hh * 256 + nk], A03[:, hh * 256:hh * 256 + nk], AF.Exp, [bA[hh // 2], b_st], [b_Psb[pr], b_st],
                                        bias=st[:, 24 + hh:25 + hh], scale=1.0, accum=st[:, 32 + hh:33 + hh])
                                tt(st[:, 40:48], sk, st[:, 24:32], ALU.add, [b_st, b_sink], [b_st])
                                act(st[:, 40:48], st[:, 40:48], AF.Exp, [b_st], [b_st])
                                tt(st[:, 40:48], st[:, 40:48], st[:, 32:40], ALU.add, [b_st], [b_st])
                                P.add("dve", lambda e, half=half: e.reciprocal(out=st[:, 48 + half * 8:56 + half * 8], in_=st[:, 40:48]), [b_st], [b_st])
                                pt = rot("PTs", 2)
                                for hh in range(8):
                                    for kb in range(nkb):
                                        sl = hh * 2 + kb
                                        Bx, bBx = (B6, bB6) if sl < 8 else (B7, bB7)
                                        tr(Bx[:, (sl % 8) * 128:(sl % 8 + 1) * 128], Psb[pr][:, hh * 256 + kb * 128:hh * 256 + (kb + 1) * 128], ident_b[:], [b_Psb[pr], b_idb], [bBx])
                                act(PTs[pt][:, 0:8, :].rearrange("p a b -> p (a b)"), B6[:], AF.Copy, [bB6], [b_PTs[pt]])
                                P.add("dve", lambda e, pt=pt: e.tensor_copy(out=PTs[pt][:, 8:16, :].rearrange("p a b -> p (a b)"), in_=B7[:]), [bB7], [b_PTs[pt]])
                                for hh in range(8):
                                    h = 2 * hh + half
                                    g = h // 8
                                    for kb in range(nkb):
                                        kblk = (i - 1 + kb) if i > 0 else 0
                                        mm(A45[:, h * 64:(h + 1) * 64], PTs[pt][:, hh * 2 + kb, :], va[:, kblk, g * 64:(g + 1) * 64], kb == 0, kb == nkb - 1, [b_PTs[pt], b_va], [bA[4 + h // 8]])
                            _a = st[:, 48:64]
                            rden = bass.AP(_a.tensor, _a.offset, [list(_a.ap[0]), [1, 8], [8, 2], [0, 64]])
                            tt(mixtok[:, 0:1024].rearrange("p (h e d) -> p h e d", h=8, e=2), A45[:].rearrange("p (h e d) -> p h e d", h=8, e=2), rden, ALU.mult, [bA[4], bA[5], b_st], [b_mixtok])

                            ck("swa%d" % i)
                            S = 128 * (i + 1)
                            nch = (S + 511) // 512

                            def indexer(ii, tsb):
                                S_ = 128 * (ii + 1)
                                tcc = slice(tsb * 128, (tsb + 1) * 128)
                                for h in range(16):
                                    pch, hb = h // 2, (h % 2) * 64
                                    for c in range((S_ + 511) // 512):
                                        w_ = min(512, S_ - c * 512)
                                        pb = rot("ix_ps", 4)
                                        mm(bank(pb)[:, 0:w_], qiT[hb:hb + 64, pch, tcc], kiT[hb:hb + 64, c * 512:c * 512 + w_], True, True, [b_qiT, b_kiT], [bA[pb]])
                                        rq = rot("rtmp", 2)
                                        act(rtmp[rq][:, 0:w_], bank(pb)[:, 0:w_], AF.Relu, [bA[pb]], [b_rtmp[rq]])
                                        a_ = acc[:, c * 512:c * 512 + w_]
                                        if h == 0:
                                            ts(a_, rtmp[rq][:, 0:w_], wi_sb[:, tsb, 0:1], ALU.mult, [b_rtmp[rq], b_wi], [b_acc])
                                        else:
                                            stt(a_, rtmp[rq][:, 0:w_], wi_sb[:, tsb, h:h + 1], a_, ALU.mult, ALU.add, [b_rtmp[rq], b_wi, b_acc], [b_acc])
                                lo, hi, wd, mid, cnt, tmp = (st[:, 64 + k:65 + k] for k in range(6))
                                P.add("dve", lambda e: e.tensor_reduce(out=lo, in_=acc[:, 0:S_], axis=AX.X, op=ALU.min), [b_acc], [b_bis])
                                P.add("dve", lambda e: e.tensor_reduce(out=hi, in_=acc[:, 0:S_], axis=AX.X, op=ALU.max), [b_acc], [b_bis])
                                tt(acc[:, ii * 128:(ii + 1) * 128], acc[:, ii * 128:(ii + 1) * 128], caus[:], ALU.add, [b_acc, b_caus], [b_acc])
                                tt(wd, hi, lo, ALU.subtract, [b_bis], [b_bis])
                                ts(bs2[:], pw[:], wd, ALU.mult, [b_bis, b_pw], [b_bis])
                                ts(bsn[:], pw[:], wd, ALU.mult, [b_bis, b_pw], [b_bis], s2=-0.5, op1=ALU.mult)
                                tt(mid, lo, bs2[:, 0:1], ALU.add, [b_bis], [b_bis])

                            def bis_iter(ii, k):
                                S_ = 128 * (ii + 1)
                                mid, cnt, tmp = st[:, 67:68], st[:, 68:69], st[:, 69:70]
                                ts(jk8[:, 0:S_], acc[:, 0:S_], mid, ALU.is_ge, [b_acc, b_bis], [b_jk8, b_bis], s2=None, op1=ALU.add, accum=cnt)
                                if k < NIT - 1:
                                    ts(tmp, cnt, 255.5, ALU.is_ge, [b_bis], [b_bis], s2=bs2[:, k:k + 1], op1=ALU.mult)
                                    stt(mid, tmp, bsn[:, k:k + 1], mid, ALU.add, ALU.add, [b_bis], [b_bis])
                                else:
                                    ts(tmp, cnt, 255.5, ALU.is_lt, [b_bis], [b_bis], s2=bs2[:, k:k + 1], op1=ALU.mult)
                                    tt(mid, mid, tmp, ALU.subtract, [b_bis], [b_bis])

                            def bis_final(ii):
                                S_ = 128 * (ii + 1)
                                ts(mneg[:, 0:S_], acc[:, 0:S_], st[:, 67:68], ALU.is_ge, [b_acc, b_bis], [b_mneg], s2=-1.0, op1=ALU.add)

                            if i >= 2 and prep[0] != i:
                                indexer(i, tsub)
                                for k in range(NIT):
                                    bis_iter(i, k)
                                bis_final(i)
                            nxt = (i + 1) if (tsub < NSUB - 1 and i + 1 >= 2) else None
                            if nxt is not None:
                                indexer(nxt, tsub + 1)
                            bis_k = [0]

                            def bis_some(n):
                                if nxt is None:
                                    return
                                for _ in range(n):
                                    if bis_k[0] < NIT:
                                        bis_iter(nxt, bis_k[0])
                                        bis_k[0] += 1

                            def pre_pe(h, ql, hs):
                                for c in range(nch):
                                    c0 = c * 512
                                    w_ = min(512, S - c0)
                                    ops = [(qlat[ql][:, hs, 0, :], ckvT[:, 0, c0:c0 + w_], 0, w_, [b_qlat[ql], b_ckvT]),
                                           (qlat[ql][:, hs, 1, :], ckvT[:, 1, c0:c0 + w_], 0, w_, [b_qlat[ql], b_ckvT])]
                                    if i >= 2:
                                        ops.append((i32k[:], mneg[:, c0:c0 + w_], 0, w_, [b_i32k, b_mneg]))
                                    for kb in ((i - 1, i) if i > 0 else (i,)):
                                        if c0 <= kb * 128 < c0 + w_:
                                            bb = (kb - (i - 1)) if i > 0 else 1
                                            ops.append((ident_b[:], BB[:, h, bb * 128:(bb + 1) * 128], kb * 128 - c0, 128, [b_idb, b_BB]))
                                    for n_, (lh, rh, o0, ow, rd) in enumerate(ops):
                                        mm(bank(c)[:, o0:o0 + ow], lh, rh, n_ == 0, n_ == len(ops) - 1, rd, [bA[c]])

                            def pre_dve(h):
                                banks_r = [bA[c] for c in range(nch)]
                                P.add("dve", lambda e, S=S: e.tensor_reduce(out=st[:, 72:73], in_=A03[:, 0:S], axis=AX.X, op=ALU.max), banks_r, [b_mx])
                                ts(st[:, 73:74], st[:, 72:73], -1.0, ALU.mult, [b_mx], [b_mx])

                            def pre_act(h):
                                banks_r = [bA[c] for c in range(nch)]
                                pr = h % 2
                                act(Psb[pr][:, 0:S], A03[:, 0:S], AF.Exp, banks_r + [b_mx], [b_Psb[pr], b_rs[h % 2]], bias=st[:, 73:74], scale=1.0, accum=st[:, 80 + h:81 + h])

                            def post_T_pe(h):
                                pr = h % 2
                                for b_ in range(i + 1):
                                    Bx, bBx = (B6, bB6) if b_ < 8 else (B7, bB7)
                                    tr(Bx[:, (b_ % 8) * 128:(b_ % 8 + 1) * 128], Psb[pr][:, b_ * 128:(b_ + 1) * 128], ident_b[:], [b_Psb[pr], b_idb], [bBx])

                            def post_T_act(h):
                                pt = h % 2
                                n6 = min(8, i + 1)
                                act(PTs[pt][:, 0:n6, :].rearrange("p a b -> p (a b)"), B6[:, 0:n6 * 128], AF.Copy, [bB6], [b_PTs[pt]])

                            def post_T_dve(h):
                                pt = h % 2
                                if i + 1 > 8:
                                    n7 = i + 1 - 8
                                    P.add("dve", lambda e, pt=pt, n7=n7: e.tensor_copy(out=PTs[pt][:, 8:8 + n7, :].rearrange("p a b -> p (a b)"), in_=B7[:, 0:n7 * 128]), [bB7], [b_PTs[pt]])

                            def post_rest(h):
                                pt = h % 2
                                for rc in range(2):
                                    for b_ in range(i + 1):
                                        mm(bank(5)[:, rc * 128:(rc + 1) * 128], ckv[:, b_, rc * 128:(rc + 1) * 128], PTs[pt][:, b_, :], b_ == 0, b_ == i, [b_ckv, b_PTs[pt]], [bA[5]])
                                ol = rot("olat", 2)
                                for rc in range(2):
                                    ts(olat[ol][:, rc, :], bank(5)[:, rc * 128:(rc + 1) * 128], kvn[:, l * 2 + rc:l * 2 + rc + 1], ALU.mult, [bA[5], b_kvn], [b_olat[ol]])
                                for rc in range(2):
                                    mm(bank(5)[:, 256:320], olat[ol][:, rc, :], wuv[:, rc, h, :], rc == 0, rc == 1, [b_olat[ol], b_wuv], [bA[5]])
                                P.add("dve", lambda e, h=h: e.reciprocal(out=st[:, 96 + h:97 + h], in_=st[:, 80 + h:81 + h]), [b_rs[h % 2]], [b_rr[h % 2]])
                                ts(mixtok[:, 1024 + h * 64:1024 + (h + 1) * 64], bank(5)[:, 256:320], st[:, 96 + h:97 + h], ALU.mult, [bA[5], b_rr[h % 2]], [b_mixtok])

                            prev = None
                            for pp in range(8):
                                ql = rot("qlat", 2)
                                for hs in range(2):
                                    hb = hs * 64
                                    for rc in range(2):
                                        mm(bank(4 + hs)[:, rc * 128:(rc + 1) * 128], wukT[hb:hb + 64, pp, rc * 128:(rc + 1) * 128], qbT[hb:hb + 64, pp, tc], True, True, [b_wukT, b_qbT], [bA[4 + hs]])
                                for hs in range(2):
                                    for rc in range(2):
                                        ts(qlat[ql][:, hs, rc, :], bank(4 + hs)[:, rc * 128:(rc + 1) * 128], kvn[:, l * 2 + rc:l * 2 + rc + 1], ALU.mult, [bA[4 + hs], b_kvn], [b_qlat[ql]])
                                for hs in range(2):
                                    h = pp * 2 + hs
                                    pre_pe(h, ql, hs)
                                    if prev is not None:
                                        post_T_pe(prev)
                                        post_T_act(prev)
                                        post_T_dve(prev)
                                    pre_dve(h)
                                    pre_act(h)
                                    if prev is not None:
                                        post_rest(prev)
                                    bis_some(2)
                                    prev = h
                            post_T_pe(prev)
                            post_T_act(prev)
                            post_T_dve(prev)
                            post_rest(prev)
                            if nxt is not None:
                                bis_some(NIT)
                                bis_final(nxt)
                                prep[0] = nxt
                            ck("dsa%d" % i)
                            for c in range(16):
                                Bx, bBx = (B6, bB6) if c < 8 else (B7, bB7)
                                tr(Bx[:, (c % 8) * 128:(c % 8 + 1) * 128], mixtok[:, c * 128:(c + 1) * 128], ident_b[:], [b_mixtok, b_idb], [bBx])
                            act(hT[:, 0:8, tc], B6[:].rearrange("p (c t) -> p c t", c=8), AF.Copy, [bB6], [b_hT])
                            P.add("dve", lambda e, tc=tc: e.tensor_copy(out=hT[:, 8:16, tc], in_=B7[:].rearrange("p (c t) -> p c t", c=8)), [bB7], [b_hT])

                        ck("attn%d" % j)
                        for g in range(4):
                            w = rot("WB", 2)
                            Wv = WB[w][:, 0:8192].rearrange("p (k c) -> p k c", k=16)
                            dma("sp", Wv, woutb_a[l][:, g * 512:(g + 1) * 512].rearrange("(k p) c -> p k c", p=128), [dW["out"][l]], [b_WB[w]], b_WB[w])
                            for mm_ in range(4):
                                m = g * 4 + mm_
                                pb = rot("pj_ps", 4)
                                for kc in range(NKC):
                                    mm(bank(pb), Wv[:, kc, mm_ * 128:(mm_ + 1) * 128], hT[:, kc, :], kc == 0, kc == NKC - 1, [b_WB[w], b_hT], [bA[pb]])
                                k = rot("xb", 4)
                                dma("sp", xb[k][:], xT_a[s, m, :, t0:t0 + TA], [dX[s][j][m % 2]], [b_xb[k]], b_xb[k])
                                stt(xb[k][:], bank(pb), modc[:, 2, m, s:s + 1], xb[k][:], ALU.mult, ALU.add, [bA[pb], b_modc, b_xb[k]], [b_xb[k]])
                                dma("pool", xT_a[s, m, :, t0:t0 + TA], xb[k][:], [b_xb[k]], [dX[s][j][m % 2]], dX[s][j][m % 2])

                    ck("att")
                    P.barrier()
                    creset()
                    actT = carve(NFC * TA, BF16).rearrange("p (c t) -> p c t", c=NFC); b_actT = Buf("actT")
                    a_t = [carve(TA * 2, F32) for _ in range(2)]; b_at = [Buf("at0"), Buf("at1")]
                    s_t = [carve(TA * 2, F32) for _ in range(2)]; b_stt = [Buf("st0"), Buf("st1")]
                    WF = [WB[0][:, 0:8192], WB[1][:, 0:8192], carve(8192, BF16), carve(8192, BF16)]
                    b_WF = [b_WB[0], b_WB[1], PB("WF2"), PB("WF3")]
                    gs2f = lambda kc: gs2[:, s, kc:kc + 1]
                    hT2 = to_hT(lambda kc: modc[:, 3, kc, s:s + 1])
                    for j in range(NTILE):
                        t0 = j * TA
                        if j == 0 or not FFN_OVERLAP:
                            norm_tile(s, j, gs2f, None, hT2)
                        for q in range(11):
                            wu = rot("WF", 4)
                            Wu = WF[wu].rearrange("p (k c) -> p k c", k=16)
                            dma("sp", Wu, wupb_a[l][:, q * 512:(q + 1) * 512].rearrange("(k p) c -> p k c", p=128), [dW["up"][l]], [b_WF[wu]], b_WF[wu])
                            wg = rot("WF", 4)
                            Wg = WF[wg].rearrange("p (k c) -> p k c", k=16)
                            dma("sp", Wg, wupb_a[l][:, FF + q * 512:FF + (q + 1) * 512].rearrange("(k p) c -> p k c", p=128), [dW["up"][l]], [b_WF[wg]], b_WF[wg])
                            for cc in range(4):
                                c = q * 4 + cc
                                pu = rot("ff_ps", 3) * 2
                                pg = pu + 1
                                for kc in range(NKC):
                                    mm(bank(pu), Wu[:, kc, cc * 128:(cc + 1) * 128], hT[:, kc, :], kc == 0, kc == NKC - 1, [b_WF[wu], b_hT], [bA[pu]])
                                for kc in range(NKC):
                                    mm(bank(pg), Wg[:, kc, cc * 128:(cc + 1) * 128], hT[:, kc, :], kc == 0, kc == NKC - 1, [b_WF[wg], b_hT], [bA[pg]])
                                ai = rot("a_t", 2)
                                a_ = a_t[ai]
                                w0 = convw[:, (l * 3 + 0) * NFC + c:(l * 3 + 0) * NFC + c + 1]
                                w1 = convw[:, (l * 3 + 1) * NFC + c:(l * 3 + 1) * NFC + c + 1]
                                w2 = convw[:, (l * 3 + 2) * NFC + c:(l * 3 + 2) * NFC + c + 1]
                                cb = convb[:, l * NFC + c:l * NFC + c + 1]
                                up = bank(pu)
                                act(a_[:, 0:TA], up, AF.Identity, [bA[pu], b_convw, b_convb], [b_at[ai]], bias=cb, scale=w2)
                                stt(a_[:, 1:TA], up[:, 0:TA - 1], w1, a_[:, 1:TA], ALU.mult, ALU.add, [bA[pu], b_at[ai], b_convw], [b_at[ai]])
                                stt(a_[:, 2:TA], up[:, 0:TA - 2], w0, a_[:, 2:TA], ALU.mult, ALU.add, [bA[pu], b_at[ai], b_convw], [b_at[ai]])
                                if j > 0:
                                    stt(a_[:, 0:1], uh[:, c, 1:2], w1, a_[:, 0:1], ALU.mult, ALU.add, [b_uh, b_at[ai], b_convw], [b_at[ai]])
                                    stt(a_[:, 0:2], uh[:, c, 0:2], w0, a_[:, 0:2], ALU.mult, ALU.add, [b_uh, b_at[ai], b_convw], [b_at[ai]])
                                P.add("dve", lambda e, c=c, up=up: e.tensor_copy(out=uh[:, c, :], in_=up[:, TA - 2:TA]), [bA[pu]], [b_uh])
                                si = rot("s_t", 2)
                                act(s_t[si][:, 0:TA], a_[:, 0:TA], AF.Silu, [b_at[ai]], [b_stt[si]])
                                tt(actT[:, c, :], s_t[si][:, 0:TA], bank(pg), ALU.mult, [b_stt[si], bA[pg]], [b_actT])
                        for mp in range(8):
                            wds = []
                            for hf in range(2):
                                wd = rot("WF", 4)
                                Wd = WF[wd][:, 0:22 * 256].rearrange("p (c m) -> p c m", c=22)
                                dma("sp", Wd, wdownb_a[l][hf * 2816:(hf + 1) * 2816, mp * 256:(mp + 1) * 256].rearrange("(c p) m -> p c m", p=128), [dW["down"][l]], [b_WF[wd]], b_WF[wd])
                                wds.append((Wd, b_WF[wd]))
                            for mm_ in range(2):
                                m = mp * 2 + mm_
                                pb = rot("pj_ps", 4)
                                for c in range(NFC):
                                    Wd, bWd = wds[c // 22]
                                    mm(bank(pb), Wd[:, c % 22, mm_ * 128:(mm_ + 1) * 128], actT[:, c, :], c == 0, c == NFC - 1, [bWd, b_actT], [bA[pb]])
                                k = rot("xb", 4)
                                dma("sp", xb[k][:], xT_a[s, m, :, t0:t0 + TA], [dX[s][j][m % 2]], [b_xb[k]], b_xb[k])
                                stt(xb[k][:], bank(pb), modc[:, 5, m, s:s + 1], xb[k][:], ALU.mult, ALU.add, [bA[pb], b_modc, b_xb[k]], [b_xb[k]])
                                dma("pool", xT_a[s, m, :, t0:t0 + TA], xb[k][:], [b_xb[k]], [dX[s][j][m % 2]], dX[s][j][m % 2])
                                if FFN_OVERLAP and j + 1 < NTILE:
                                    if m < 8:
                                        norm_p1(s, j + 1, 2 * m)
                                        norm_p1(s, j + 1, 2 * m + 1)
                                        if m == 7:
                                            norm_mid()
                                    else:
                                        norm_p2(s, j + 1, 2 * (m - 8), gs2f, hT2)
                                        norm_p2(s, j + 1, 2 * (m - 8) + 1, gs2f, hT2)

            ck("layers")
            P.barrier()
            creset()
            orow = carve(4 * 2048 * 2, F32).rearrange("p (a b) -> p a b", a=4)
            b_orow = [Buf(f"orow{i}") for i in range(4)]
            for s in range(NSEQ):
                for j in range(NTILE):
                    def fin(kc, tm, b_tm):
                        pb = rot("xt_ps", 4)
                        for tsub in range(NSUB):
                            tr(bank(pb)[:, tsub * 128:(tsub + 1) * 128], tm[:, tsub * 128:(tsub + 1) * 128], ident_f[:], [b_tm, b_idf], [bA[pb]])
                        for tsub in range(NSUB):
                            copy_alt(orow[:, tsub, kc * 128:(kc + 1) * 128], bank(pb)[:, tsub * 128:(tsub + 1) * 128], [bA[pb]], [b_orow[tsub]])
                    norm_tile(s, j, lambda kc: nfin[:, kc:kc + 1], None, fin)
                    for tsub in range(NSUB):
                        r0 = j * TA + tsub * 128
                        dma("pool", out_a[s, r0:r0 + 128, :], orow[:, tsub, :], [b_orow[tsub]], [dOUT], dOUT)

        try:
            _body()
        except _Stop:
            pass
        P.add("sp", None, [dOUT, dDBG], [])
        P.emit(es)
    return nc, P


_CACHE = {}


def kernel(**inputs):
    n = 8
    x = np.ascontiguousarray(np.asarray(inputs["x"], dtype=np.float32))
    c = np.ascontiguousarray(np.asarray(inputs["c"], dtype=np.float32))
    if "nc" not in _CACHE:
        _CACHE["nc"] = build()[0]
    nc = _CACHE["nc"]
    oh = _onehot_const().reshape(33, 768)
    shared = {k: np.ascontiguousarray(np.asarray(inputs[k], dtype=np.float32)) for k in
              ("w_mod", "b_mod", "norm_attn", "w_in", "attn_sinks", "kv_norm", "w_uk", "w_uv", "rel_bias",
               "w_out", "norm_ffn", "w_up", "conv_w", "conv_b", "w_down", "norm_final")}
    in_maps = []
    for i in range(n):
        m = dict(shared)
        m["x"] = x[2 * i:2 * i + 2]
        m["c"] = c[2 * i:2 * i + 2]
        m["oh"] = oh
        in_maps.append(m)
    res = run_bass_kernel_spmd(nc, in_maps, core_ids=list(range(n)))
    return np.concatenate([r["out"] for r in res.results], axis=0)
```

```python
import math
from contextlib import ExitStack
import numpy as np
import concourse.bass as bass
import concourse.mybir as mybir
from concourse.bass_utils import run_bass_kernel_spmd

F32 = mybir.dt.float32
BF16 = mybir.dt.bfloat16
AF = mybir.ActivationFunctionType
ALU = mybir.AluOpType
AX = mybir.AxisListType

D = 2048
T = 2048
NKC = 16
TA = 512
NTILE = T // TA
NSUB = TA // 128
FF = 5632
NFC = 44
INW = 3664
EPS = 1e-6
NIT = 26
UN = 50432
FFN_OVERLAP = True


class Buf:
    __slots__ = ("name", "lw", "rdc", "rdd", "dsem", "dcnt", "excl")

    def __init__(self, name, excl=False):
        self.name = name
        self.excl = excl
        self.lw = None
        self.rdc = {}
        self.rdd = []
        self.dsem = None
        self.dcnt = 0


class Op:
    __slots__ = ("eng", "fn", "dma", "deps", "tok", "inc", "idx")


class Prog:
    ENGS = ("pe", "act", "dve", "pool", "sp")

    def __init__(self, nc):
        self.nc = nc
        self.ops = []
        self.last = {e: None for e in self.ENGS}
        self.dmas_since_bar = []
        self.dbufs = []

    def add(self, eng, fn, reads=(), writes=(), dma_dst=None):
        o = Op()
        o.eng = eng
        o.fn = fn
        o.dma = dma_dst is not None
        o.inc = False
        o.idx = len(self.ops)
        o.tok = None
        deps = set()
        for b in reads:
            if b.lw is not None:
                deps.add(b.lw)
            if b.excl:
                for e, r in b.rdc.items():
                    if e != eng:
                        deps.add(r)
        for b in writes:
            if b.lw is not None:
                deps.add(b.lw)
            for e, r in b.rdc.items():
                if e == eng and not o.dma:
                    continue
                deps.add(r)
            for r in b.rdd:
                deps.add(r)
        if eng == "pe" and not o.dma:
            deps = {d for d in deps if d.dma or d.eng != "pe"}
        o.deps = deps
        for d in deps:
            if not d.dma:
                d.inc = True
        if o.dma:
            if dma_dst.dsem is None:
                dma_dst.dsem = "pending"
                self.dbufs.append(dma_dst)
            dma_dst.dcnt += 16
            o.tok = (dma_dst, dma_dst.dcnt)
            self.dmas_since_bar.append(o)
        for b in reads:
            if o.dma:
                b.rdd.append(o)
            else:
                b.rdc[eng] = o
        for b in writes:
            b.lw = o
            b.rdc = {}
            b.rdd = []
        self.ops.append(o)
        self.last[eng] = o
        return o

    def barrier(self):
        lasts = [o for o in self.last.values() if o is not None]
        pend = list(self.dmas_since_bar)
        self.dmas_since_bar = []
        for e in self.ENGS:
            o = Op()
            o.eng = e
            o.fn = None
            o.dma = False
            o.inc = False
            o.idx = len(self.ops)
            o.tok = None
            o.deps = set(x for x in lasts if (x.dma or x.eng != e)) | set(pend)
            for d in o.deps:
                if not d.dma:
                    d.inc = True
            self.ops.append(o)
            self.last[e] = o

    def emit(self, es):
        nc = self.nc
        esem = {}
        for e in ("pe", "act", "dve", "pool"):
            esem[e] = es.enter_context(nc.semaphore("es_" + e))
        for b in self.dbufs:
            b.dsem = es.enter_context(nc.semaphore("ds%d_%s" % (self.dbufs.index(b), b.name)))
        cnt = {e: 0 for e in self.ENGS}
        for o in self.ops:
            if o.dma or o.fn is None:
                continue
            if o.inc:
                cnt[o.eng] += 1
                o.tok = (o.eng, cnt[o.eng])
        by_eng = {e: [o for o in self.ops if o.eng == e] for e in self.ENGS}
        self.stats = {e: len(v) for e, v in by_eng.items()}

        def run(ename, eng):
            seen = {}
            for o in by_eng[ename]:
                waits = {}
                for d in o.deps:
                    if d.tok is None:
                        continue
                    key, val = d.tok
                    kid = key if isinstance(key, str) else id(key)
                    if kid not in waits or waits[kid][1] < val:
                        waits[kid] = (key, val)
                for kid, (key, val) in waits.items():
                    if seen.get(kid, 0) >= val:
                        continue
                    seen[kid] = val
                    sem = esem[key] if isinstance(key, str) else key.dsem
                    eng.wait_ge(sem, val)
                if o.fn is None:
                    continue
                inst = o.fn(eng)
                if o.dma:
                    inst.then_inc(o.tok[0].dsem, 16)
                elif o.inc:
                    inst.then_inc(esem[ename], 1)

        with nc.Block() as block:
            @block.tensor
            def _(eng):
                run("pe", eng)

            @block.scalar
            def _(eng):
                run("act", eng)

            @block.vector
            def _(eng):
                run("dve", eng)

            @block.gpsimd
            def _(eng):
                run("pool", eng)

            @block.sync
            def _(eng):
                run("sp", eng)


def _rel_bucket_np(n):
    n = np.maximum(n, 0)
    max_exact = 16
    nf = np.maximum(n, 1).astype(np.float32)
    v = (np.log(nf / np.float32(max_exact)) / np.float32(math.log(128 / max_exact))
         * np.float32(32 - max_exact)).astype(np.float32)
    large = max_exact + v.astype(np.int32)
    large = np.minimum(large, 31)
    return np.where(n < max_exact, n, large)


def _onehot_const():
    oh = np.zeros((33, 2, 384), np.float32)
    for u in range(383):
        dist = 255 - u
        if 0 <= dist < 128:
            oh[int(_rel_bucket_np(np.array(dist))), 0, u] = 1.0
        else:
            oh[32, 0, u] = -30000.0
        if dist >= 0:
            oh[int(_rel_bucket_np(np.array(dist))), 1, u] += 1.0
            oh[31, 1, u] -= 1.0
        else:
            oh[32, 1, u] = -30000.0
    oh[32, :, 383] = -30000.0
    return oh


class _Stop(Exception):
    pass


def build(NL=4, NSEQ=2, debug=False, stop=None):
    nc = bass.Bass("TRN2", target_bir_lowering=False)
    P = Prog(nc)
    dbg_d = nc.dram_tensor("dbg", [16, 128, 2048], F32, kind="ExternalOutput") if debug else None
    dDBG = Buf("dDBG")

    def ck(name):
        if stop == name:
            raise _Stop()

    def dump(slot, ap, bufs, n):
        if debug:
            P.add("pool", lambda e: e.dma_start(out=dbg_d.ap()[slot, 0:ap.shape[0], 0:n], in_=ap), bufs, [dDBG], dma_dst=dDBG)

    def din(name, shape):
        return nc.dram_tensor(name, list(shape), F32, kind="ExternalInput")

    x_d = din("x", [NSEQ, T, D])
    c_d = din("c", [NSEQ, D])
    wmod_d = din("w_mod", [NL, D, 6 * D])
    bmod_d = din("b_mod", [NL, 6 * D])
    nattn_d = din("norm_attn", [NL, D])
    win_d = din("w_in", [NL, D, INW])
    sinks_d = din("attn_sinks", [NL, 16])
    kvn_d = din("kv_norm", [NL, 256])
    wuk_d = din("w_uk", [NL, 16, 256, 64])
    wuv_d = din("w_uv", [NL, 16, 256, 64])
    rel_d = din("rel_bias", [32, 32])
    wout_d = din("w_out", [NL, D, D])
    nffn_d = din("norm_ffn", [NL, D])
    wup_d = din("w_up", [NL, D, 2 * FF])
    convw_d = din("conv_w", [NL, 3, FF])
    convb_d = din("conv_b", [NL, FF])
    wdown_d = din("w_down", [NL, FF, D])
    nfin_d = din("norm_final", [D])
    oh_d = din("oh", [33, 768])
    out_d = nc.dram_tensor("out", [NSEQ, T, D], F32, kind="ExternalOutput")

    xT_d = nc.dram_tensor("xT", [NSEQ, NKC, 128, T], F32, kind="Internal")
    fv_d = nc.dram_tensor("fv", [32, 384], F32, kind="Internal")
    winb_d = nc.dram_tensor("winb", [NL, D, INW], BF16, kind="Internal")
    woutb_d = nc.dram_tensor("woutb", [NL, D, D], BF16, kind="Internal")
    wupb_d = nc.dram_tensor("wupb", [NL, D, 2 * FF], BF16, kind="Internal")
    wdownb_d = nc.dram_tensor("wdownb", [NL, FF, D], BF16, kind="Internal")
    wukb_d = nc.dram_tensor("wukb", [NL, 16, 256, 64], BF16, kind="Internal")
    wuvb_d = nc.dram_tensor("wuvb", [NL, 16, 256, 64], BF16, kind="Internal")

    x_a, c_a, wmod_a, win_a = x_d.ap(), c_d.ap(), wmod_d.ap(), win_d.ap()
    out_a, xT_a = out_d.ap(), xT_d.ap()
    winb_a, woutb_a, wupb_a, wdownb_a = winb_d.ap(), woutb_d.ap(), wupb_d.ap(), wdownb_d.ap()
    wukb_a, wuvb_a = wukb_d.ap(), wuvb_d.ap()

    dX = [[[Buf(f"dX{s}_{j}_{q}") for q in range(2)] for j in range(NTILE)] for s in range(NSEQ)]
    dW = {k: [Buf(f"dW{k}{l}") for l in range(4)] for k in ("in", "out", "up", "down", "uk", "uv")}
    dFV = Buf("dFV")
    dOUT = Buf("dOUT")

    with ExitStack() as es:
        def sb(name, shape, dt):
            return es.enter_context(nc.sbuf_tensor(name, list(shape), dt))

        def pst(name, shape, dt):
            return es.enter_context(nc.psum_tensor(name, list(shape), dt))

        A03 = pst("A03", [128, 2048], F32)
        A45 = pst("A45", [128, 1024], F32)
        B6 = pst("B6", [128, 1024], BF16)
        B7 = pst("B7", [128, 1024], BF16)
        bA = [Buf(f"bA{i}", excl=True) for i in range(6)]
        bB6, bB7 = Buf("bB6", excl=True), Buf("bB7", excl=True)

        def bank(i):
            if i < 4:
                return A03[:, i * 512:(i + 1) * 512]
            return A45[:, (i - 4) * 512:(i - 3) * 512]

        ident_f = sb("ident_f", [128, 128], F32); b_idf = Buf("idf")
        ident_b = sb("ident_b", [128, 128], BF16); b_idb = Buf("idb")
        i32k = sb("i32k", [128, 128], BF16); b_i32k = Buf("i32k")
        J_f = sb("J_f", [128, 128], F32); b_J = Buf("J")
        ones_f = sb("ones_f", [128, 128], F32); b_ones = Buf("ones")
        zer_f = sb("zer_f", [128, 128], F32); b_zer = Buf("zer")
        caus = sb("caus", [128, 128], F32); b_caus = Buf("caus")
        BA = sb("BA", [128, 16, 256], BF16); b_BA = Buf("BA")
        BB = sb("BB", [128, 16, 256], BF16); b_BB = Buf("BB")
        bmodc = sb("bmodc", [128, 384], F32); b_bmodc = Buf("bmodc")
        nattn = sb("nattn", [128, 64], F32); b_nattn = Buf("nattn")
        nffn = sb("nffn", [128, 64], F32); b_nffn = Buf("nffn")
        kvn = sb("kvn", [128, 8], F32); b_kvn = Buf("kvn")
        convw = sb("convw", [128, 528], F32); b_convw = Buf("convw")
        convb = sb("convb", [128, 176], F32); b_convb = Buf("convb")
        nfin = sb("nfin", [128, 16], F32); b_nfin = Buf("nfin")
        cTs = sb("cTs", [128, 32], F32); b_cTs = Buf("cTs")
        cT2 = sb("cT2", [128, 16, NSEQ], F32); b_cT2 = Buf("cT2")
        sink_bc = sb("sink_bc", [128, 64], F32); b_sink = Buf("sink")
        modc = sb("modc", [128, 6, 16, NSEQ], F32); b_modc = Buf("modc")
        gs1 = sb("gs1", [128, NSEQ, 16], F32); b_gs1 = Buf("gs1")
        gs2 = sb("gs2", [128, NSEQ, 16], F32); b_gs2 = Buf("gs2")
        wukT = sb("wukT", [128, 8, 256], BF16); b_wukT = Buf("wukT")
        wuv = sb("wuv", [128, 2, 16, 64], BF16); b_wuv = Buf("wuv")
        mrow = sb("mrow", [2, 256], F32); b_mrow = Buf("mrow")
        uh = sb("uh", [128, NFC, 2], F32); b_uh = Buf("uh")

        hT = sb("hT", [128, 16, TA], BF16); b_hT = Buf("hT")
        WB = [sb(f"WB{i}", [128, 8192], BF16) for i in range(2)]
        b_WB = [Buf(f"WB{i}") for i in range(2)]
        xb = [sb(f"xb{i}", [128, TA], F32) for i in range(4)]
        b_xb = [Buf(f"xb{i}") for i in range(4)]
        sqb = [sb(f"sqb{i}", [128, TA], F32) for i in range(2)]
        b_sqb = [Buf(f"sqb{i}") for i in range(2)]
        rstd = sb("rstd", [128, TA], F32); b_rstd = Buf("rstd")
        tmpn = [sb(f"tmpn{i}", [128, TA], F32) for i in range(2)]
        b_tmpn = [Buf(f"tmpn{i}") for i in range(2)]
        st = sb("st", [128, 128], F32)
        pw = sb("pw", [128, NIT + 1], F32); b_pw = Buf("pw")
        bs2 = sb("bs2", [128, NIT + 1], F32)
        bsn = sb("bsn", [128, NIT + 1], F32)
        wi_sb_t = sb("wi_sb", [128, NSUB, 16], F32)
        junkf_t = sb("junkf", [128, 256], F32)
        U = sb("U", [128, UN], BF16)

        ucur = [0]

        def carve(nbytes_elems, dt, shape=None):
            n2 = nbytes_elems
            a = U[:, ucur[0]:ucur[0] + n2]
            ucur[0] += n2
            assert ucur[0] <= UN, ucur[0]
            if dt == F32:
                a = a.bitcast(F32)
            return a

        def creset():
            ucur[0] = 0

        rr = {}
        _pb = {}

        def PB(name):
            if name not in _pb:
                _pb[name] = Buf(name)
            return _pb[name]

        def rot(key, n):
            v = rr.get(key, 0)
            rr[key] = v + 1
            return v % n

        def dma(q, out, in_, reads, writes, dst):
            P.add(q, lambda e: e.dma_start(out=out, in_=in_), reads, writes, dma_dst=dst)

        def act(out, in_, func, reads, writes, bias=None, scale=None, accum=None):
            kw = {}
            if bias is not None:
                kw["bias"] = bias
            if scale is not None:
                kw["scale"] = scale
            if accum is not None:
                kw["accum_out"] = accum
            P.add("act", lambda e: e.activation(out=out, in_=in_, func=func, **kw), reads, writes)

        def ts(out, in0, s1, op0, reads, writes, s2=None, op1=None, accum=None, eng="dve"):
            kw = {}
            if op1 is not None:
                kw["op1"] = op1
            if accum is not None:
                kw["accum_out"] = accum
            P.add(eng, lambda e: e.tensor_scalar(out=out, in0=in0, scalar1=s1, scalar2=s2, op0=op0, **kw), reads, writes)

        def tt(out, in0, in1, op, reads, writes, eng="dve"):
            P.add(eng, lambda e: e.tensor_tensor(out=out, in0=in0, in1=in1, op=op), reads, writes)

        def stt(out, in0, scalar, in1, op0, op1, reads, writes):
            P.add("dve", lambda e: e.scalar_tensor_tensor(out=out, in0=in0, scalar=scalar, in1=in1, op0=op0, op1=op1), reads, writes)

        def mm(out, lhsT, rhs, start, stop, reads, writes):
            P.add("pe", lambda e: e.matmul(out, lhsT=lhsT, rhs=rhs, start=start, stop=stop), reads, writes)

        def tr(out, in_, ident, reads, writes):
            P.add("pe", lambda e: e.transpose(out=out, in_=in_, identity=ident), reads, writes)

        def copy_alt(out, in_, reads, writes, scale=None):
            if rot("cp", 2) == 0:
                if scale is None:
                    act(out, in_, AF.Copy, reads, writes)
                else:
                    act(out, in_, AF.Copy, reads, writes, scale=scale)
            else:
                if scale is None:
                    P.add("dve", lambda e: e.tensor_copy(out=out, in_=in_), reads, writes)
                else:
                    ts(out, in_, scale, ALU.mult, reads, writes)

        def _body():
            P.add("pool", lambda e: e.memset(ones_f[:], 1.0), [], [b_ones])
            P.add("pool", lambda e: e.memset(zer_f[:], 0.0), [], [b_zer])
            P.add("pool", lambda e: e.memset(st[:], 0.0), [], [])
            P.add("pool", lambda e: e.affine_select(out=ident_f[:], in_=ones_f[:], pattern=[[-1, 128]], compare_op=ALU.is_equal, fill=0.0, base=0, channel_multiplier=1), [b_ones], [b_idf])
            P.add("pool", lambda e: e.affine_select(out=J_f[:], in_=ones_f[:], pattern=[[1, 128]], compare_op=ALU.is_equal, fill=0.0, base=-127, channel_multiplier=1), [b_ones], [b_J])
            P.add("pool", lambda e: e.affine_select(out=caus[:], in_=zer_f[:], pattern=[[-1, 128]], compare_op=ALU.is_ge, fill=-1e30, base=0, channel_multiplier=1), [b_zer], [b_caus])
            P.add("dve", lambda e: e.tensor_copy(out=ident_b[:], in_=ident_f[:]), [b_idf], [b_idb])
            ts(i32k[:], ident_f[:], 32768.0, ALU.mult, [b_idf], [b_i32k])
            P.add("pool", lambda e: e.memset(uh[:], 0.0), [], [b_uh])
            for k in range(NIT + 1):
                P.add("pool", lambda e, k=k: e.memset(pw[:, k:k + 1], 2.0 ** -(k + 1)), [], [b_pw])

            ck("c0")
            def cast_copy(src_t, dst_t, l, nelem, dbuf):
                rows = nelem // 2048
                r0 = 0
                while r0 < rows:
                    n = min(2048, rows - r0)
                    si = bass.AP(src_t, l * nelem + r0 * 2048, [[2048, n], [1, 2048]])
                    di = bass.AP(dst_t, l * nelem + r0 * 2048, [[2048, n], [1, 2048]])
                    dma("pool", di, si, [], [dbuf], dbuf)
                    r0 += n

            for l in range(NL):
                cast_copy(win_d, winb_d, l, D * INW, dW["in"][l])
                cast_copy(wuk_d, wukb_d, l, 16 * 256 * 64, dW["uk"][l])
                cast_copy(wuv_d, wuvb_d, l, 16 * 256 * 64, dW["uv"][l])
                cast_copy(wout_d, woutb_d, l, D * D, dW["out"][l])
                cast_copy(wup_d, wupb_d, l, D * 2 * FF, dW["up"][l])
                cast_copy(wdown_d, wdownb_d, l, FF * D, dW["down"][l])

            ck("cast")
            rowbuf = [carve(256, F32) for i in range(2)]
            b_rowbuf = [Buf(f"rowbuf{i}") for i in range(2)]

            def col_load(dst, b_dst, src_t, nrows):
                r0 = 0
                while r0 < nrows:
                    nb = min(128, nrows - r0)
                    k = rot("rowbuf", 2)
                    src = bass.AP(src_t, r0 * 128, [[128, nb], [1, 128]])
                    dma("sp", rowbuf[k][0:nb, :], src, [], [b_rowbuf[k]], b_rowbuf[k])
                    pb = rot("cl_ps", 2)
                    tr(bank(4 + pb)[:, 0:nb], rowbuf[k][0:nb, :], ident_f[0:nb, 0:nb], [b_rowbuf[k], b_idf], [bA[4 + pb]])
                    copy_alt(dst[:, r0:r0 + nb], bank(4 + pb)[:, 0:nb], [bA[4 + pb]], [b_dst])
                    r0 += nb

            col_load(bmodc, b_bmodc, bmod_d, NL * 96)
            col_load(nattn, b_nattn, nattn_d, NL * 16)
            col_load(nffn, b_nffn, nffn_d, NL * 16)
            col_load(kvn, b_kvn, kvn_d, NL * 2)
            col_load(convw, b_convw, convw_d, NL * 132)
            col_load(convb, b_convb, convb_d, NL * 44)
            col_load(nfin, b_nfin, nfin_d, 16)
            col_load(cTs, b_cTs, c_d, NSEQ * 16)
            act(cTs[:, 0:NSEQ * 16], cTs[:, 0:NSEQ * 16], AF.Silu, [b_cTs], [b_cTs])
            P.add("dve", lambda e: e.tensor_copy(out=cT2[:], in_=cTs[:, 0:NSEQ * 16].rearrange("p (b k) -> p k b", b=NSEQ)), [b_cTs], [b_cT2])
            dma("sp", sink_bc[:, 0:NL * 16], bass.AP(sinks_d, 0, [[0, 128], [1, NL * 16]]), [], [b_sink], b_sink)

            ck("cols")
            rel_aug = sb("rel_aug", [33, 32], F32); b_rel = Buf("rel_aug")
            oh_sb = carve(768 * 2, F32); b_oh = Buf("oh_sb")
            fv_sb = carve(384 * 2, F32); b_fvsb = Buf("fv_sb")
            hk = [carve(512, F32) for i in range(2)]
            b_hk = [Buf(f"hk{i}") for i in range(2)]
            P.add("pool", lambda e: e.memset(rel_aug[32:33, :], 1.0), [], [b_rel])
            dma("sp", rel_aug[0:32, :], rel_d.ap(), [], [b_rel], b_rel)
            dma("sp", oh_sb[0:33, :], oh_d.ap(), [], [b_oh], b_oh)
            for kind in range(2):
                mm(bank(4)[0:16, 0:384], rel_aug[0:33, kind * 16:(kind + 1) * 16], oh_sb[0:33, kind * 384:(kind + 1) * 384], True, True, [b_rel, b_oh], [bA[4]])
                P.add("dve", lambda e: e.tensor_copy(out=fv_sb[0:16, :], in_=bank(4)[0:16, 0:384]), [bA[4]], [b_fvsb])
                dma("sp", fv_d.ap()[kind * 16:(kind + 1) * 16, :], fv_sb[0:16, :], [b_fvsb], [dFV], dFV)
            for hh in range(32):
                k = rot("hk", 2)
                dma("sp", hk[k][:], bass.AP(fv_d, hh * 384, [[1, 128], [1, 256]]), [dFV], [b_hk[k]], b_hk[k])
                pb = rot("cl_ps", 2)
                mm(bank(4 + pb)[:, 0:256], J_f[:], hk[k][:], True, True, [b_J, b_hk[k]], [bA[4 + pb]])
                if hh < 16:
                    copy_alt(BA[:, hh, :], bank(4 + pb)[:, 0:256], [bA[4 + pb]], [b_BA])
                else:
                    copy_alt(BB[:, hh - 16, :], bank(4 + pb)[:, 0:256], [bA[4 + pb]], [b_BB])

            ck("bias")
            P.barrier()
            creset()
            xrow = carve(4 * 2048 * 2, F32).rearrange("p (a b) -> p a b", a=4)
            b_xrow = [Buf(f"xrow{i}") for i in range(4)]
            for s in range(NSEQ):
                for j in range(NTILE):
                    for tsub in range(NSUB):
                        r0 = j * TA + tsub * 128
                        dma("sp", xrow[:, tsub, :], x_a[s, r0:r0 + 128, :], [], [b_xrow[tsub]], b_xrow[tsub])
                    for kc in range(NKC):
                        pb = rot("xt_ps", 4)
                        for tsub in range(NSUB):
                            tr(bank(pb)[:, tsub * 128:(tsub + 1) * 128], xrow[:, tsub, kc * 128:(kc + 1) * 128], ident_f[:], [b_xrow[tsub], b_idf], [bA[pb]])
                        k = rot("xb", 4)
                        copy_alt(xb[k][:], bank(pb), [bA[pb]], [b_xb[k]])
                        dma("sp", xT_a[s, kc, :, j * TA:(j + 1) * TA], xb[k][:], [b_xb[k]], [dX[s][j][kc % 2]], dX[s][j][kc % 2])

            ck("xT")
            def norm_p1(s, j, kc):
                t0 = j * TA
                k = rot("xb", 4)
                dma("sp", xb[k][:], xT_a[s, kc, :, t0:t0 + TA], [dX[s][j][kc % 2]], [b_xb[k]], b_xb[k])
                q = rot("sqb", 2)
                act(sqb[q][:], xb[k][:], AF.Square, [b_xb[k]], [b_sqb[q]])
                mm(bank(5), ones_f[:], sqb[q][:], kc == 0, kc == NKC - 1, [b_ones, b_sqb[q]], [bA[5]])

            def norm_mid():
                act(rstd[:], bank(5), AF.Sqrt, [bA[5]], [b_rstd], bias=EPS, scale=1.0 / D)
                P.add("dve", lambda e: e.reciprocal(out=rstd[:], in_=rstd[:]), [b_rstd], [b_rstd])

            def norm_p2(s, j, kc, gs_ap, dst_fn):
                t0 = j * TA
                k = rot("xb", 4)
                dma("sp", xb[k][:], xT_a[s, kc, :, t0:t0 + TA], [dX[s][j][kc % 2]], [b_xb[k]], b_xb[k])
                q = rot("tmpn", 2)
                stt(tmpn[q][:], xb[k][:], gs_ap(kc), rstd[:], ALU.mult, ALU.mult, [b_xb[k], b_rstd, b_gs1, b_gs2, b_nfin], [b_tmpn[q]])
                dst_fn(kc, tmpn[q], b_tmpn[q])

            def norm_tile(s, j, gs_ap, sh_ap, dst_fn):
                for kc in range(NKC):
                    norm_p1(s, j, kc)
                norm_mid()
                for kc in range(NKC):
                    norm_p2(s, j, kc, gs_ap, dst_fn)

            def to_hT(sh_ap):
                def f(kc, tm, b_tm):
                    act(hT[:, kc, :], tm[:], AF.Identity, [b_tm, b_modc], [b_hT], bias=sh_ap(kc), scale=1.0)
                return f

            for l in range(NL):
                for kind in range(6):
                    for grp in range(8):
                        c0 = kind * 2048 + grp * 256
                        w = rot("WB", 2)
                        Wt = WB[w][:, 0:8192].bitcast(F32).rearrange("p (k c) -> p k c", k=16)
                        dma("sp", Wt, wmod_a[l, :, c0:c0 + 256].rearrange("(k p) c -> p k c", p=128), [], [b_WB[w]], b_WB[w])
                        for kc in range(NKC):
                            mm(bank(4)[0:NSEQ, 0:256], cT2[:, kc, :], Wt[:, kc, :], kc == 0, kc == NKC - 1, [b_cT2, b_WB[w]], [bA[4]])
                        act(mrow[0:NSEQ, :], bank(4)[0:NSEQ, 0:256], AF.Copy, [bA[4]], [b_mrow])
                        for jj in range(2):
                            tr(bank(5)[:, jj * NSEQ:(jj + 1) * NSEQ], mrow[0:NSEQ, jj * 128:(jj + 1) * 128], ident_f[0:NSEQ, 0:NSEQ], [b_mrow, b_idf], [bA[5]])
                        for jj in range(2):
                            ch = grp * 2 + jj
                            col = l * 96 + kind * 16 + ch
                            ts(modc[:, kind, ch, :], bank(5)[:, jj * NSEQ:(jj + 1) * NSEQ], bmodc[:, col:col + 1], ALU.add, [bA[5], b_bmodc], [b_modc])
                for s in range(NSEQ):
                    stt(gs1[:, s, :], modc[:, 1, :, s], 1.0, nattn[:, l * 16:(l + 1) * 16], ALU.add, ALU.mult, [b_modc, b_nattn], [b_gs1])
                    stt(gs2[:, s, :], modc[:, 4, :, s], 1.0, nffn[:, l * 16:(l + 1) * 16], ALU.add, ALU.mult, [b_modc, b_nffn], [b_gs2])
                ck("mod")
                P.barrier()
                creset()
                wukraw = carve(2 * 16 * 64, BF16).rearrange("p (r h d) -> p r h d", r=2, h=16)
                b_wukraw = PB("wukraw")
                for rc in range(2):
                    dma("sp", wukraw[:, rc, :, :], wukb_a[l, :, rc * 128:(rc + 1) * 128, :].rearrange("h p d -> p h d"), [dW["uk"][l]], [b_wukraw], b_wukraw)
                    dma("sp", wuv[:, rc, :, :], wuvb_a[l, :, rc * 128:(rc + 1) * 128, :].rearrange("h p d -> p h d"), [dW["uv"][l]], [b_wuv], b_wuv)
                for pp in range(8):
                    for rc in range(2):
                        idx = pp * 2 + rc
                        Bx, bBx = (B6, bB6) if idx < 8 else (B7, bB7)
                        tr(Bx[:, (idx % 8) * 128:(idx % 8 + 1) * 128], wukraw[:, rc, 2 * pp:2 * pp + 2, :].rearrange("p h d -> p (h d)"), ident_b[:], [b_wukraw, b_idb], [bBx])
                act(wukT[:, 0:4, :].rearrange("p a b -> p (a b)"), B6[:], AF.Copy, [bB6], [b_wukT])
                P.add("dve", lambda e: e.tensor_copy(out=wukT[:, 4:8, :].rearrange("p a b -> p (a b)"), in_=B7[:]), [bB7], [b_wukT])

                ck("wuk")
                for s in range(NSEQ):
                    P.barrier()
                    creset()
                    kaT0 = carve(2048, BF16); kaT1 = carve(2048, BF16); kiT = carve(2048, BF16)
                    b_kaT0, b_kaT1, b_kiT = Buf("kaT0"), Buf("kaT1"), Buf("kiT")
                    va = carve(16 * 128, BF16).rearrange("p (b d) -> p b d", b=16); b_va = Buf("va")
                    ckv = carve(16 * 256, BF16).rearrange("p (b d) -> p b d", b=16); b_ckv = Buf("ckv")
                    ckvT = carve(2 * 2048, BF16).rearrange("p (r t) -> p r t", r=2); b_ckvT = Buf("ckvT")
                    qaT = carve(8 * TA, BF16).rearrange("p (c t) -> p c t", c=8); b_qaT = Buf("qaT")
                    qbT = carve(8 * TA, BF16).rearrange("p (c t) -> p c t", c=8); b_qbT = Buf("qbT")
                    qiT = carve(8 * TA, BF16).rearrange("p (c t) -> p c t", c=8); b_qiT = Buf("qiT")
                    acc = carve(2048 * 2, F32); b_acc = Buf("acc")
                    rtmp = [carve(512 * 2, F32) for _ in range(2)]; b_rtmp = [Buf("rtmp0"), Buf("rtmp1")]
                    mneg = carve(2048, BF16); b_mneg = Buf("mneg")
                    Psb = [carve(2048, BF16) for _ in range(2)]; b_Psb = [Buf("P0"), Buf("P1")]
                    PTs = [carve(16 * 128, BF16).rearrange("p (b t) -> p b t", b=16) for _ in range(2)]; b_PTs = [Buf("PT0"), Buf("PT1")]
                    qlat = [carve(4 * 128, BF16).rearrange("p (h r t) -> p h r t", h=2, r=2) for _ in range(2)]; b_qlat = [Buf("ql0"), Buf("ql1")]
                    olat = [carve(2 * 128, BF16).rearrange("p (r t) -> p r t", r=2) for _ in range(2)]; b_olat = [Buf("ol0"), Buf("ol1")]
                    mixtok = carve(2048, BF16); b_mixtok = Buf("mixtok")
                    wi_sb = wi_sb_t[:]; b_wi = Buf("wi")
                    junkf = junkf_t[:]; b_junkf = Buf("junkf")
                    jk8 = carve(1024, BF16).bitcast(mybir.dt.uint8); b_jk8 = Buf("jk8")
                    b_mx = Buf("mx"); b_rs = [Buf("rs0"), Buf("rs1")]; b_rr = [Buf("rr0"), Buf("rr1")]
                    prep = [None]
                    b_st = Buf("stA")
                    b_bis = Buf("bis")

                    for j in range(NTILE):
                        t0 = j * TA
                        norm_tile(s, j, lambda kc: gs1[:, s, kc:kc + 1], None, to_hT(lambda kc: modc[:, 0, kc, s:s + 1]))
                        ck("norm")
                        def load_w(pieces, src_a, dbuf):
                            w = rot("WB", 2)
                            Wv = WB[w][:, 0:8192].rearrange("p (k c) -> p k c", k=16)
                            for (off, c0, n) in pieces:
                                dma("sp", Wv[:, :, off:off + n], src_a[:, c0:c0 + n].rearrange("(k p) c -> p k c", p=128), [dbuf], [b_WB[w]], b_WB[w])
                            return Wv, b_WB[w]

                        def feat_chunk(Wv, bW, cc, dest, b_dest, scale):
                            pb = rot("pj_ps", 4)
                            for kc in range(NKC):
                                mm(bank(pb), Wv[:, kc, cc * 128:(cc + 1) * 128], hT[:, kc, :], kc == 0, kc == NKC - 1, [bW, b_hT], [bA[pb]])
                            copy_alt(dest, bank(pb), [bA[pb]], [b_dest], scale=scale)

                        wl = winb_a[l]
                        for half in range(2):
                            Wv, bW = load_w([(0, half * 512, 512)], wl, dW["in"][l])
                            for cc in range(4):
                                feat_chunk(Wv, bW, cc, qaT[:, half * 4 + cc, :], b_qaT, 0.125)
                        ck("pqa")
                        Wv, bW = load_w([(0, 1024, 64), (64, 1024, 64), (128, 1088, 64), (192, 1088, 64), (256, 3584, 64), (320, 3584, 64)], wl, dW["in"][l])
                        feat_chunk(Wv, bW, 0, kaT0[:, t0:t0 + TA], b_kaT0, None)
                        feat_chunk(Wv, bW, 1, kaT1[:, t0:t0 + TA], b_kaT1, None)
                        feat_chunk(Wv, bW, 2, kiT[:, t0:t0 + TA], b_kiT, None)
                        ck("pkq")
                        for half in range(2):
                            Wv, bW = load_w([(0, 1280 + half * 512, 512)], wl, dW["in"][l])
                            for cc in range(4):
                                feat_chunk(Wv, bW, cc, qbT[:, half * 4 + cc, :], b_qbT, 0.125)
                        for half in range(2):
                            Wv, bW = load_w([(0, 2560 + half * 512, 512)], wl, dW["in"][l])
                            for cc in range(4):
                                feat_chunk(Wv, bW, cc, qiT[:, half * 4 + cc, :], b_qiT, None)
                        ck("pq")
                        Wv, bW = load_w([(0, 1152, 128), (128, 2304, 256), (384, 3600, 64)], wl, dW["in"][l])
                        ck("tm_ld")
                        for tsub in range(NSUB):
                            blk = j * NSUB + tsub
                            pb = rot("pj_ps", 4)
                            for kc in range(NKC):
                                mm(bank(pb)[:, 0:448], hT[:, kc, tsub * 128:(tsub + 1) * 128], Wv[:, kc, 0:448], kc == 0, kc == NKC - 1, [bW, b_hT], [bA[pb]])
                            ck("tm_mm")
                            P.add("dve", lambda e, blk=blk, pb=pb: e.tensor_copy(out=va[:, blk, :], in_=bank(pb)[:, 0:128]), [bA[pb]], [b_va])
                            ck("tm_va")
                            act(wi_sb[:, tsub, :], bank(pb)[:, 432:448], AF.Copy, [bA[pb]], [b_wi])
                            ck("tm_cp")
                            act(junkf[:], bank(pb)[:, 128:384], AF.Square, [bA[pb]], [b_junkf, b_st], accum=st[:, 0:1])
                            ck("tm_sq")
                            act(st[:, 1:2], st[:, 0:1], AF.Sqrt, [b_st], [b_st], bias=EPS, scale=1.0 / 256)
                            P.add("dve", lambda e: e.reciprocal(out=st[:, 2:3], in_=st[:, 1:2]), [b_st], [b_st])
                            ts(ckv[:, blk, :], bank(pb)[:, 128:384], st[:, 2:3], ALU.mult, [bA[pb], b_st], [b_ckv])
                            ck("tm_ckv")
                            for rc in range(2):
                                tr(B6[:, rc * 128:(rc + 1) * 128], ckv[:, blk, rc * 128:(rc + 1) * 128], ident_b[:], [b_ckv, b_idb], [bB6])
                            copy_alt(ckvT[:, :, blk * 128:(blk + 1) * 128], B6[:, 0:256].rearrange("p (r t) -> p r t", r=2), [bB6], [b_ckvT])

                        ck("proj")
                        for tsub in range(NSUB):
                            i = j * NSUB + tsub
                            tc = slice(tsub * 128, (tsub + 1) * 128)
                            nk = 256 if i > 0 else 128
                            ks = (i - 1) * 128 if i > 0 else 0
                            bc0 = 0 if i > 0 else 128
                            nkb = nk // 128
                            for half in range(2):
                                pr = rot("Psb", 2)
                                hb = half * 64
                                for hh in range(8):
                                    h = 2 * hh + half
                                    kaT, b_kaT = (kaT0, b_kaT0) if h < 8 else (kaT1, b_kaT1)
                                    o_ = A03[:, hh * 256:hh * 256 + nk]
                                    mm(o_, qaT[hb:hb + 64, hh, tc], kaT[hb:hb + 64, ks:ks + nk], True, False, [b_qaT, b_kaT], [bA[hh // 2]])
                                    mm(o_, ident_b[:], BA[:, h, bc0:bc0 + nk], False, True, [b_idb, b_BA], [bA[hh // 2]])
                                S3 = A03[:].rearrange("p (h k) -> p h k", h=8)[:, :, 0:nk]
                                P.add("dve", lambda e, S3=S3: e.tensor_reduce(out=st[:, 8:16], in_=S3, axis=AX.X, op=ALU.max), [bA[0], bA[1], bA[2], bA[3], b_st], [b_st])
                                _s = sink_bc[:, l * 16 + half:l * 16 + half + 1]
                                sk = bass.AP(_s.tensor, _s.offset, [list(_s.ap[0]), [2, 8]])
                                tt(st[:, 16:24], st[:, 8:16], sk, ALU.max, [b_st, b_sink], [b_st])
                                ts(st[:, 24:32], st[:, 16:24], -1.0, ALU.mult, [b_st], [b_st])
                                for hh in range(8):
                                    act(Psb[pr][:, hh * 256:hh * 256 + nk], A03[:, hh * 256:hh * 256 + nk], AF.Exp, [bA[hh // 2], b_st], [b_Psb[pr], b_st],
                                        bias=st[:, 24 + hh:25 + hh], scale=1.0, accum=st[:, 32 + hh:33 + hh])
                                tt(st[:, 40:48], sk, st[:, 24:32], ALU.add, [b_st, b_sink], [b_st])
                                act(st[:, 40:48], st[:, 40:48], AF.Exp, [b_st], [b_st])
                                tt(st[:, 40:48], st[:, 40:48], st[:, 32:40], ALU.add, [b_st], [b_st])
                                P.add("dve", lambda e, half=half: e.reciprocal(out=st[:, 48 + half * 8:56 + half * 8], in_=st[:, 40:48]), [b_st], [b_st])
                                pt = rot("PTs", 2)
                                for hh in range(8):
                                    for kb in range(nkb):
                                        sl = hh * 2 + kb
                                        Bx, bBx = (B6, bB6) if sl < 8 else (B7, bB7)
                                        tr(Bx[:, (sl % 8) * 128:(sl % 8 + 1) * 128], Psb[pr][:, hh * 256 + kb * 128:hh * 256 + (kb + 1) * 128], ident_b[:], [b_Psb[pr], b_idb], [bBx])
                                act(PTs[pt][:, 0:8, :].rearrange("p a b -> p (a b)"), B6[:], AF.Copy, [bB6], [b_PTs[pt]])
                                P.add("dve", lambda e, pt=pt: e.tensor_copy(out=PTs[pt][:, 8:16, :].rearrange("p a b -> p (a b)"), in_=B7[:]), [bB7], [b_PTs[pt]])
                                for hh in range(8):
                                    h = 2 * hh + half
                                    g = h // 8
                                    for kb in range(nkb):
                                        kblk = (i - 1 + kb) if i > 0 else 0
                                        mm(A45[:, h * 64:(h + 1) * 64], PTs[pt][:, hh * 2 + kb, :], va[:, kblk, g * 64:(g + 1) * 64], kb == 0, kb == nkb - 1, [b_PTs[pt], b_va], [bA[4 + h // 8]])
                            _a = st[:, 48:64]
                            rden = bass.AP(_a.tensor, _a.offset, [list(_a.ap[0]), [1, 8], [8, 2], [0, 64]])
                            tt(mixtok[:, 0:1024].rearrange("p (h e d) -> p h e d", h=8, e=2), A45[:].rearrange("p (h e d) -> p h e d", h=8, e=2), rden, ALU.mult, [bA[4], bA[5], b_st], [b_mixtok])

                            ck("swa%d" % i)
                            S = 128 * (i + 1)
                            nch = (S + 511) // 512

                            def indexer(ii, tsb):
                                S_ = 128 * (ii + 1)
                                tcc = slice(tsb * 128, (tsb + 1) * 128)
                                for h in range(16):
                                    pch, hb = h // 2, (h % 2) * 64
                                    for c in range((S_ + 511) // 512):
                                        w_ = min(512, S_ - c * 512)
                                        pb = rot("ix_ps", 4)
                                        mm(bank(pb)[:, 0:w_], qiT[hb:hb + 64, pch, tcc], kiT[hb:hb + 64, c * 512:c * 512 + w_], True, True, [b_qiT, b_kiT], [bA[pb]])
                                        rq = rot("rtmp", 2)
                                        act(rtmp[rq][:, 0:w_], bank(pb)[:, 0:w_], AF.Relu, [bA[pb]], [b_rtmp[rq]])
                                        a_ = acc[:, c * 512:c * 512 + w_]
                                        if h == 0:
                                            ts(a_, rtmp[rq][:, 0:w_], wi_sb[:, tsb, 0:1], ALU.mult, [b_rtmp[rq], b_wi], [b_acc])
                                        else:
                                            stt(a_, rtmp[rq][:, 0:w_], wi_sb[:, tsb, h:h + 1], a_, ALU.mult, ALU.add, [b_rtmp[rq], b_wi, b_acc], [b_acc])
                                lo, hi, wd, mid, cnt, tmp = (st[:, 64 + k:65 + k] for k in range(6))
                                P.add("dve", lambda e: e.tensor_reduce(out=lo, in_=acc[:, 0:S_], axis=AX.X, op=ALU.min), [b_acc], [b_bis])
                                P.add("dve", lambda e: e.tensor_reduce(out=hi, in_=acc[:, 0:S_], axis=AX.X, op=ALU.max), [b_acc], [b_bis])
                                tt(acc[:, ii * 128:(ii + 1) * 128], acc[:, ii * 128:(ii + 1) * 128], caus[:], ALU.add, [b_acc, b_caus], [b_acc])
                                tt(wd, hi, lo, ALU.subtract, [b_bis], [b_bis])
                                ts(bs2[:], pw[:], wd, ALU.mult, [b_bis, b_pw], [b_bis])
                                ts(bsn[:], pw[:], wd, ALU.mult, [b_bis, b_pw], [b_bis], s2=-0.5, op1=ALU.mult)
                                tt(mid, lo, bs2[:, 0:1], ALU.add, [b_bis], [b_bis])

                            def bis_iter(ii, k):
                                S_ = 128 * (ii + 1)
                                mid, cnt, tmp = st[:, 67:68], st[:, 68:69], st[:, 69:70]
                                ts(jk8[:, 0:S_], acc[:, 0:S_], mid, ALU.is_ge, [b_acc, b_bis], [b_jk8, b_bis], s2=None, op1=ALU.add, accum=cnt)
                                if k < NIT - 1:
                                    ts(tmp, cnt, 255.5, ALU.is_ge, [b_bis], [b_bis], s2=bs2[:, k:k + 1], op1=ALU.mult)
                                    stt(mid, tmp, bsn[:, k:k + 1], mid, ALU.add, ALU.add, [b_bis], [b_bis])
                                else:
                                    ts(tmp, cnt, 255.5, ALU.is_lt, [b_bis], [b_bis], s2=bs2[:, k:k + 1], op1=ALU.mult)
                                    tt(mid, mid, tmp, ALU.subtract, [b_bis], [b_bis])

                            def bis_final(ii):
                                S_ = 128 * (ii + 1)
                                ts(mneg[:, 0:S_], acc[:, 0:S_], st[:, 67:68], ALU.is_ge, [b_acc, b_bis], [b_mneg], s2=-1.0, op1=ALU.add)

                            if i >= 2 and prep[0] != i:
                                indexer(i, tsub)
                                for k in range(NIT):
                                    bis_iter(i, k)
                                bis_final(i)
                            nxt = (i + 1) if (tsub < NSUB - 1 and i + 1 >= 2) else None
                            if nxt is not None:
                                indexer(nxt, tsub + 1)
                            bis_k = [0]

                            def bis_some(n):
                                if nxt is None:
                                    return
                                for _ in range(n):
                                    if bis_k[0] < NIT:
                                        bis_iter(nxt, bis_k[0])
                                        bis_k[0] += 1

                            def pre_pe(h, ql, hs):
                                for c in range(nch):
                                    c0 = c * 512
                                    w_ = min(512, S - c0)
                                    ops = [(qlat[ql][:, hs, 0, :], ckvT[:, 0, c0:c0 + w_], 0, w_, [b_qlat[ql], b_ckvT]),
                                           (qlat[ql][:, hs, 1, :], ckvT[:, 1, c0:c0 + w_], 0, w_, [b_qlat[ql], b_ckvT])]
                                    if i >= 2:
                                        ops.append((i32k[:], mneg[:, c0:c0 + w_], 0, w_, [b_i32k, b_mneg]))
                                    for kb in ((i - 1, i) if i > 0 else (i,)):
                                        if c0 <= kb * 128 < c0 + w_:
                                            bb = (kb - (i - 1)) if i > 0 else 1
                                            ops.append((ident_b[:], BB[:, h, bb * 128:(bb + 1) * 128], kb * 128 - c0, 128, [b_idb, b_BB]))
                                    for n_, (lh, rh, o0, ow, rd) in enumerate(ops):
                                        mm(bank(c)[:, o0:o0 + ow], lh, rh, n_ == 0, n_ == len(ops) - 1, rd, [bA[c]])

                            def pre_dve(h):
                                banks_r = [bA[c] for c in range(nch)]
                                P.add("dve", lambda e, S=S: e.tensor_reduce(out=st[:, 72:73], in_=A03[:, 0:S], axis=AX.X, op=ALU.max), banks_r, [b_mx])
                                ts(st[:, 73:74], st[:, 72:73], -1.0, ALU.mult, [b_mx], [b_mx])

                            def pre_act(h):
                                banks_r = [bA[c] for c in range(nch)]
                                pr = h % 2
                                act(Psb[pr][:, 0:S], A03[:, 0:S], AF.Exp, banks_r + [b_mx], [b_Psb[pr], b_rs[h % 2]], bias=st[:, 73:74], scale=1.0, accum=st[:, 80 + h:81 + h])

                            def post_T_pe(h):
                                pr = h % 2
                                for b_ in range(i + 1):
                                    Bx, bBx = (B6, bB6) if b_ < 8 else (B7, bB7)
                                    tr(Bx[:, (b_ % 8) * 128:(b_ % 8 + 1) * 128], Psb[pr][:, b_ * 128:(b_ + 1) * 128], ident_b[:], [b_Psb[pr], b_idb], [bBx])

                            def post_T_act(h):
                                pt = h % 2
                                n6 = min(8, i + 1)
                                act(PTs[pt][:, 0:n6, :].rearrange("p a b -> p (a b)"), B6[:, 0:n6 * 128], AF.Copy, [bB6], [b_PTs[pt]])

                            def post_T_dve(h):
                                pt = h % 2
                                if i + 1 > 8:
                                    n7 = i + 1 - 8
                                    act(PTs[pt][:, 8:8 + n7, :].rearrange("p a b -> p (a b)"), B7[:, 0:n7 * 128], AF.Copy, [bB7], [b_PTs[pt]])

                            def post_rest(h):
                                pt = h % 2
                                for rc in range(2):
                                    for b_ in range(i + 1):
                                        mm(bank(5)[:, rc * 128:(rc + 1) * 128], ckv[:, b_, rc * 128:(rc + 1) * 128], PTs[pt][:, b_, :], b_ == 0, b_ == i, [b_ckv, b_PTs[pt]], [bA[5]])
                                ol = rot("olat", 2)
                                for rc in range(2):
                                    act(olat[ol][:, rc, :], bank(5)[:, rc * 128:(rc + 1) * 128], AF.Identity, [bA[5], b_kvn], [b_olat[ol]], bias=0.0, scale=kvn[:, l * 2 + rc:l * 2 + rc + 1])
                                for rc in range(2):
                                    mm(bank(5)[:, 256:320], olat[ol][:, rc, :], wuv[:, rc, h, :], rc == 0, rc == 1, [b_olat[ol], b_wuv], [bA[5]])
                                P.add("dve", lambda e, h=h: e.reciprocal(out=st[:, 96 + h:97 + h], in_=st[:, 80 + h:81 + h]), [b_rs[h % 2]], [b_rr[h % 2]])
                                ts(mixtok[:, 1024 + h * 64:1024 + (h + 1) * 64], bank(5)[:, 256:320], st[:, 96 + h:97 + h], ALU.mult, [bA[5], b_rr[h % 2]], [b_mixtok])

                            prev = None
                            for pp in range(8):
                                ql = rot("qlat", 2)
                                for hs in range(2):
                                    hb = hs * 64
                                    for rc in range(2):
                                        mm(bank(4 + hs)[:, rc * 128:(rc + 1) * 128], wukT[hb:hb + 64, pp, rc * 128:(rc + 1) * 128], qbT[hb:hb + 64, pp, tc], True, True, [b_wukT, b_qbT], [bA[4 + hs]])
                                for hs in range(2):
                                    for rc in range(2):
                                        ts(qlat[ql][:, hs, rc, :], bank(4 + hs)[:, rc * 128:(rc + 1) * 128], kvn[:, l * 2 + rc:l * 2 + rc + 1], ALU.mult, [bA[4 + hs], b_kvn], [b_qlat[ql]])
                                for hs in range(2):
                                    h = pp * 2 + hs
                                    pre_pe(h, ql, hs)
                                    if prev is not None:
                                        post_T_pe(prev)
                                        post_T_act(prev)
                                        post_T_dve(prev)
                                    pre_dve(h)
                                    pre_act(h)
                                    if prev is not None:
                                        post_rest(prev)
                                    bis_some(2)
                                    prev = h
                            post_T_pe(prev)
                            post_T_act(prev)
                            post_T_dve(prev)
                            post_rest(prev)
                            if nxt is not None:
                                bis_some(NIT)
                                bis_final(nxt)
                                prep[0] = nxt
                            ck("dsa%d" % i)
                            for c in range(16):
                                Bx, bBx = (B6, bB6) if c < 8 else (B7, bB7)
                                tr(Bx[:, (c % 8) * 128:(c % 8 + 1) * 128], mixtok[:, c * 128:(c + 1) * 128], ident_b[:], [b_mixtok, b_idb], [bBx])
                            act(hT[:, 0:8, tc], B6[:].rearrange("p (c t) -> p c t", c=8), AF.Copy, [bB6], [b_hT])
                            P.add("dve", lambda e, tc=tc: e.tensor_copy(out=hT[:, 8:16, tc], in_=B7[:].rearrange("p (c t) -> p c t", c=8)), [bB7], [b_hT])

                        ck("attn%d" % j)
                        for g in range(4):
                            w = rot("WB", 2)
                            Wv = WB[w][:, 0:8192].rearrange("p (k c) -> p k c", k=16)
                            dma("sp", Wv, woutb_a[l][:, g * 512:(g + 1) * 512].rearrange("(k p) c -> p k c", p=128), [dW["out"][l]], [b_WB[w]], b_WB[w])
                            for mm_ in range(4):
                                m = g * 4 + mm_
                                pb = rot("pj_ps", 4)
                                for kc in range(NKC):
                                    mm(bank(pb), Wv[:, kc, mm_ * 128:(mm_ + 1) * 128], hT[:, kc, :], kc == 0, kc == NKC - 1, [b_WB[w], b_hT], [bA[pb]])
                                k = rot("xb", 4)
                                dma("sp", xb[k][:], xT_a[s, m, :, t0:t0 + TA], [dX[s][j][m % 2]], [b_xb[k]], b_xb[k])
                                stt(xb[k][:], bank(pb), modc[:, 2, m, s:s + 1], xb[k][:], ALU.mult, ALU.add, [bA[pb], b_modc, b_xb[k]], [b_xb[k]])
                                dma("pool", xT_a[s, m, :, t0:t0 + TA], xb[k][:], [b_xb[k]], [dX[s][j][m % 2]], dX[s][j][m % 2])

                    ck("att")
                    P.barrier()
                    creset()
                    actT = carve(NFC * TA, BF16).rearrange("p (c t) -> p c t", c=NFC); b_actT = Buf("actT")
                    a_t = [carve(TA * 2, F32) for _ in range(2)]; b_at = [Buf("at0"), Buf("at1")]
                    s_t = [carve(TA * 2, F32) for _ in range(2)]; b_stt = [Buf("st0"), Buf("st1")]
                    WF = [WB[0][:, 0:8192], WB[1][:, 0:8192], carve(8192, BF16), carve(8192, BF16)]
                    b_WF = [b_WB[0], b_WB[1], PB("WF2"), PB("WF3")]
                    gs2f = lambda kc: gs2[:, s, kc:kc + 1]
                    hT2 = to_hT(lambda kc: modc[:, 3, kc, s:s + 1])
                    for j in range(NTILE):
                        t0 = j * TA
                        if j == 0 or not FFN_OVERLAP:
                            norm_tile(s, j, gs2f, None, hT2)
                        for q in range(11):
                            wu = rot("WF", 4)
                            Wu = WF[wu].rearrange("p (k c) -> p k c", k=16)
                            dma("sp", Wu, wupb_a[l][:, q * 512:(q + 1) * 512].rearrange("(k p) c -> p k c", p=128), [dW["up"][l]], [b_WF[wu]], b_WF[wu])
                            wg = rot("WF", 4)
                            Wg = WF[wg].rearrange("p (k c) -> p k c", k=16)
                            dma("sp", Wg, wupb_a[l][:, FF + q * 512:FF + (q + 1) * 512].rearrange("(k p) c -> p k c", p=128), [dW["up"][l]], [b_WF[wg]], b_WF[wg])
                            for cc in range(4):
                                c = q * 4 + cc
                                pu = rot("ff_ps", 3) * 2
                                pg = pu + 1
                                for kc in range(NKC):
                                    mm(bank(pu), Wu[:, kc, cc * 128:(cc + 1) * 128], hT[:, kc, :], kc == 0, kc == NKC - 1, [b_WF[wu], b_hT], [bA[pu]])
                                for kc in range(NKC):
                                    mm(bank(pg), Wg[:, kc, cc * 128:(cc + 1) * 128], hT[:, kc, :], kc == 0, kc == NKC - 1, [b_WF[wg], b_hT], [bA[pg]])
                                ai = rot("a_t", 2)
                                a_ = a_t[ai]
                                w0 = convw[:, (l * 3 + 0) * NFC + c:(l * 3 + 0) * NFC + c + 1]
                                w1 = convw[:, (l * 3 + 1) * NFC + c:(l * 3 + 1) * NFC + c + 1]
                                w2 = convw[:, (l * 3 + 2) * NFC + c:(l * 3 + 2) * NFC + c + 1]
                                cb = convb[:, l * NFC + c:l * NFC + c + 1]
                                up = bank(pu)
                                act(a_[:, 0:TA], up, AF.Identity, [bA[pu], b_convw, b_convb], [b_at[ai]], bias=cb, scale=w2)
                                stt(a_[:, 1:TA], up[:, 0:TA - 1], w1, a_[:, 1:TA], ALU.mult, ALU.add, [bA[pu], b_at[ai], b_convw], [b_at[ai]])
                                stt(a_[:, 2:TA], up[:, 0:TA - 2], w0, a_[:, 2:TA], ALU.mult, ALU.add, [bA[pu], b_at[ai], b_convw], [b_at[ai]])
                                if j > 0:
                                    stt(a_[:, 0:1], uh[:, c, 1:2], w1, a_[:, 0:1], ALU.mult, ALU.add, [b_uh, b_at[ai], b_convw], [b_at[ai]])
                                    stt(a_[:, 0:2], uh[:, c, 0:2], w0, a_[:, 0:2], ALU.mult, ALU.add, [b_uh, b_at[ai], b_convw], [b_at[ai]])
                                P.add("dve", lambda e, c=c, up=up: e.tensor_copy(out=uh[:, c, :], in_=up[:, TA - 2:TA]), [bA[pu]], [b_uh])
                                si = rot("s_t", 2)
                                act(s_t[si][:, 0:TA], a_[:, 0:TA], AF.Silu, [b_at[ai]], [b_stt[si]])
                                tt(actT[:, c, :], s_t[si][:, 0:TA], bank(pg), ALU.mult, [b_stt[si], bA[pg]], [b_actT])
                        for mp in range(8):
                            wds = []
                            for hf in range(2):
                                wd = rot("WF", 4)
                                Wd = WF[wd][:, 0:22 * 256].rearrange("p (c m) -> p c m", c=22)
                                dma("sp", Wd, wdownb_a[l][hf * 2816:(hf + 1) * 2816, mp * 256:(mp + 1) * 256].rearrange("(c p) m -> p c m", p=128), [dW["down"][l]], [b_WF[wd]], b_WF[wd])
                                wds.append((Wd, b_WF[wd]))
                            for mm_ in range(2):
                                m = mp * 2 + mm_
                                pb = rot("pj_ps", 4)
                                for c in range(NFC):
                                    Wd, bWd = wds[c // 22]
                                    mm(bank(pb), Wd[:, c % 22, mm_ * 128:(mm_ + 1) * 128], actT[:, c, :], c == 0, c == NFC - 1, [bWd, b_actT], [bA[pb]])
                                k = rot("xb", 4)
                                dma("sp", xb[k][:], xT_a[s, m, :, t0:t0 + TA], [dX[s][j][m % 2]], [b_xb[k]], b_xb[k])
                                stt(xb[k][:], bank(pb), modc[:, 5, m, s:s + 1], xb[k][:], ALU.mult, ALU.add, [bA[pb], b_modc, b_xb[k]], [b_xb[k]])
                                dma("pool", xT_a[s, m, :, t0:t0 + TA], xb[k][:], [b_xb[k]], [dX[s][j][m % 2]], dX[s][j][m % 2])
                                if FFN_OVERLAP and j + 1 < NTILE:
                                    if m < 8:
                                        norm_p1(s, j + 1, 2 * m)
                                        norm_p1(s, j + 1, 2 * m + 1)
                                        if m == 7:
                                            norm_mid()
                                    else:
                                        norm_p2(s, j + 1, 2 * (m - 8), gs2f, hT2)
                                        norm_p2(s, j + 1, 2 * (m - 8) + 1, gs2f, hT2)

            ck("layers")
            P.barrier()
            creset()
            orow = carve(4 * 2048 * 2, F32).rearrange("p (a b) -> p a b", a=4)
            b_orow = [Buf(f"orow{i}") for i in range(4)]
            for s in range(NSEQ):
                for j in range(NTILE):
                    def fin(kc, tm, b_tm):
                        pb = rot("xt_ps", 4)
                        for tsub in range(NSUB):
                            tr(bank(pb)[:, tsub * 128:(tsub + 1) * 128], tm[:, tsub * 128:(tsub + 1) * 128], ident_f[:], [b_tm, b_idf], [bA[pb]])
                        for tsub in range(NSUB):
                            copy_alt(orow[:, tsub, kc * 128:(kc + 1) * 128], bank(pb)[:, tsub * 128:(tsub + 1) * 128], [bA[pb]], [b_orow[tsub]])
                    norm_tile(s, j, lambda kc: nfin[:, kc:kc + 1], None, fin)
                    for tsub in range(NSUB):
                        r0 = j * TA + tsub * 128
                        dma("pool", out_a[s, r0:r0 + 128, :], orow[:, tsub, :], [b_orow[tsub]], [dOUT], dOUT)

        try:
            _body()
        except _Stop:
            pass
        P.add("sp", None, [dOUT, dDBG], [])
        P.emit(es)
    return nc, P


_CACHE = {}


def kernel(**inputs):
    n = 8
    x = np.ascontiguousarray(np.asarray(inputs["x"], dtype=np.float32))
    c = np.ascontiguousarray(np.asarray(inputs["c"], dtype=np.float32))
    if "nc" not in _CACHE:
        _CACHE["nc"] = build()[0]
    nc = _CACHE["nc"]
    oh = _onehot_const().reshape(33, 768)
    shared = {k: np.ascontiguousarray(np.asarray(inputs[k], dtype=np.float32)) for k in
              ("w_mod", "b_mod", "norm_attn", "w_in", "attn_sinks", "kv_norm", "w_uk", "w_uv", "rel_bias",
               "w_out", "norm_ffn", "w_up", "conv_w", "conv_b", "w_down", "norm_final")}
    in_maps = []
    for i in range(n):
        m = dict(shared)
        m["x"] = x[2 * i:2 * i + 2]
        m["c"] = c[2 * i:2 * i + 2]
        m["oh"] = oh
        in_maps.append(m)
    res = run_bass_kernel_spmd(nc, in_maps, core_ids=list(range(n)))
    return np.concatenate([r["out"] for r in res.results], axis=0)
```
